# Optimizing a Trainium2 kernel written in Bass

```python
import jax
import jax.numpy as jnp
from jax import lax
import numpy as np

D_MODEL = 1024
BATCH = 2
SEQ = 16384
DEPTH = 1

GRID_W = 64
CTX_LEN = 256
HEAD_DIM = 64
A_HEADS = 8
A_KV_HEADS = 2
A_GROUPS = A_HEADS // A_KV_HEADS
B_HEADS = 8
B_KV_HEADS = 2
B_GROUPS = B_HEADS // B_KV_HEADS
WINDOW = 128
BLOCK = 128
ROPE_THETA = 10000.0
D_FF = 2816
CONV_WIDTH = 3
LN_EPS = 1e-5
QK_EPS = 1e-6
N_MOD = 6
DEEPNORM_ALPHA = (2.0 * DEPTH) ** 0.25
DEEPNORM_BETA = (8.0 * DEPTH) ** -0.25

OFF_QA = 0
OFF_KA = OFF_QA + A_HEADS * HEAD_DIM
OFF_VA = OFF_KA + A_KV_HEADS * HEAD_DIM
OFF_QB = OFF_VA + A_KV_HEADS * HEAD_DIM
OFF_KB = OFF_QB + B_HEADS * HEAD_DIM
OFF_VB = OFF_KB + B_KV_HEADS * HEAD_DIM
OFF_GA = OFF_VB + B_KV_HEADS * HEAD_DIM
OFF_GB = OFF_GA + D_MODEL
IN_COLS = OFF_GB + D_MODEL

kernel_name = "hybrid_window_axial_gqa_convffn_dit_layer"


def _layer_norm(x, g, b):
    xf = x.astype(jnp.float32)
    mu = jnp.mean(xf, axis=-1, keepdims=True)
    var = jnp.mean(jnp.square(xf - mu), axis=-1, keepdims=True)
    return ((xf - mu) * lax.rsqrt(var + LN_EPS) * g + b).astype(x.dtype)


def _qk_rms(t, g):
    tf = t.astype(jnp.float32)
    return (tf * lax.rsqrt(jnp.mean(tf * tf, axis=-1, keepdims=True) + QK_EPS) * g).astype(t.dtype)


def _axial_rope_tables(n_tok, dtype):
    pos = jnp.arange(n_tok, dtype=jnp.int32)
    rows = (pos // GRID_W).astype(jnp.float32)
    cols = (pos % GRID_W).astype(jnp.float32)
    n_freq = HEAD_DIM // 4
    inv_freq = ROPE_THETA ** (-jnp.arange(n_freq, dtype=jnp.float32) / n_freq)
    ang_r = rows[:, None, None] * inv_freq
    ang_c = cols[:, None, None] * inv_freq
    return (jnp.cos(ang_r).astype(dtype), jnp.sin(ang_r).astype(dtype),
            jnp.cos(ang_c).astype(dtype), jnp.sin(ang_c).astype(dtype))


def _rotate_half(t, cos, sin):
    t1, t2 = jnp.split(t, 2, axis=-1)
    return jnp.concatenate([t1 * cos - t2 * sin, t2 * cos + t1 * sin], axis=-1)


def _rope_2d(t, tables):
    cos_r, sin_r, cos_c, sin_c = tables
    t_row, t_col = jnp.split(t, 2, axis=-1)
    return jnp.concatenate([_rotate_half(t_row, cos_r, sin_r), _rotate_half(t_col, cos_c, sin_c)], axis=-1)


def _gqa_scores(q, k):
    return jnp.einsum("bqhgd,bkhd->bhgqk", q, k).astype(jnp.float32) * (HEAD_DIM ** -0.5)


def _gqa_values(p, v):
    return jnp.einsum("bhgqk,bkhd->bqhgd", p.astype(v.dtype), v)


def _sink_softmax(scores, sink):
    sink = sink.astype(jnp.float32)[:, :, None, None]
    m = jnp.maximum(jnp.max(scores, axis=-1, keepdims=True), sink)
    e = jnp.exp(scores - m)
    return e / (jnp.sum(e, axis=-1, keepdims=True) + jnp.exp(sink - m))


def _windowed_attention(q, k, v, k_ctx, v_ctx, sink):
    bsz, n_tok = q.shape[:2]
    pad = ((0, 0), (BLOCK, BLOCK), (0, 0), (0, 0))
    k_pad = jnp.pad(k, pad)
    v_pad = jnp.pad(v, pad)
    ctx_mask = jnp.ones((BLOCK, k_ctx.shape[1]), dtype=bool)

    def one_block(i):
        start = i * BLOCK
        q_blk = lax.dynamic_slice_in_dim(q, start, BLOCK, axis=1)
        k_blk = jnp.concatenate([lax.dynamic_slice_in_dim(k_pad, start, 3 * BLOCK, axis=1), k_ctx], axis=1)
        v_blk = jnp.concatenate([lax.dynamic_slice_in_dim(v_pad, start, 3 * BLOCK, axis=1), v_ctx], axis=1)
        q_pos = start + jnp.arange(BLOCK)
        k_pos = start - BLOCK + jnp.arange(3 * BLOCK)
        band = ((jnp.abs(q_pos[:, None] - k_pos[None, :]) <= WINDOW)
                & (k_pos[None, :] >= 0) & (k_pos[None, :] < n_tok))
        mask = jnp.concatenate([band, ctx_mask], axis=1)
        scores = jnp.where(mask, _gqa_scores(q_blk, k_blk), -jnp.inf)
        return _gqa_values(_sink_softmax(scores, sink), v_blk)

    out = lax.map(one_block, jnp.arange(n_tok // BLOCK))
    return jnp.moveaxis(out, 0, 1).reshape(bsz, n_tok, -1)


def _global_attention(q, k_all, v_all):
    bsz, n_tok = q.shape[:2]

    def one_block(i):
        q_blk = lax.dynamic_slice_in_dim(q, i * BLOCK, BLOCK, axis=1)
        p = jax.nn.softmax(_gqa_scores(q_blk, k_all), axis=-1)
        return _gqa_values(p, v_all)

    out = lax.map(one_block, jnp.arange(n_tok // BLOCK))
    return jnp.moveaxis(out, 0, 1).reshape(bsz, n_tok, -1)


def _merge_branches(o_a, o_b, gate_logits, w_branch_a, w_branch_b, w_out):
    g_a, g_b = jnp.split(jax.nn.sigmoid(gate_logits), 2, axis=-1)
    return (g_a * (o_a @ w_branch_a) + g_b * (o_b @ w_branch_b)) @ w_out


def _context_kv(h_c, w_in, b_in, k_norm_g):
    bsz, n_ctx, _ = h_c.shape
    kv_a = (h_c @ w_in[:, OFF_KA:OFF_QB] + b_in[OFF_KA:OFF_QB]).reshape(bsz, n_ctx, 2 * A_KV_HEADS, HEAD_DIM)
    kv_b = (h_c @ w_in[:, OFF_KB:OFF_GA] + b_in[OFF_KB:OFF_GA]).reshape(bsz, n_ctx, 2 * B_KV_HEADS, HEAD_DIM)
    k_a, v_a = jnp.split(kv_a, 2, axis=2)
    k_b, v_b = jnp.split(kv_b, 2, axis=2)
    return (k_a, v_a, _qk_rms(k_b, k_norm_g), v_b)


def _latent_token_mixer(h, kv_ctx, w_in, b_in, sink, q_norm_g, k_norm_g,
                        w_branch_a, w_branch_b, w_out, rope):
    bsz, n_tok, _ = h.shape
    k_a_c, v_a_c, k_b_c, v_b_c = kv_ctx
    proj = h @ w_in + b_in

    def heads(lo, hi, n):
        return proj[..., lo:hi].reshape(bsz, n_tok, n, HEAD_DIM)

    q_a = _rope_2d(heads(OFF_QA, OFF_KA, A_HEADS), rope).reshape(bsz, n_tok, A_KV_HEADS, A_GROUPS, HEAD_DIM)
    k_a = _rope_2d(heads(OFF_KA, OFF_VA, A_KV_HEADS), rope)
    v_a = heads(OFF_VA, OFF_QB, A_KV_HEADS)
    q_b = _rope_2d(_qk_rms(heads(OFF_QB, OFF_KB, B_HEADS), q_norm_g), rope).reshape(
        bsz, n_tok, B_KV_HEADS, B_GROUPS, HEAD_DIM)
    k_b = _rope_2d(_qk_rms(heads(OFF_KB, OFF_VB, B_KV_HEADS), k_norm_g), rope)
    v_b = heads(OFF_VB, OFF_GA, B_KV_HEADS)
    o_a = _windowed_attention(q_a, k_a, v_a, k_a_c, v_a_c, sink.reshape(A_KV_HEADS, A_GROUPS))
    o_b = _global_attention(q_b, jnp.concatenate([k_b, k_b_c], axis=1), jnp.concatenate([v_b, v_b_c], axis=1))
    return _merge_branches(o_a, o_b, proj[..., OFF_GA:], w_branch_a, w_branch_b, w_out)


def _context_token_mixer(h_c, kv_ctx, w_in, b_in, sink, q_norm_g, w_branch_a, w_branch_b, w_out):
    bsz, n_ctx, _ = h_c.shape
    k_a_c, v_a_c, k_b_c, v_b_c = kv_ctx
    q_a = (h_c @ w_in[:, OFF_QA:OFF_KA] + b_in[OFF_QA:OFF_KA]).reshape(bsz, n_ctx, A_KV_HEADS, A_GROUPS, HEAD_DIM)
    q_b = _qk_rms((h_c @ w_in[:, OFF_QB:OFF_KB] + b_in[OFF_QB:OFF_KB]).reshape(bsz, n_ctx, B_HEADS, HEAD_DIM),
                  q_norm_g).reshape(bsz, n_ctx, B_KV_HEADS, B_GROUPS, HEAD_DIM)
    gate_logits = h_c @ w_in[:, OFF_GA:] + b_in[OFF_GA:]
    o_a = _gqa_values(_sink_softmax(_gqa_scores(q_a, k_a_c), sink.reshape(A_KV_HEADS, A_GROUPS)), v_a_c)
    o_b = _gqa_values(jax.nn.softmax(_gqa_scores(q_b, k_b_c), axis=-1), v_b_c)
    return _merge_branches(o_a.reshape(bsz, n_ctx, -1), o_b.reshape(bsz, n_ctx, -1), gate_logits,
                           w_branch_a, w_branch_b, w_out)


def _conv_ffn(h, w_up, conv_w, conv_b, w_down):
    u = h @ w_up
    half = CONV_WIDTH // 2
    u = lax.conv_general_dilated(u, conv_w[:, None, :], window_strides=(1,), padding=((half, half),),
                                 dimension_numbers=("NWC", "WIO", "NWC"),
                                 feature_group_count=u.shape[-1]) + conv_b
    gate, val = jnp.split(u, 2, axis=-1)
    return (jax.nn.silu(gate) * val) @ w_down


def setup_inputs(seed: int = 0) -> dict:
    key = jax.random.key(seed)
    ks = jax.random.split(key, 24)
    f32 = jnp.float32

    def normal(k, shape, scale):
        return jax.random.normal(k, shape, f32) * scale

    L = DEPTH
    return {
        "x": normal(ks[0], (BATCH, SEQ, D_MODEL), 1.0),
        "c": normal(ks[1], (BATCH, D_MODEL), 1.0),
        "ctx": normal(ks[2], (BATCH, CTX_LEN, D_MODEL), 1.0),
        "c_ctx": normal(ks[3], (D_MODEL,), 1.0),
        "w_mod": normal(ks[4], (L, D_MODEL, N_MOD * D_MODEL), 0.5 * D_MODEL ** -0.5),
        "b_mod": normal(ks[5], (L, N_MOD * D_MODEL), 0.02),
        "w_in": normal(ks[6], (L, D_MODEL, IN_COLS), D_MODEL ** -0.5),
        "b_in": normal(ks[7], (L, IN_COLS), 0.02),
        "attn_sink": normal(ks[8], (L, A_HEADS), 0.5),
        "q_norm_g": 1.0 + normal(ks[9], (L, HEAD_DIM), 0.05),
        "k_norm_g": 1.0 + normal(ks[10], (L, HEAD_DIM), 0.05),
        "w_branch_a": normal(ks[11], (L, A_HEADS * HEAD_DIM, D_MODEL), (A_HEADS * HEAD_DIM) ** -0.5),
        "w_branch_b": normal(ks[12], (L, B_HEADS * HEAD_DIM, D_MODEL), (B_HEADS * HEAD_DIM) ** -0.5),
        "w_out": normal(ks[13], (L, D_MODEL, D_MODEL), DEEPNORM_BETA * D_MODEL ** -0.5),
        "ln1_g": 1.0 + normal(ks[14], (L, D_MODEL), 0.05),
        "ln1_b": normal(ks[15], (L, D_MODEL), 0.02),
        "w_up": normal(ks[16], (L, D_MODEL, 2 * D_FF), D_MODEL ** -0.5),
        "conv_w": normal(ks[17], (L, CONV_WIDTH, 2 * D_FF), CONV_WIDTH ** -0.5),
        "conv_b": normal(ks[18], (L, 2 * D_FF), 0.02),
        "w_down": normal(ks[19], (L, D_FF, D_MODEL), DEEPNORM_BETA * D_FF ** -0.5),
        "ln2_g": 1.0 + normal(ks[20], (L, D_MODEL), 0.05),
        "ln2_b": normal(ks[21], (L, D_MODEL), 0.02),
    }


def reference(x, c, ctx, c_ctx, w_mod, b_mod, w_in, b_in, attn_sink, q_norm_g, k_norm_g,
              w_branch_a, w_branch_b, w_out, ln1_g, ln1_b, w_up, conv_w, conv_b, w_down,
              ln2_g, ln2_b):
    n_tok = x.shape[1]
    rope = _axial_rope_tables(n_tok, x.dtype)
    for l in range(DEPTH):
        last = l == DEPTH - 1
        mod = jax.nn.silu(c) @ w_mod[l] + b_mod[l]
        shift1, scale1, gate1, shift2, scale2, gate2 = jnp.split(mod[:, None, :], N_MOD, axis=-1)
        n_mod_c = 2 if last else N_MOD
        mod_c = jax.nn.silu(c_ctx) @ w_mod[l, :, :n_mod_c * D_MODEL] + b_mod[l, :n_mod_c * D_MODEL]
        mods_c = jnp.split(mod_c, n_mod_c)
        h_c = ctx * (1.0 + mods_c[1]) + mods_c[0]
        kv_ctx = _context_kv(h_c, w_in[l], b_in[l], k_norm_g[l])

        h = x * (1.0 + scale1) + shift1
        y = _latent_token_mixer(h, kv_ctx, w_in[l], b_in[l], attn_sink[l], q_norm_g[l], k_norm_g[l],
                                w_branch_a[l], w_branch_b[l], w_out[l], rope)
        x = _layer_norm(DEEPNORM_ALPHA * x + gate1 * y, ln1_g[l], ln1_b[l])

        h = x * (1.0 + scale2) + shift2
        y = _conv_ffn(h, w_up[l], conv_w[l], conv_b[l], w_down[l])
        x = _layer_norm(DEEPNORM_ALPHA * x + gate2 * y, ln2_g[l], ln2_b[l])

        if not last:
            y_c = _context_token_mixer(h_c, kv_ctx, w_in[l], b_in[l], attn_sink[l], q_norm_g[l],
                                       w_branch_a[l], w_branch_b[l], w_out[l])
            ctx = _layer_norm(DEEPNORM_ALPHA * ctx + mods_c[2] * y_c, ln1_g[l], ln1_b[l])
            h_c = ctx * (1.0 + mods_c[4]) + mods_c[3]
            y_c = _conv_ffn(h_c, w_up[l], conv_w[l], conv_b[l], w_down[l])
            ctx = _layer_norm(DEEPNORM_ALPHA * ctx + mods_c[5] * y_c, ln2_g[l], ln2_b[l])
    return x
```

```python
import contextlib
import numpy as np
import concourse.bass as bass
import concourse.mybir as mybir
from concourse.bass_utils import run_bass_kernel_spmd

F32 = mybir.dt.float32
BF16 = mybir.dt.bfloat16
AF = mybir.ActivationFunctionType
ALU = mybir.AluOpType
AX = mybir.AxisListType

D = 1024
KC = 8
GRID_W = 64
CTX = 256
HD = 64
DFF = 2816
NFC = 22
LN_EPS = 1e-5
QK_EPS = 1e-6
ALPHA = 2.0 ** 0.25
OFF_QA, OFF_KA, OFF_VA, OFF_QB, OFF_KB, OFF_VB, OFF_GA, OFF_GB = 0, 512, 640, 768, 1280, 1408, 1536, 2560
F_QA, F_QAS, F_QB, F_QBS, F_KA, F_KAS, F_KB, F_KBS, F_G = 0, 4, 8, 12, 16, 17, 18, 19, 20
NF = 36


class Buf:
    __slots__ = ("name", "w", "r")

    def __init__(self, name=""):
        self.name = name
        self.w = None
        self.r = []


class _Op:
    __slots__ = ("eng", "fn", "deps", "dma", "sig", "cnt", "sem")

    def __init__(self, eng, fn, deps, dma):
        self.eng, self.fn, self.deps, self.dma = eng, fn, deps, dma
        self.sig, self.cnt, self.sem = False, 0, None


COMPUTE = ("pe", "act", "dve", "pool")


class Sched:
    def __init__(self, nc, n_dma_sems=24, same_engine_sync=True):
        self.nc = nc
        self.ops = []
        self.n_dma_sems = n_dma_sems
        self.same_engine_sync = same_engine_sync
        self.last_on = {}
        self.pending = {}
        self.dma_ids = []

    def op(self, eng, fn, reads=(), writes=(), dma=False):
        deps = set()
        for b in reads:
            if b.w is not None:
                deps.add(b.w)
        for b in writes:
            if b.w is not None:
                deps.add(b.w)
            deps.update(b.r)
        i = len(self.ops)
        for b in reads:
            b.r.append(i)
        for b in writes:
            b.w = i
            b.r = []
        if eng in self.pending:
            deps.update(self.pending.pop(eng))
        self.ops.append(_Op(eng, fn, deps, dma))
        self.last_on[eng] = i
        if dma:
            self.dma_ids.append(i)
        return i

    def dma(self, queue, fn, reads=(), writes=()):
        return self.op(queue, fn, reads, writes, dma=True)

    def barrier(self):
        lasts = set(v for e, v in self.last_on.items() if e in COMPUTE)
        lasts.update(self.dma_ids[-self.n_dma_sems:])
        for e in ("pe", "act", "dve", "pool", "sp"):
            self.pending.setdefault(e, set()).update(lasts)

    def emit(self, final_wait_engine="sp"):
        nc, ops = self.nc, self.ops
        for o in ops:
            for d in o.deps:
                p = ops[d]
                if p.dma:
                    continue
                if p.eng == o.eng and not o.dma and (o.eng == "pe" or not self.same_engine_sync):
                    continue
                p.sig = True
        for e, i in self.last_on.items():
            if e in COMPUTE:
                ops[i].sig = True
        with contextlib.ExitStack() as st:
            esem = {e: st.enter_context(nc.semaphore("s_" + e)) for e in COMPUTE}
            dsems = [st.enter_context(nc.semaphore("d_%d" % k)) for k in range(self.n_dma_sems)]
            ecnt = {e: 0 for e in COMPUTE}
            dcnt = [0] * self.n_dma_sems
            rr = 0
            for o in ops:
                if o.dma:
                    o.sem = rr
                    dcnt[rr] += 16
                    o.cnt = dcnt[rr]
                    rr = (rr + 1) % self.n_dma_sems
                elif o.sig:
                    ecnt[o.eng] += 1
                    o.cnt = ecnt[o.eng]
            per_eng = {}
            for i, o in enumerate(ops):
                per_eng.setdefault(o.eng, []).append(i)
            block = st.enter_context(nc.Block())

            def run(engname, eng):
                seen = {}

                def wait(key, semh, val):
                    if seen.get(key, 0) >= val:
                        return
                    seen[key] = val
                    eng.wait_ge(semh, val)

                for i in per_eng.get(engname, []):
                    o = ops[i]
                    for d in sorted(o.deps):
                        p = ops[d]
                        if p.dma:
                            wait(("d", p.sem), dsems[p.sem], p.cnt)
                        else:
                            if p.eng == engname and not o.dma and (engname == "pe" or not self.same_engine_sync):
                                continue
                            wait(("e", p.eng), esem[p.eng], p.cnt)
                    if o.dma:
                        if o.cnt > 16:
                            wait(("d", o.sem), dsems[o.sem], o.cnt - 16)
                        o.fn(eng).then_inc(dsems[o.sem], 16)
                    else:
                        ins = o.fn(eng)
                        if o.sig:
                            ins.then_inc(esem[o.eng], 1)
                if engname == final_wait_engine:
                    for e in COMPUTE:
                        if ecnt[e]:
                            wait(("e", e), esem[e], ecnt[e])
                    for k in range(self.n_dma_sems):
                        if dcnt[k]:
                            wait(("d", k), dsems[k], dcnt[k])

            @block.tensor
            def _(e):
                run("pe", e)

            @block.scalar
            def _(e):
                run("act", e)

            @block.vector
            def _(e):
                run("dve", e)

            @block.gpsimd
            def _(e):
                run("pool", e)

            @block.sync
            def _(e):
                run("sp", e)


PERM = np.r_[0:16, 32:48, 16:32, 48:64]
PERM_SW = PERM[(np.arange(64) + 32) % 64]


def _chunk_cols():
    cols = np.zeros((NF, 128), np.int64)
    for g in range(4):
        for kvh in range(2):
            cols[F_QA + g, kvh * 64:(kvh + 1) * 64] = OFF_QA + (kvh * 4 + g) * 64 + PERM
            cols[F_QAS + g, kvh * 64:(kvh + 1) * 64] = OFF_QA + (kvh * 4 + g) * 64 + PERM_SW
            cols[F_QB + g, kvh * 64:(kvh + 1) * 64] = OFF_QB + (kvh * 4 + g) * 64 + PERM
            cols[F_QBS + g, kvh * 64:(kvh + 1) * 64] = OFF_QB + (kvh * 4 + g) * 64 + PERM_SW
    for kvh in range(2):
        cols[F_KA, kvh * 64:(kvh + 1) * 64] = OFF_KA + kvh * 64 + PERM
        cols[F_KAS, kvh * 64:(kvh + 1) * 64] = OFF_KA + kvh * 64 + PERM_SW
        cols[F_KB, kvh * 64:(kvh + 1) * 64] = OFF_KB + kvh * 64 + PERM
        cols[F_KBS, kvh * 64:(kvh + 1) * 64] = OFF_KB + kvh * 64 + PERM_SW
    for j in range(16):
        cols[F_G + j] = OFF_GA + j * 128 + np.arange(128)
    return cols


def _rope_tables(pos, valid):
    n = pos.shape[0]
    rows = (pos // GRID_W).astype(np.float32)
    colsp = (pos % GRID_W).astype(np.float32)
    nfreq = HD // 4
    inv = (np.float32(10000.0) ** (-np.arange(nfreq, dtype=np.float32) / np.float32(nfreq))).astype(np.float32)
    ang_r = rows[None, :] * inv[:, None]
    ang_c = colsp[None, :] * inv[:, None]
    cos64 = np.zeros((64, n), np.float32)
    sin64 = np.zeros((64, n), np.float32)
    for half in range(2):
        sgn = -1.0 if half == 0 else 1.0
        cos64[half * 32:half * 32 + 16] = np.cos(ang_r)
        cos64[half * 32 + 16:half * 32 + 32] = np.cos(ang_c)
        sin64[half * 32:half * 32 + 16] = sgn * np.sin(ang_r)
        sin64[half * 32 + 16:half * 32 + 32] = sgn * np.sin(ang_c)
    cos64[:, ~valid] = 1.0
    sin64[:, ~valid] = 0.0
    return np.concatenate([cos64, cos64], 0), np.concatenate([sin64, sin64], 0)


def _shared_inputs(inp):
    f = np.float32
    w_in = np.asarray(inp["w_in"][0], f)
    b_in = np.asarray(inp["b_in"][0], f)
    cols = _chunk_cols()
    sh = {}
    sh["w_in_fm"] = np.ascontiguousarray(w_in[:, cols.reshape(-1)])
    sh["w_in_v"] = np.ascontiguousarray(np.concatenate([w_in[:, OFF_VA:OFF_VA + 128], w_in[:, OFF_VB:OFF_VB + 128]], 1))
    sh["bias_fm"] = np.ascontiguousarray(b_in[cols].T)
    gq = np.asarray(inp["q_norm_g"][0], f)
    gk = np.asarray(inp["k_norm_g"][0], f)
    cv = np.stack([np.tile(gq[PERM], 2), np.tile(gq[PERM_SW], 2), np.tile(gk[PERM], 2), np.tile(gk[PERM_SW], 2)], 1)
    sh["cvec"] = np.ascontiguousarray(cv.astype(f))
    vb = np.concatenate([b_in[OFF_VA:OFF_VA + 128], b_in[OFF_VB:OFF_VB + 128]])
    sh["vbias"] = np.ascontiguousarray(np.tile(vb[None, :], (128, 1)))
    p = np.arange(128)
    sh["blockones"] = (p[:, None] // 64 == p[None, :] // 64).astype(f)
    sh["ident"] = np.eye(128, dtype=f)
    sink = np.asarray(inp["attn_sink"][0], f)
    sr = np.zeros((128, 4, 128), f)
    for g in range(4):
        sr[0:64, g, :] = sink[4 + g]
        sr[64:128, g, :] = sink[g]
    sh["sinkraw"] = sr.reshape(128, 512)
    for nm, key in (("wba", "w_branch_a"), ("wbb", "w_branch_b")):
        w = np.asarray(inp[key][0], f).reshape(2, 4, 64, D)
        sh[nm] = np.ascontiguousarray(w.transpose(0, 2, 1, 3).reshape(128, 4 * D))
    sh["w_out"] = np.asarray(inp["w_out"][0], f)
    sh["w_mod"] = np.asarray(inp["w_mod"][0], f)
    sh["b_mod"] = np.asarray(inp["b_mod"], f).reshape(1, 6 * D)
    sh["lnp"] = np.ascontiguousarray(np.tile(np.stack([inp["ln1_g"][0], inp["ln1_b"][0], inp["ln2_g"][0],
                                                        inp["ln2_b"][0]], 0).astype(f).reshape(1, 4 * D), (128, 1)))
    sh["w_up"] = np.asarray(inp["w_up"][0], f)
    sh["w_down"] = np.asarray(inp["w_down"][0], f)
    cw = np.asarray(inp["conv_w"][0], f)
    sh["cw"] = np.ascontiguousarray(cw.reshape(3, 44, 128).transpose(2, 1, 0).reshape(128, 132))
    sh["cb"] = np.ascontiguousarray(np.asarray(inp["conv_b"][0], f).reshape(44, 128).T)
    return sh


def _core_inputs(inp, sh, core, S):
    f = np.float32
    NOWN = S // 4
    b, r = core // 4, core % 4
    start, end = r * NOWN, (r + 1) * NOWN
    x = np.asarray(inp["x"][b], f)
    NSLOT = S + 1024
    pos = np.full(NSLOT, -1, np.int64)
    pos[0:S] = (start + np.arange(S)) % S
    pos[S:S + 128] = start - 128 + np.arange(128)
    pos[S + 128:S + 256] = end + np.arange(128)
    pos[S + 256] = start - 129
    pos[S + 257] = end + 128
    pos[S + 384] = start - 1
    pos[S + 385] = end
    valid = (pos >= 0) & (pos < S)
    xs = np.zeros((NSLOT, D), f)
    xs[valid] = x[pos[valid]]
    xs[S + 512:S + 768] = np.asarray(inp["ctx"][b], f)
    d = dict(sh)
    d["xT"] = np.ascontiguousarray(xs.T.reshape(KC, 128, NSLOT))
    d["xtok"] = np.ascontiguousarray(np.concatenate([x[start:end], xs[S + 384:S + 512]], 0))
    cs, sn = _rope_tables(np.where(valid, pos, 0), valid)
    d["cs"], d["sn"] = np.ascontiguousarray(cs), np.ascontiguousarray(sn)
    cf = np.stack([np.asarray(inp["c"][b], f), np.asarray(inp["c_ctx"], f)], 1)
    d["c_fm"] = np.ascontiguousarray(cf.reshape(KC, 128, 2).transpose(1, 0, 2).reshape(128, 16))
    j = np.arange(128)[:, None]
    q = np.arange(128)[None, :]
    lv, rv = float(start > 0), float(end < S)
    m = np.zeros((9, 128, 128), f)
    m[0] = (j >= q)
    m[1] = (j >= q) * lv
    m[2] = (j <= q)
    m[3] = (j <= q) * rv
    m[4][0, 0] = lv
    m[4][1, 1] = rv
    m[5][:, 0] = lv
    m[6][:, 0] = 1.0
    m[7][:, 1] = 1.0
    m[8][:, 1] = rv
    d["masks"] = np.ascontiguousarray(m.transpose(1, 0, 2).reshape(128, 9 * 128))
    d["hm"] = np.ascontiguousarray(np.tile(np.array([[lv, rv]], f), (128, 1)))
    return d


IN_SHAPES = lambda S: {
    "xT": [KC, 128, S + 1024], "xtok": [S // 4 + 128, D], "cs": [128, S + 1024], "sn": [128, S + 1024],
    "c_fm": [128, 16], "w_mod": [D, 6 * D], "b_mod": [1, 6 * D], "w_in_fm": [D, NF * 128], "w_in_v": [D, 256],
    "bias_fm": [128, NF], "cvec": [128, 4], "vbias": [128, 256], "blockones": [128, 128], "ident": [128, 128],
    "masks": [128, 9 * 128], "sinkraw": [128, 512], "wba": [128, 4 * D], "wbb": [128, 4 * D], "w_out": [D, D],
    "lnp": [128, 4 * D], "w_up": [D, 2 * DFF], "w_down": [DFF, D], "cw": [128, 132], "cb": [128, 44], "hm": [128, 2],
}


def _bc_mid(ap, n):
    return bass.AP(ap.tensor, ap.offset, [list(ap.ap[0]), [0, n], list(ap.ap[1])])


DEBUG = False


def build_program(S):
    NOWN = S // 4
    NT = NOWN // 128
    NTQ = NT + 1
    NSLOT = S + 1024
    NCH = NSLOT // 512
    NCH_OWN = NOWN // 512
    NQ = NOWN + 512
    NKB = S // 128 + 2
    NA = NOWN + 1024
    NAB = NA // 128
    NBS = NSLOT // 128
    X1 = S
    X2 = S + 512
    nc = bass.Bass("TRN2", target_bir_lowering=False)
    shp = IN_SHAPES(S)
    din = {k: nc.dram_tensor(k, v, F32, kind="ExternalInput").ap() for k, v in shp.items()}
    out_d = nc.dram_tensor("out", [NOWN, D], F32, kind="ExternalOutput").ap()
    qa_d = nc.dram_tensor("qa_d", [4, 128, NQ], BF16, kind=("ExternalOutput" if DEBUG else "Internal")).ap()
    qb_d = nc.dram_tensor("qb_d", [4, 128, NQ], BF16, kind=("ExternalOutput" if DEBUG else "Internal")).ap()
    g_d = nc.dram_tensor("g_d", [16, 128, NQ], BF16, kind=("ExternalOutput" if DEBUG else "Internal")).ap()
    oa_d = nc.dram_tensor("oa_d", [NTQ, 128, 512], BF16, kind=("ExternalOutput" if DEBUG else "Internal")).ap()
    ob_d = nc.dram_tensor("ob_d", [NTQ, 128, 512], BF16, kind=("ExternalOutput" if DEBUG else "Internal")).ap()
    x1_d = nc.dram_tensor("x1_d", [NOWN + 128, D], F32, kind=("ExternalOutput" if DEBUG else "Internal")).ap()
    h2_d = nc.dram_tensor("h2_d", [KC, 128, NOWN + 2], BF16, kind=("ExternalOutput" if DEBUG else "Internal")).ap()
    gate_d = nc.dram_tensor("gate_d", [128, 2 * D], F32, kind=("ExternalOutput" if DEBUG else "Internal")).ap()

    S_ = Sched(nc)
    op, dma = S_.op, S_.dma
    PS = nc.alloc_psum_tensor("ps_all", [128, 4096], F32)

    def bank(i, n=1):
        return PS[:, i * 512:(i + n) * 512]

    PB = [Buf("bank%d" % i) for i in range(8)]

    class Arena:
        def __init__(self, base):
            self.off = base

        def alloc(self, name, cols, dt):
            nbytes = cols * (4 if dt == F32 else 2)
            nbytes = (nbytes + 31) // 32 * 32
            t = nc.alloc_sbuf_tensor_at(name, [128, cols], dt, offset=self.off)
            self.off += nbytes
            assert self.off <= 229376, (name, self.off)
            return t

    A0 = Arena(16384 + 256)
    modfm = A0.alloc("modfm", 64, F32)
    mod1p = A0.alloc("mod1p", 32, F32)
    bias_fm = A0.alloc("bias_fm_sb", NF, F32)
    cvec = A0.alloc("cvec_sb", 4, F32)
    ident = A0.alloc("ident_sb", 128, F32)
    hm = A0.alloc("hm_sb", 2, F32)
    epsb = A0.alloc("epsb", 2, F32)
    B_const = Buf("const")
    PERSIST = A0.off

    A = Arena(PERSIST)
    c_sb = A.alloc("c_sb", 16, F32)
    sc = A.alloc("sc", 16, F32)
    ones_t = A.alloc("ones_t", 128, F32)
    screp = A.alloc("screp", 2 * KC * 128, F32)
    bmod = A.alloc("bmod_sb", 6 * D, F32)
    modbc = A.alloc("modbc", 6 * D, F32)
    modcc = A.alloc("modcc", 2 * D, F32)
    tmpd = A.alloc("tmpd", 6 * D, F32)
    wst = [A.alloc("wst%d" % i, KC * 512, F32) for i in range(2)]
    b_c, b_sc, b_ones, b_screp, b_bmod, b_modbc, b_modcc, b_tmpd = [Buf(n) for n in "c sc ones screp bmod modbc modcc tmpd".split()]
    b_wst = [Buf("wst0"), Buf("wst1")]

    dma("sp", lambda e: e.dma_start(out=c_sb[:], in_=din["c_fm"]), writes=[b_c])
    dma("sp", lambda e: e.dma_start(out=bias_fm[:], in_=din["bias_fm"]), writes=[B_const])
    dma("sp", lambda e: e.dma_start(out=cvec[:], in_=din["cvec"]), writes=[B_const])
    dma("sp", lambda e: e.dma_start(out=ident[:], in_=din["ident"]), writes=[B_const])
    dma("sp", lambda e: e.dma_start(out=hm[:], in_=din["hm"]), writes=[B_const])
    dma("sp", lambda e: e.dma_start(out=bmod[0:1, :], in_=din["b_mod"]), writes=[b_bmod])
    op("dve", lambda e: e.memset(ones_t[:], 1.0), writes=[b_ones])

    def _eps(e):
        e.memset(epsb[:, 0:1], LN_EPS)
        return e.memset(epsb[:, 1:2], QK_EPS)
    op("dve", _eps, writes=[B_const])
    op("act", lambda e: e.activation(sc[:], c_sb[:], AF.Silu), reads=[b_c], writes=[b_sc])
    for j in range(2):
        for kc in range(KC):
            op("dve", lambda e, j=j, kc=kc: e.tensor_scalar(screp[:, (j * KC + kc) * 128:(j * KC + kc + 1) * 128], ones_t[:],
                                                           sc[:, kc * 2 + j:kc * 2 + j + 1], None, ALU.mult),
               reads=[b_ones, b_sc], writes=[b_screp])
    wmod_v = din["w_mod"].rearrange("(kc p) n -> p kc n", p=128)
    jobs = [(0, cj) for cj in range(12)] + [(1, cj) for cj in range(4)]
    for ji, (who, cj) in enumerate(jobs):
        wb = ji % 2
        dma("sp", lambda e, wb=wb, cj=cj: e.dma_start(out=wst[wb][:, :].rearrange("p (kc n) -> p kc n", kc=KC),
                                                       in_=wmod_v[:, :, cj * 512:(cj + 1) * 512]), writes=[b_wst[wb]])
        pb = ji % 2

        def _mm(e, who=who, cj=cj, wb=wb, pb=pb):
            for kc in range(KC):
                e.matmul(bank(pb), screp[:, (who * KC + kc) * 128:(who * KC + kc + 1) * 128],
                         wst[wb][:, kc * 512:(kc + 1) * 512], start=(kc == 0), stop=False)
            return e.matmul(bank(pb), ones_t[0:1, :], bmod[0:1, cj * 512:(cj + 1) * 512], start=False, stop=True)
        op("pe", _mm, reads=[b_screp, b_wst[wb], b_bmod, b_ones], writes=[PB[pb]])
        dst = (modbc if who == 0 else modcc)
        op("act", lambda e, dst=dst, cj=cj, pb=pb: e.activation(dst[:, cj * 512:(cj + 1) * 512], bank(pb), AF.Copy),
           writes=[PB[pb], b_modbc if who == 0 else b_modcc])
    op("dve", lambda e: e.tensor_tensor(tmpd[:, :].rearrange("p (c f) -> p c f", f=128),
                                        modbc[:, :].rearrange("p (c f) -> p c f", f=128), _bc_mid(ident[:, :], 48), ALU.mult),
       reads=[b_modbc, B_const], writes=[b_tmpd])
    op("dve", lambda e: e.tensor_reduce(modfm[:, 0:48], tmpd[:, :].rearrange("p (c f) -> p c f", f=128), AX.X, ALU.add),
       reads=[b_tmpd], writes=[B_const])
    op("dve", lambda e: e.tensor_tensor(tmpd[:, 0:2 * D].rearrange("p (c f) -> p c f", f=128),
                                        modcc[:, :].rearrange("p (c f) -> p c f", f=128), _bc_mid(ident[:, :], 16), ALU.mult),
       reads=[b_modcc, B_const], writes=[b_tmpd])
    op("dve", lambda e: e.tensor_reduce(modfm[:, 48:64], tmpd[:, 0:2 * D].rearrange("p (c f) -> p c f", f=128), AX.X, ALU.add),
       reads=[b_tmpd], writes=[B_const])

    def _m1p(e):
        e.tensor_scalar(mod1p[:, 0:8], modfm[:, 8:16], 1.0, None, ALU.add)
        e.tensor_scalar(mod1p[:, 8:16], modfm[:, 32:40], 1.0, None, ALU.add)
        return e.tensor_scalar(mod1p[:, 16:24], modfm[:, 56:64], 1.0, None, ALU.add)
    op("dve", _m1p, writes=[B_const])
    dma("sp", lambda e: e.dma_start(out=gate_d[:, 0:D], in_=modbc[:, 2 * D:3 * D]), reads=[b_modbc])
    dma("sp", lambda e: e.dma_start(out=gate_d[:, D:2 * D], in_=modbc[:, 5 * D:6 * D]), reads=[b_modbc])
    S_.barrier()

    AK = Arena(PERSIST)
    KbT = AK.alloc("KbT", NSLOT, BF16)
    Vb = AK.alloc("Vb", NBS * 192, BF16)
    KaT = AK.alloc("KaT", NA, BF16)
    Va = AK.alloc("Va", NAB * 192, BF16)
    KV_END = AK.off
    b_kv = Buf("kv")

    def _ones_init(e):
        e.memset(Vb[:, :].rearrange("p (b c) -> p b c", c=192)[:, :, 64:128], 1.0)
        return e.memset(Va[:, :].rearrange("p (b c) -> p b c", c=192)[:, :, 64:128], 1.0)
    op("pool", _ones_init, writes=[b_kv])

    w_fm_v = din["w_in_fm"].rearrange("(kc p) n -> p kc n", p=128)
    w_v_v = din["w_in_v"].rearrange("(kc p) n -> p kc n", p=128)

    def run_pass(pname, chunks, feats, vjob):
        A = Arena(KV_END)
        nfe = sorted(set(sum([[j["f"]] + ([j["fs"]] if "fs" in j else []) for j in feats], [])))
        fidx = {f: i for i, f in enumerate(nfe)}
        W = A.alloc(pname + "W", len(nfe) * KC * 128, BF16)
        Wv = A.alloc(pname + "Wv", KC * 256, BF16)
        vbias = A.alloc(pname + "vb", 256, F32)
        bones = A.alloc(pname + "bo", 128, F32)
        xs = [A.alloc(pname + "xs%d" % i, 512, F32) for i in range(3)]
        hT = [A.alloc(pname + "hT%d" % i, KC * 512, BF16) for i in range(2)]
        cs_t = A.alloc(pname + "cs", 512, F32)
        sn_t = A.alloc(pname + "sn", 512, F32)
        tt = [A.alloc(pname + "t%d" % i, 512, F32) for i in range(6)]
        ob = [A.alloc(pname + "ob%d" % i, 512, BF16) for i in range(3)]
        b_W, b_cs = Buf("W"), Buf("cs")
        b_xs = [Buf("xs") for _ in xs]
        b_hT = [Buf("hT") for _ in hT]
        b_tt = [Buf("t") for _ in tt]
        b_ob = [Buf("ob") for _ in ob]
        wsg = [A.alloc(pname + "wsg%d" % i, 1024, F32) for i in range(2)]
        b_wsg = [Buf("wsg") for _ in wsg]
        for n_, f in enumerate(nfe):
            w_ = n_ % 2
            dma("sp", lambda e, f=f, w_=w_: e.dma_start(out=wsg[w_][:, :].rearrange("p (kc c) -> p kc c", kc=KC),
                                                        in_=w_fm_v[:, :, f * 128:(f + 1) * 128]), writes=[b_wsg[w_]])
            op("dve", lambda e, f=f, w_=w_: e.tensor_copy(W[:, fidx[f] * KC * 128:(fidx[f] + 1) * KC * 128], wsg[w_][:]),
               reads=[b_wsg[w_]], writes=[b_W])
        if vjob:
            for hv in range(2):
                dma("sp", lambda e, hv=hv: e.dma_start(out=wsg[hv][:, :].rearrange("p (kc c) -> p kc c", kc=4),
                                                       in_=w_v_v[:, hv * 4:(hv + 1) * 4, :]), writes=[b_wsg[hv]])
                op("dve", lambda e, hv=hv: e.tensor_copy(Wv[:, hv * 1024:(hv + 1) * 1024], wsg[hv][:]),
                   reads=[b_wsg[hv]], writes=[b_W])
            dma("sp", lambda e: e.dma_start(out=vbias[:], in_=din["vbias"]), writes=[b_W])
        dma("sp", lambda e: e.dma_start(out=bones[:], in_=din["blockones"]), writes=[b_W])
        obi = 0
        pbi = 0
        for ci, c in enumerate(chunks):
            s0 = c * 512
            is_ctx = (s0 == X2)
            hb = ci % 2
            for kc in range(KC):
                xi = (ci * KC + kc) % 3
                dma("sp", lambda e, xi=xi, kc=kc, s0=s0: e.dma_start(out=xs[xi][:], in_=din["xT"][kc, :, s0:s0 + 512]),
                    writes=[b_xs[xi]])
                sc_ap = mod1p[:, 16 + kc:17 + kc] if is_ctx else mod1p[:, kc:kc + 1]
                sh_ap = modfm[:, 48 + kc:49 + kc] if is_ctx else modfm[:, kc:kc + 1]
                eng = "dve"
                op(eng, lambda e, xi=xi, kc=kc, hb=hb, sc_ap=sc_ap, sh_ap=sh_ap: e.tensor_scalar(
                    hT[hb][:, kc * 512:(kc + 1) * 512], xs[xi][:], sc_ap, sh_ap, ALU.mult, ALU.add),
                   reads=[b_xs[xi], B_const], writes=[b_hT[hb]])
            need_rope = any(j["kind"] == "rope" for j in feats)
            if need_rope:
                dma("sp", lambda e, s0=s0: e.dma_start(out=cs_t[:], in_=din["cs"][:, s0:s0 + 512]), writes=[b_cs])
                dma("sp", lambda e, s0=s0: e.dma_start(out=sn_t[:], in_=din["sn"][:, s0:s0 + 512]), writes=[b_cs])
            for j in feats:
                f = j["f"]

                def mm(e, f, pb, hb=hb):
                    for kc in range(KC):
                        r = e.matmul(bank(pb), W[:, (fidx[f] * KC + kc) * 128:(fidx[f] * KC + kc + 1) * 128],
                                     hT[hb][:, kc * 512:(kc + 1) * 512], start=(kc == 0), stop=(kc == KC - 1))
                    return r
                if j["kind"] == "gate":
                    pb = pbi % 4
                    pbi += 1
                    op("pe", lambda e, f=f, pb=pb, mm=mm: mm(e, f, pb), reads=[b_W, b_hT[hb]], writes=[PB[pb]])
                    o = obi % 3
                    obi += 1
                    op("act", lambda e, f=f, pb=pb, o=o: e.activation(ob[o][:], bank(pb), AF.Sigmoid, bias=bias_fm[:, f:f + 1]),
                       reads=[B_const], writes=[PB[pb], b_ob[o]])
                    dst = j["dst"](s0)
                    dma("sp", lambda e, o=o, dst=dst: e.dma_start(out=dst, in_=ob[o][:]), reads=[b_ob[o]])
                    continue
                fs = j["fs"]
                pa = pbi % 4
                pbi += 1
                pbs = pbi % 4
                pbi += 1
                op("pe", lambda e, f=f, pa=pa, mm=mm: mm(e, f, pa), reads=[b_W, b_hT[hb]], writes=[PB[pa]])
                op("pe", lambda e, fs=fs, pbs=pbs, mm=mm: mm(e, fs, pbs), reads=[b_W, b_hT[hb]], writes=[PB[pbs]])
                if j.get("dst_sb") is not None:
                    dst_ap, dst_buf = j["dst_sb"](s0), b_kv
                    o = None
                else:
                    o = obi % 3
                    obi += 1
                    dst_ap, dst_buf = ob[o][:], b_ob[o]
                if not j["norm"]:
                    op("dve", lambda e, f=f, pa=pa: e.scalar_tensor_tensor(tt[0][:], bank(pa), bias_fm[:, f:f + 1], cs_t[:], ALU.add, ALU.mult),
                       reads=[b_cs, B_const], writes=[PB[pa], b_tt[0]])
                    op("dve", lambda e, fs=fs, pbs=pbs: e.scalar_tensor_tensor(tt[1][:], bank(pbs), bias_fm[:, fs:fs + 1], sn_t[:], ALU.add, ALU.mult),
                       reads=[b_cs, B_const], writes=[PB[pbs], b_tt[1]])
                    op("pool", lambda e, dst_ap=dst_ap: e.tensor_tensor(dst_ap, tt[0][:], tt[1][:], ALU.add),
                       reads=[b_tt[0], b_tt[1]], writes=[dst_buf])
                else:
                    gi = j["g"]
                    op("act", lambda e, f=f, pa=pa: e.activation(tt[2][:], bank(pa), AF.Square, bias=bias_fm[:, f:f + 1]),
                       reads=[B_const], writes=[PB[pa], b_tt[2]])
                    op("dve", lambda e, f=f, pa=pa, gi=gi: e.tensor_scalar(tt[0][:], bank(pa), bias_fm[:, f:f + 1], cvec[:, gi:gi + 1], ALU.add, ALU.mult),
                       reads=[B_const], writes=[PB[pa], b_tt[0]])
                    op("dve", lambda e, fs=fs, pbs=pbs, gi=gi: e.tensor_scalar(tt[1][:], bank(pbs), bias_fm[:, fs:fs + 1], cvec[:, gi + 1:gi + 2], ALU.add, ALU.mult),
                       reads=[B_const], writes=[PB[pbs], b_tt[1]])
                    op("pe", lambda e: e.matmul(bank(4), bones[:], tt[2][:], start=True, stop=True), reads=[b_W, b_tt[2]], writes=[PB[4]])
                    op("pool", lambda e: e.tensor_tensor(tt[0][:], tt[0][:], cs_t[:], ALU.mult), reads=[b_cs], writes=[b_tt[0]])
                    op("pool", lambda e: e.tensor_tensor(tt[1][:], tt[1][:], sn_t[:], ALU.mult), reads=[b_cs], writes=[b_tt[1]])
                    op("pool", lambda e: e.tensor_tensor(tt[0][:], tt[0][:], tt[1][:], ALU.add), reads=[b_tt[1]], writes=[b_tt[0]])
                    op("act", lambda e: e.activation(tt[3][:], bank(4), AF.Sqrt, bias=epsb[:, 1:2], scale=1.0 / 64),
                       reads=[B_const], writes=[PB[4], b_tt[3]])
                    op("dve", lambda e: e.reciprocal(tt[3][:], tt[3][:]), writes=[b_tt[3]])
                    op("dve", lambda e, dst_ap=dst_ap: e.tensor_tensor(dst_ap, tt[0][:], tt[3][:], ALU.mult),
                       reads=[b_tt[0], b_tt[3]], writes=[dst_buf])
                if o is not None:
                    dst = j["dst"](s0)
                    dma("sp", lambda e, o=o, dst=dst: e.dma_start(out=dst, in_=ob[o][:]), reads=[b_ob[o]])
            if vjob:
                for blk in range(4):
                    pv = 5 + (blk % 2)
                    ncol = 256 if vjob == "ab" else 128
                    c0 = 0 if vjob == "ab" else 128

                    def mmv(e, blk=blk, pv=pv, ncol=ncol, c0=c0, hb=hb):
                        for kc in range(KC):
                            r = e.matmul(bank(pv)[:, 0:ncol], hT[hb][:, kc * 512 + blk * 128:kc * 512 + (blk + 1) * 128],
                                         Wv[:, kc * 256 + c0:kc * 256 + c0 + ncol], start=(kc == 0), stop=(kc == KC - 1))
                        return r
                    op("pe", mmv, reads=[b_W, b_hT[hb]], writes=[PB[pv]])
                    sblk = s0 // 128 + blk
                    if vjob == "ab":
                        ablk = (sblk if s0 < NOWN else NOWN // 128 + (s0 - S) // 128 + blk)
                        op("dve", lambda e, pv=pv, ablk=ablk: e.tensor_tensor(
                            Va[:, ablk * 192:(ablk + 1) * 192].rearrange("p (a c) -> p a c", c=64)[:, 0:3:2, :],
                            bank(pv)[:, 0:128].rearrange("p (a c) -> p a c", c=64),
                            vbias[:, 0:128].rearrange("p (a c) -> p a c", c=64), ALU.add),
                           reads=[b_W], writes=[PB[pv], b_kv])
                    op("dve", lambda e, pv=pv, sblk=sblk, ncol=ncol: e.tensor_tensor(
                        Vb[:, sblk * 192:(sblk + 1) * 192].rearrange("p (a c) -> p a c", c=64)[:, 0:3:2, :],
                        bank(pv)[:, ncol - 128:ncol].rearrange("p (a c) -> p a c", c=64),
                        vbias[:, 128:256].rearrange("p (a c) -> p a c", c=64), ALU.add),
                       reads=[b_W], writes=[PB[pv], b_kv])
        S_.barrier()

    def acol(s0):
        return s0 if s0 < NOWN else NOWN + (s0 - S)

    own_chunks = list(range(NCH_OWN))
    x1c, x2c = X1 // 512, X2 // 512
    run_pass("pA", list(range(NCH)),
             [dict(kind="rope", f=F_KB, fs=F_KBS, norm=True, g=2, dst_sb=lambda s0: KbT[:, s0:s0 + 512])], "b")
    featsB = [dict(kind="rope", f=F_KA, fs=F_KAS, norm=False, dst_sb=lambda s0: KaT[:, acol(s0):acol(s0) + 512])]
    for g in range(4):
        featsB.append(dict(kind="rope", f=F_QA + g, fs=F_QAS + g, norm=False, dst_sb=None,
                           dst=lambda s0, g=g: qa_d[g, :, acol(s0):acol(s0) + 512]))
        featsB.append(dict(kind="rope", f=F_QB + g, fs=F_QBS + g, norm=True, g=0, dst_sb=None,
                           dst=lambda s0, g=g: qb_d[g, :, acol(s0):acol(s0) + 512]))
    run_pass("pB", own_chunks + [x1c], featsB, "ab")
    run_pass("pB2", [x2c], featsB[:1], "ab")
    featsC = [dict(kind="gate", f=F_G + j, dst=lambda s0, j=j: g_d[j, :, acol(s0):acol(s0) + 512]) for j in range(16)]
    run_pass("pC", own_chunks + [x1c], featsC, None)

    if DEBUG:
        dbg_mod = nc.dram_tensor("dbg_mod", [128, 96], F32, kind="ExternalOutput").ap()
        dbg_kb = nc.dram_tensor("dbg_kb", [128, NSLOT], BF16, kind="ExternalOutput").ap()
        dbg_ka = nc.dram_tensor("dbg_ka", [128, NA], BF16, kind="ExternalOutput").ap()
        dbg_vb = nc.dram_tensor("dbg_vb", [128, NBS * 192], BF16, kind="ExternalOutput").ap()
        dbg_va = nc.dram_tensor("dbg_va", [128, NAB * 192], BF16, kind="ExternalOutput").ap()
        dma("sp", lambda e: e.dma_start(out=dbg_mod[:, 0:64], in_=modfm[:]), reads=[B_const])
        dma("sp", lambda e: e.dma_start(out=dbg_mod[:, 64:96], in_=mod1p[:]), reads=[B_const])
        dma("sp", lambda e: e.dma_start(out=dbg_kb, in_=KbT[:]), reads=[b_kv])
        dma("sp", lambda e: e.dma_start(out=dbg_ka, in_=KaT[:]), reads=[b_kv])
        dma("sp", lambda e: e.dma_start(out=dbg_vb, in_=Vb[:]), reads=[b_kv])
        dma("sp", lambda e: e.dma_start(out=dbg_va, in_=Va[:]), reads=[b_kv])
        S_.barrier()
    A = Arena(KV_END)
    masks = A.alloc("masks_sb", 9 * 128, BF16)
    sinkexp = A.alloc("sinkexp", 512, F32)
    qa_t = [A.alloc("qa_t%d" % i, 512, BF16) for i in range(2)]
    qb_t = [A.alloc("qb_t%d" % i, 512, BF16) for i in range(2)]
    pt = [A.alloc("pt%d" % i, 1024, BF16) for i in range(3)]
    rs = A.alloc("rs", 512, F32)
    rsh = A.alloc("rsh", 512, F32)
    oT = [A.alloc("oT%d" % i, 512, BF16) for i in range(4)]
    b_q = [Buf("q0"), Buf("q1")]
    b_pt = [Buf("pt") for _ in pt]
    b_rs, b_rsh = Buf("rs"), Buf("rsh")
    b_oT = [Buf("oT") for _ in oT]
    b_m = Buf("m")
    masks32 = A.alloc("masks32", 9 * 128, F32)
    b_m32 = Buf("m32")
    dma("sp", lambda e: e.dma_start(out=masks32[:], in_=din["masks"]), writes=[b_m32])
    op("dve", lambda e: e.tensor_copy(masks[:], masks32[:]), reads=[b_m32], writes=[b_m])
    dma("sp", lambda e: e.dma_start(out=sinkexp[:], in_=din["sinkraw"]), writes=[b_m])
    op("act", lambda e: e.activation(sinkexp[:], sinkexp[:], AF.Exp), writes=[b_m])
    SB = [0, 2]
    ACC = {"a": (4, 5), "b": (6, 7)}
    ablk0 = NOWN // 128
    ucount = [0]
    oi = [0]

    def attention(qi, br, qtile, units):
        a0, a1 = ACC[br]
        n = len(units)

        def s_op(u):
            KT, c0, V, vb, mk = units[u]
            sb = SB[(ucount[0] + u) % 2]

            def f(e):
                e.matmul(bank(sb), KT[0:64, c0:c0 + 128], qtile[0:64, :], start=True, stop=True)
                return e.matmul(bank(sb + 1), KT[64:128, c0:c0 + 128], qtile[64:128, :], start=True, stop=True)
            op("pe", f, reads=[b_kv, b_q[qi]], writes=[PB[sb], PB[sb + 1]])

        s_op(0)
        for u in range(n):
            KT, c0, V, vb, mk = units[u]
            if u + 1 < n:
                s_op(u + 1)
            sb = SB[(ucount[0] + u) % 2]
            pi = (ucount[0] + u) % 3
            op("act", lambda e, sb=sb, pi=pi: e.activation(pt[pi][:], bank(sb, 2), AF.Exp, scale=0.125),
               writes=[PB[sb], PB[sb + 1], b_pt[pi]])
            if mk is not None:
                op("dve", lambda e, pi=pi, mk=mk: e.tensor_tensor(
                    pt[pi][:, :].rearrange("p (h q) -> p h q", q=128), pt[pi][:, :].rearrange("p (h q) -> p h q", q=128),
                    _bc_mid(masks[:, mk * 128:(mk + 1) * 128], 8), ALU.mult), reads=[b_m], writes=[b_pt[pi]])

            def pv(e, V=V, vb=vb, pi=pi, u=u):
                e.matmul(bank(a0), V[:, vb * 192:vb * 192 + 128], pt[pi][:, 0:512], start=(u == 0), stop=(u == n - 1))
                return e.matmul(bank(a1), V[:, vb * 192 + 64:vb * 192 + 192], pt[pi][:, 512:1024], start=(u == 0), stop=(u == n - 1))
            op("pe", pv, reads=[b_kv, b_pt[pi]], writes=[PB[a0], PB[a1]])
        ucount[0] += n
        if br == "a":
            def f(e):
                e.tensor_tensor(rs[64:128, :], bank(a0)[64:128, :], sinkexp[64:128, :], ALU.add)
                return e.tensor_tensor(rs[0:64, :], bank(a1)[0:64, :], sinkexp[0:64, :], ALU.add)
        else:
            def f(e):
                e.tensor_copy(rs[64:128, :], bank(a0)[64:128, :])
                return e.tensor_copy(rs[0:64, :], bank(a1)[0:64, :])
        op("dve", f, reads=[b_m], writes=[PB[a0], PB[a1], b_rs])
        op("dve", lambda e: e.reciprocal(rs[:], rs[:]), writes=[b_rs])

        def f2(e):
            e.activation(rsh[0:64, :], rs[64:128, :], AF.Copy)
            return e.activation(rsh[64:128, :], rs[0:64, :], AF.Copy)
        op("act", f2, reads=[b_rs], writes=[b_rsh])
        o = oi[0] % 4
        oi[0] += 1

        def f3(e):
            e.tensor_tensor(oT[o][0:64, :], bank(a0)[0:64, :], rsh[0:64, :], ALU.mult)
            return e.tensor_tensor(oT[o][64:128, :], bank(a1)[64:128, :], rsh[64:128, :], ALU.mult)
        op("dve", f3, reads=[b_rsh], writes=[PB[a0], PB[a1], b_oT[o]])
        return o

    for t in range(NTQ):
        qi = t % 2
        qc = t * 128 if t < NT else NOWN + 384
        dma("sp", lambda e, qi=qi, qc=qc: e.dma_start(out=qa_t[qi][:, :].rearrange("p (g q) -> p g q", g=4),
                                                       in_=qa_d[:, :, qc:qc + 128].rearrange("g p q -> p g q")), writes=[b_q[qi]])
        dma("sp", lambda e, qi=qi, qc=qc: e.dma_start(out=qb_t[qi][:, :].rearrange("p (g q) -> p g q", g=4),
                                                       in_=qb_d[:, :, qc:qc + 128].rearrange("g p q -> p g q")), writes=[b_q[qi]])
        ctxu = [(KaT, (ablk0 + 4) * 128, Va, ablk0 + 4, None), (KaT, (ablk0 + 5) * 128, Va, ablk0 + 5, None)]

        def ka(blk, mk):
            return (KaT, blk * 128, Va, blk, mk)
        if t < NT:
            left = ka(t - 1, 0) if t > 0 else ka(ablk0 + 0, 1)
            right = ka(t + 1, 2) if t < NT - 1 else ka(ablk0 + 1, 3)
            ua = [left, ka(t, None), right] + ctxu
        else:
            ua = [ka(ablk0 + 2, 4), ka(ablk0 + 0, 5), ka(0, 6), ka(NT - 1, 7), ka(ablk0 + 1, 8)] + ctxu
        o = attention(qi, "a", qa_t[qi], ua)
        dma("sp", lambda e, o=o, t=t: e.dma_start(out=oa_d[t], in_=oT[o][:]), reads=[b_oT[o]])
        ub = [(KbT, kb * 128, Vb, kb, None) for kb in range(S // 128)]
        ub += [(KbT, X2 + kb * 128, Vb, X2 // 128 + kb, None) for kb in range(2)]
        o = attention(qi, "b", qb_t[qi], ub)
        dma("sp", lambda e, o=o, t=t: e.dma_start(out=ob_d[t], in_=oT[o][:]), reads=[b_oT[o]])
    S_.barrier()

    def layer_norm(z, b_z, scr, b_scr, st, b_st, g_ap, b_ap, dst, b_dst, b_par):
        op("dve", lambda e: e.memset(st[:, 0:2], 0.0), writes=[b_st])
        op("act", lambda e: e.activation(scr[:], z[:], AF.Identity, accum_out=st[:, 0:1]), reads=[b_z], writes=[b_scr, b_st])
        op("act", lambda e: e.activation(scr[:], z[:], AF.Square, accum_out=st[:, 1:2]), reads=[b_z], writes=[b_scr, b_st])

        op("dve", lambda e: e.tensor_scalar(st[:, 2:3], st[:, 0:1], 1.0 / D, None, ALU.mult), writes=[b_st])
        op("dve", lambda e: e.tensor_tensor(st[:, 3:4], st[:, 2:3], st[:, 2:3], ALU.mult), writes=[b_st])
        op("dve", lambda e: e.scalar_tensor_tensor(st[:, 4:5], st[:, 1:2], 1.0 / D, st[:, 3:4], ALU.mult, ALU.subtract), writes=[b_st])
        op("act", lambda e: e.activation(st[:, 5:6], st[:, 4:5], AF.Sqrt, bias=epsb[:, 0:1]), reads=[B_const], writes=[b_st])
        op("dve", lambda e: e.reciprocal(st[:, 5:6], st[:, 5:6]), writes=[b_st])
        op("dve", lambda e: e.scalar_tensor_tensor(st[:, 6:7], st[:, 2:3], -1.0, st[:, 5:6], ALU.mult, ALU.mult), writes=[b_st])
        op("act", lambda e: e.activation(scr[:], z[:], AF.Identity, bias=st[:, 6:7], scale=st[:, 5:6]),
           reads=[b_z, b_st], writes=[b_scr])
        op("dve", lambda e: e.tensor_tensor(scr[:], scr[:], g_ap, ALU.mult), reads=[b_par], writes=[b_scr])
        op("pool", lambda e: e.tensor_tensor(dst[:], scr[:], b_ap, ALU.add), reads=[b_par, b_scr], writes=[b_dst])

    A = Arena(PERSIST)
    Wa = A.alloc("Wa", 4 * D, BF16)
    Wb = A.alloc("Wb", 4 * D, BF16)
    Wo = A.alloc("Wo", KC * D, BF16)
    lnp = A.alloc("lnp_sb", 2 * D, F32)
    gbc = A.alloc("gbc", D, F32)
    wstg = [A.alloc("wstg%d" % i, D, F32) for i in range(2)]
    oa_t = [A.alloc("oa_t%d" % i, 512, BF16) for i in range(2)]
    ob_t = [A.alloc("ob_t%d" % i, 512, BF16) for i in range(2)]
    g_t = [A.alloc("g_t%d" % i, 16 * 128, BF16) for i in range(2)]
    x_t = [A.alloc("x_t%d" % i, D, F32) for i in range(2)]
    mt1 = A.alloc("mt1", 512, F32)
    mt2 = A.alloc("mt2", 512, F32)
    mT = A.alloc("mT", KC * 128, BF16)
    z_t = A.alloc("z_t", D, F32)
    scr = A.alloc("scr", D, F32)
    st = A.alloc("st", 8, F32)
    x1_t = [A.alloc("x1_t%d" % i, D, F32) for i in range(2)]
    h2s = [A.alloc("h2s%d" % i, KC * 128, BF16) for i in range(2)]
    b_w2, b_gbc = Buf("w2"), Buf("gbc")
    b_wstg = [Buf("wstg") for _ in wstg]
    b_in2 = [Buf("in2") for _ in range(2)]
    b_xt = [Buf("xt") for _ in range(2)]
    b_mt1, b_mt2, b_mT, b_z, b_scr, b_st = [Buf(n) for n in "mt1 mt2 mT z scr st".split()]
    b_x1 = [Buf("x1") for _ in range(2)]
    b_h2s = [Buf("h2s") for _ in range(2)]
    for n_, (Wt, nm) in enumerate(((Wa, "wba"), (Wb, "wbb"))):
        for g in range(4):
            w = (n_ * 4 + g) % 2
            dma("sp", lambda e, w=w, nm=nm, g=g: e.dma_start(out=wstg[w][:], in_=din[nm][:, g * D:(g + 1) * D]), writes=[b_wstg[w]])
            op("dve", lambda e, w=w, Wt=Wt, g=g: e.tensor_copy(Wt[:, g * D:(g + 1) * D], wstg[w][:]), reads=[b_wstg[w]], writes=[b_w2])
    dma("sp", lambda e: e.dma_start(out=lnp[:], in_=din["lnp"][:, 0:2 * D]), writes=[b_w2])
    dma("sp", lambda e: e.dma_start(out=gbc[:], in_=gate_d[:, 0:D]), writes=[b_gbc])
    wout_v = din["w_out"].rearrange("(kc p) n -> p kc n", p=128)
    for kc in range(KC):
        w = kc % 2
        dma("sp", lambda e, w=w, kc=kc: e.dma_start(out=wstg[w][:], in_=wout_v[:, kc, :]), writes=[b_wstg[w]])
        op("dve", lambda e, w=w, kc=kc: e.tensor_tensor(Wo[:, kc * D:(kc + 1) * D], wstg[w][:], gbc[:], ALU.mult),
           reads=[b_wstg[w], b_gbc], writes=[b_w2])
    for t in range(NTQ):
        i2 = t % 2
        qc = t * 128 if t < NT else NOWN + 384
        dma("sp", lambda e, i2=i2, t=t: e.dma_start(out=oa_t[i2][:], in_=oa_d[t]), writes=[b_in2[i2]])
        dma("sp", lambda e, i2=i2, t=t: e.dma_start(out=ob_t[i2][:], in_=ob_d[t]), writes=[b_in2[i2]])
        dma("sp", lambda e, i2=i2, qc=qc: e.dma_start(out=g_t[i2][:, :].rearrange("p (j q) -> p j q", j=16),
                                                       in_=g_d[:, :, qc:qc + 128].rearrange("j p q -> p j q")), writes=[b_in2[i2]])
        dma("sp", lambda e, i2=i2, t=t: e.dma_start(out=x_t[i2][:], in_=din["xtok"][t * 128:(t + 1) * 128, :]), writes=[b_xt[i2]])
        for half in range(2):
            def mmA(e, half=half, i2=i2):
                for oc in range(4):
                    for g in range(4):
                        r = e.matmul(bank(0)[:, oc * 128:(oc + 1) * 128], Wa[:, g * D + (half * 4 + oc) * 128:g * D + (half * 4 + oc + 1) * 128],
                                     oa_t[i2][:, g * 128:(g + 1) * 128], start=(g == 0), stop=(g == 3), skip_group_check=True)
                return r

            def mmB(e, half=half, i2=i2):
                for oc in range(4):
                    for g in range(4):
                        r = e.matmul(bank(1)[:, oc * 128:(oc + 1) * 128], Wb[:, g * D + (half * 4 + oc) * 128:g * D + (half * 4 + oc + 1) * 128],
                                     ob_t[i2][:, g * 128:(g + 1) * 128], start=(g == 0), stop=(g == 3), skip_group_check=True)
                return r
            op("pe", mmA, reads=[b_w2, b_in2[i2]], writes=[PB[0]])
            op("pe", mmB, reads=[b_w2, b_in2[i2]], writes=[PB[1]])
            op("dve", lambda e, half=half, i2=i2: e.tensor_tensor(mt1[:], bank(0), g_t[i2][:, half * 512:(half + 1) * 512], ALU.mult),
               reads=[b_in2[i2]], writes=[PB[0], b_mt1])
            op("dve", lambda e, half=half, i2=i2: e.tensor_tensor(mt2[:], bank(1), g_t[i2][:, 1024 + half * 512:1024 + (half + 1) * 512], ALU.mult),
               reads=[b_in2[i2]], writes=[PB[1], b_mt2])
            op("pool", lambda e, half=half: e.tensor_tensor(mT[:, half * 512:(half + 1) * 512], mt1[:], mt2[:], ALU.add),
               reads=[b_mt1, b_mt2], writes=[b_mT])

        def mmY(e):
            for hh in range(2):
                for kc in range(KC):
                    r = e.matmul(bank(2 + hh), mT[:, kc * 128:(kc + 1) * 128], Wo[:, kc * D + hh * 512:kc * D + (hh + 1) * 512],
                                 start=(kc == 0), stop=(kc == KC - 1))
            return r
        op("pe", mmY, reads=[b_w2, b_mT], writes=[PB[2], PB[3]])
        op("dve", lambda e, i2=i2: e.scalar_tensor_tensor(z_t[:], x_t[i2][:], ALPHA, bank(2, 2), ALU.mult, ALU.add),
           reads=[b_xt[i2]], writes=[PB[2], PB[3], b_z])
        layer_norm(z_t, b_z, scr, b_scr, st, b_st, lnp[:, 0:D], lnp[:, D:2 * D], x1_t[i2], b_x1[i2], b_w2)
        dma("sp", lambda e, i2=i2, t=t: e.dma_start(out=x1_d[t * 128:(t + 1) * 128, :], in_=x1_t[i2][:]), reads=[b_x1[i2]])

        def tr(e, i2=i2):
            for kc in range(KC):
                r = e.transpose(bank(4, 2)[:, kc * 128:(kc + 1) * 128], x1_t[i2][:, kc * 128:(kc + 1) * 128], ident[:])
            return r
        op("pe", tr, reads=[b_x1[i2], B_const], writes=[PB[4], PB[5]])
        for kc in range(KC):
            eng = "act" if kc % 2 == 0 else "dve"
            if eng == "act":
                op("act", lambda e, kc=kc, i2=i2: e.activation(h2s[i2][:, kc * 128:(kc + 1) * 128], bank(4, 2)[:, kc * 128:(kc + 1) * 128],
                                                               AF.Identity, bias=modfm[:, 24 + kc:25 + kc], scale=mod1p[:, 8 + kc:9 + kc]),
                   reads=[B_const], writes=[PB[4], PB[5], b_h2s[i2]])
            else:
                op("dve", lambda e, kc=kc, i2=i2: e.tensor_scalar(h2s[i2][:, kc * 128:(kc + 1) * 128], bank(4, 2)[:, kc * 128:(kc + 1) * 128],
                                                                  mod1p[:, 8 + kc:9 + kc], modfm[:, 24 + kc:25 + kc], ALU.mult, ALU.add),
                   reads=[B_const], writes=[PB[4], PB[5], b_h2s[i2]])
        h2v = h2s[i2][:, :].rearrange("p (kc q) -> p kc q", kc=KC)
        if t < NT:
            dma("sp", lambda e, h2v=h2v, t=t: e.dma_start(out=h2_d[:, :, 1 + t * 128:1 + (t + 1) * 128].rearrange("kc p q -> p kc q"), in_=h2v),
                reads=[b_h2s[i2]])
        else:
            op("dve", lambda e, h2v=h2v: e.tensor_tensor(h2v[:, :, 0:2], h2v[:, :, 0:2], _bc_mid(hm[:, 0:2], KC), ALU.mult),
               reads=[B_const], writes=[b_h2s[i2]])
            dma("sp", lambda e, h2v=h2v: e.dma_start(out=h2_d[:, :, 0:1].rearrange("kc p q -> p kc q"), in_=h2v[:, :, 0:1], allow_slow_non_contiguous=True), reads=[b_h2s[i2]])
            dma("sp", lambda e, h2v=h2v: e.dma_start(out=h2_d[:, :, NOWN + 1:NOWN + 2].rearrange("kc p q -> p kc q"), in_=h2v[:, :, 1:2], allow_slow_non_contiguous=True),
                reads=[b_h2s[i2]])
    S_.barrier()

    A = Arena(PERSIST)
    Wup = A.alloc("Wup", KC * 2 * DFF, BF16)
    Wdn = A.alloc("Wdn", NFC * D, BF16)
    lnp2 = A.alloc("lnp2", 2 * D, F32)
    cw = A.alloc("cw_sb", 132, F32)
    cb = A.alloc("cb_sb", 44, F32)
    gbc2 = A.alloc("gbc2", D, F32)
    wstg2 = [A.alloc("wstg2_%d" % i, D, F32) for i in range(2)]
    h2t = [A.alloc("h2t%d" % i, KC * 258, BF16) for i in range(2)]
    x1r = [A.alloc("x1r%d" % i, D, F32) for i in range(4)]
    cg = [A.alloc("cg%d" % i, 256, F32) for i in range(2)]
    cv_ = [A.alloc("cv%d" % i, 256, F32) for i in range(2)]
    sg = [A.alloc("sg%d" % i, 256, F32) for i in range(2)]
    aT = [A.alloc("aT%d" % i, 256, BF16) for i in range(3)]
    z2 = A.alloc("z2", D, F32)
    scr2 = A.alloc("scr2", D, F32)
    st2 = A.alloc("st2", 8, F32)
    o_t = [A.alloc("o_t%d" % i, D, F32) for i in range(2)]
    b_w3, b_gbc2 = Buf("w3"), Buf("gbc2")
    b_wstg2 = [Buf("wstg2") for _ in range(2)]
    b_h2t = [Buf("h2t") for _ in range(2)]
    b_x1r = [Buf("x1r") for _ in range(4)]
    b_cg = [Buf("cg") for _ in range(2)]
    b_cv = [Buf("cv") for _ in range(2)]
    b_sg = [Buf("sg") for _ in range(2)]
    b_aT = [Buf("aT") for _ in range(3)]
    b_z2, b_scr2, b_st2 = Buf("z2"), Buf("scr2"), Buf("st2")
    b_ot = [Buf("ot") for _ in range(2)]
    wup_v = din["w_up"].rearrange("(kc p) n -> p kc n", p=128)
    n_ = 0
    for kc in range(KC):
        for c0 in range(0, 2 * DFF, 1024):
            c1 = min(c0 + 1024, 2 * DFF)
            w = n_ % 2
            n_ += 1
            dma("sp", lambda e, w=w, kc=kc, c0=c0, c1=c1: e.dma_start(out=wstg2[w][:, 0:c1 - c0], in_=wup_v[:, kc, c0:c1]), writes=[b_wstg2[w]])
            eng = "dve" if n_ % 2 == 0 else "act"
            if eng == "dve":
                op("dve", lambda e, w=w, kc=kc, c0=c0, c1=c1: e.tensor_copy(Wup[:, kc * 2 * DFF + c0:kc * 2 * DFF + c1], wstg2[w][:, 0:c1 - c0]),
                   reads=[b_wstg2[w]], writes=[b_w3])
            else:
                op("act", lambda e, w=w, kc=kc, c0=c0, c1=c1: e.activation(Wup[:, kc * 2 * DFF + c0:kc * 2 * DFF + c1], wstg2[w][:, 0:c1 - c0], AF.Copy),
                   reads=[b_wstg2[w]], writes=[b_w3])
    dma("sp", lambda e: e.dma_start(out=lnp2[:], in_=din["lnp"][:, 2 * D:4 * D]), writes=[b_w3])
    dma("sp", lambda e: e.dma_start(out=cw[:], in_=din["cw"]), writes=[b_w3])
    dma("sp", lambda e: e.dma_start(out=cb[:], in_=din["cb"]), writes=[b_w3])
    dma("sp", lambda e: e.dma_start(out=gbc2[:], in_=gate_d[:, D:2 * D]), writes=[b_gbc2])
    wdn_v = din["w_down"].rearrange("(fc p) n -> p fc n", p=128)
    for fc in range(NFC):
        w = fc % 2
        dma("sp", lambda e, w=w, fc=fc: e.dma_start(out=wstg2[w][:], in_=wdn_v[:, fc, :]), writes=[b_wstg2[w]])
        op("dve", lambda e, w=w, fc=fc: e.tensor_tensor(Wdn[:, fc * D:(fc + 1) * D], wstg2[w][:], gbc2[:], ALU.mult),
           reads=[b_wstg2[w], b_gbc2], writes=[b_w3])
    NT3 = NOWN // 256
    step = [0]
    for i in range(NT3):
        hi = i % 2
        dma("sp", lambda e, hi=hi, i=i: e.dma_start(out=h2t[hi][:, :].rearrange("p (kc q) -> p kc q", kc=KC),
                                                     in_=h2_d[:, :, i * 256:i * 256 + 258].rearrange("kc p q -> p kc q")), writes=[b_h2t[hi]])
        for tt_ in range(2):
            xi = (i * 2 + tt_) % 4
            dma("sp", lambda e, xi=xi, i=i, tt_=tt_: e.dma_start(out=x1r[xi][:], in_=x1_d[i * 256 + tt_ * 128:i * 256 + (tt_ + 1) * 128, :]),
                writes=[b_x1r[xi]])

        def up(fc, hi=hi):
            k = step[0] + fc
            pg, pv_ = (k % 2) * 2, (k % 2) * 2 + 1

            def f(e):
                for (pb, col) in ((pg, fc * 128), (pv_, DFF + fc * 128)):
                    for kc in range(KC):
                        r = e.matmul(bank(pb)[:, 0:258], Wup[:, kc * 2 * DFF + col:kc * 2 * DFF + col + 128],
                                     h2t[hi][:, kc * 258:(kc + 1) * 258], start=(kc == 0), stop=(kc == KC - 1))
                return r
            op("pe", f, reads=[b_w3, b_h2t[hi]], writes=[PB[pg], PB[pv_]])

        up(0)
        for fc in range(NFC):
            k = step[0] + fc
            pg, pv_ = (k % 2) * 2, (k % 2) * 2 + 1
            ci = k % 2
            if fc + 1 < NFC:
                up(fc + 1)
            for (pb, dst, bd, fi) in ((pg, cg[ci], b_cg[ci], fc), (pv_, cv_[ci], b_cv[ci], NFC + fc)):
                op("act", lambda e, pb=pb, dst=dst, fi=fi: e.activation(dst[:], bank(pb)[:, 1:257], AF.Identity,
                                                                        bias=cb[:, fi:fi + 1], scale=cw[:, fi * 3 + 1:fi * 3 + 2]),
                   reads=[b_w3], writes=[PB[pb], bd])
                op("dve", lambda e, pb=pb, dst=dst, fi=fi: e.scalar_tensor_tensor(dst[:], bank(pb)[:, 0:256], cw[:, fi * 3:fi * 3 + 1], dst[:],
                                                                                  ALU.mult, ALU.add), reads=[b_w3], writes=[PB[pb], bd])
                op("dve", lambda e, pb=pb, dst=dst, fi=fi: e.scalar_tensor_tensor(dst[:], bank(pb)[:, 2:258], cw[:, fi * 3 + 2:fi * 3 + 3], dst[:],
                                                                                  ALU.mult, ALU.add), reads=[b_w3], writes=[PB[pb], bd])
            op("act", lambda e, ci=ci: e.activation(sg[ci][:], cg[ci][:], AF.Silu), reads=[b_cg[ci]], writes=[b_sg[ci]])
            ai = k % 3
            op("pool", lambda e, ci=ci, ai=ai: e.tensor_tensor(aT[ai][:], sg[ci][:], cv_[ci][:], ALU.mult),
               reads=[b_sg[ci], b_cv[ci]], writes=[b_aT[ai]])

            def dn(e, fc=fc, ai=ai):
                for tt_ in range(2):
                    for hh in range(2):
                        r = e.matmul(bank(4 + tt_ * 2 + hh), aT[ai][:, tt_ * 128:(tt_ + 1) * 128], Wdn[:, fc * D + hh * 512:fc * D + (hh + 1) * 512],
                                     start=(fc == 0), stop=(fc == NFC - 1))
                return r
            op("pe", dn, reads=[b_w3, b_aT[ai]], writes=[PB[4], PB[5], PB[6], PB[7]])
        step[0] += NFC
        for tt_ in range(2):
            xi = (i * 2 + tt_) % 4
            oi_ = (i * 2 + tt_) % 2
            op("dve", lambda e, xi=xi, tt_=tt_: e.scalar_tensor_tensor(z2[:], x1r[xi][:], ALPHA, bank(4 + tt_ * 2, 2), ALU.mult, ALU.add),
               reads=[b_x1r[xi]], writes=[PB[4 + tt_ * 2], PB[5 + tt_ * 2], b_z2])
            layer_norm(z2, b_z2, scr2, b_scr2, st2, b_st2, lnp2[:, 0:D], lnp2[:, D:2 * D], o_t[oi_], b_ot[oi_], b_w3)
            r0 = i * 256 + tt_ * 128
            dma("sp", lambda e, oi_=oi_, r0=r0: e.dma_start(out=out_d[r0:r0 + 128, :], in_=o_t[oi_][:]), reads=[b_ot[oi_]])
    S_.emit()
    return nc


_CACHE = {}


def run(inputs, S):
    sh = _shared_inputs(inputs)
    in_maps = []
    for core in range(8):
        d = _core_inputs(inputs, sh, core, S)
        in_maps.append({k: np.ascontiguousarray(d[k], dtype=np.float32) for k in IN_SHAPES(S)})
    if S not in _CACHE:
        _CACHE[S] = build_program(S)
    res = run_bass_kernel_spmd(_CACHE[S], in_maps, core_ids=list(range(8)))
    global LAST
    LAST = (res, in_maps)
    NOWN = S // 4
    out = np.zeros((2, S, D), np.float32)
    for core in range(8):
        b, r = core // 4, core % 4
        out[b, r * NOWN:(r + 1) * NOWN] = res.results[core]["out"]
    return out


def kernel(**inputs):
    return run(inputs, 16384)
```

```python
import contextlib
import numpy as np
import concourse.bass as bass
import concourse.mybir as mybir
from concourse.bass_utils import run_bass_kernel_spmd

F32 = mybir.dt.float32
BF16 = mybir.dt.bfloat16
AF = mybir.ActivationFunctionType
ALU = mybir.AluOpType
AX = mybir.AxisListType

D = 1024
KC = 8
GRID_W = 64
CTX = 256
HD = 64
DFF = 2816
NFC = 22
LN_EPS = 1e-5
QK_EPS = 1e-6
ALPHA = 2.0 ** 0.25
OFF_QA, OFF_KA, OFF_VA, OFF_QB, OFF_KB, OFF_VB, OFF_GA, OFF_GB = 0, 512, 640, 768, 1280, 1408, 1536, 2560
F_QA, F_QAS, F_QB, F_QBS, F_KA, F_KAS, F_KB, F_KBS, F_G = 0, 4, 8, 12, 16, 17, 18, 19, 20
NF = 36


class Buf:
    __slots__ = ("name", "w", "r")

    def __init__(self, name=""):
        self.name = name
        self.w = None
        self.r = []


class _Op:
    __slots__ = ("eng", "fn", "deps", "dma", "sig", "cnt", "sem")

    def __init__(self, eng, fn, deps, dma):
        self.eng, self.fn, self.deps, self.dma = eng, fn, deps, dma
        self.sig, self.cnt, self.sem = False, 0, None


COMPUTE = ("pe", "act", "dve", "pool")


class Sched:
    def __init__(self, nc, n_dma_sems=24, same_engine_sync=True):
        self.nc = nc
        self.ops = []
        self.n_dma_sems = n_dma_sems
        self.same_engine_sync = same_engine_sync
        self.last_on = {}
        self.pending = {}
        self.dma_ids = []

    def op(self, eng, fn, reads=(), writes=(), dma=False):
        deps = set()
        for b in reads:
            if b.w is not None:
                deps.add(b.w)
        for b in writes:
            if b.w is not None:
                deps.add(b.w)
            deps.update(b.r)
        i = len(self.ops)
        for b in reads:
            b.r.append(i)
        for b in writes:
            b.w = i
            b.r = []
        if eng in self.pending:
            deps.update(self.pending.pop(eng))
        self.ops.append(_Op(eng, fn, deps, dma))
        self.last_on[eng] = i
        if dma:
            self.dma_ids.append(i)
        return i

    def dma(self, queue, fn, reads=(), writes=()):
        return self.op(queue, fn, reads, writes, dma=True)

    def barrier(self):
        lasts = set(v for e, v in self.last_on.items() if e in COMPUTE)
        lasts.update(self.dma_ids[-self.n_dma_sems:])
        for e in ("pe", "act", "dve", "pool", "sp"):
            self.pending.setdefault(e, set()).update(lasts)

    def emit(self, final_wait_engine="sp"):
        nc, ops = self.nc, self.ops
        for o in ops:
            for d in o.deps:
                p = ops[d]
                if p.dma:
                    continue
                if p.eng == o.eng and not o.dma and (o.eng == "pe" or not self.same_engine_sync):
                    continue
                p.sig = True
        for e, i in self.last_on.items():
            if e in COMPUTE:
                ops[i].sig = True
        with contextlib.ExitStack() as st:
            esem = {e: st.enter_context(nc.semaphore("s_" + e)) for e in COMPUTE}
            dsems = [st.enter_context(nc.semaphore("d_%d" % k)) for k in range(self.n_dma_sems)]
            ecnt = {e: 0 for e in COMPUTE}
            dcnt = [0] * self.n_dma_sems
            rr = 0
            for o in ops:
                if o.dma:
                    o.sem = rr
                    dcnt[rr] += 16
                    o.cnt = dcnt[rr]
                    rr = (rr + 1) % self.n_dma_sems
                elif o.sig:
                    ecnt[o.eng] += 1
                    o.cnt = ecnt[o.eng]
            per_eng = {}
            for i, o in enumerate(ops):
                per_eng.setdefault(o.eng, []).append(i)
            block = st.enter_context(nc.Block())

            def run(engname, eng):
                seen = {}

                def wait(key, semh, val):
                    if seen.get(key, 0) >= val:
                        return
                    seen[key] = val
                    eng.wait_ge(semh, val)

                for i in per_eng.get(engname, []):
                    o = ops[i]
                    need = {}
                    for d in o.deps:
                        p = ops[d]
                        if p.dma:
                            key, semh = ("d", p.sem), dsems[p.sem]
                        else:
                            if p.eng == engname and not o.dma and (engname == "pe" or not self.same_engine_sync):
                                continue
                            key, semh = ("e", p.eng), esem[p.eng]
                        if key not in need or need[key][1] < p.cnt:
                            need[key] = (semh, p.cnt)
                    for key in sorted(need):
                        wait(key, need[key][0], need[key][1])
                    if o.dma:
                        if o.cnt > 16:
                            wait(("d", o.sem), dsems[o.sem], o.cnt - 16)
                        o.fn(eng).then_inc(dsems[o.sem], 16)
                    else:
                        ins = o.fn(eng)
                        if o.sig:
                            ins.then_inc(esem[o.eng], 1)
                if engname == final_wait_engine:
                    for e in COMPUTE:
                        if ecnt[e]:
                            wait(("e", e), esem[e], ecnt[e])
                    for k in range(self.n_dma_sems):
                        if dcnt[k]:
                            wait(("d", k), dsems[k], dcnt[k])

            @block.tensor
            def _(e):
                run("pe", e)

            @block.scalar
            def _(e):
                run("act", e)

            @block.vector
            def _(e):
                run("dve", e)

            @block.gpsimd
            def _(e):
                run("pool", e)

            @block.sync
            def _(e):
                run("sp", e)


PERM = np.r_[0:16, 32:48, 16:32, 48:64]
PERM_SW = PERM[(np.arange(64) + 32) % 64]


def _chunk_cols():
    cols = np.zeros((NF, 128), np.int64)
    for g in range(4):
        for kvh in range(2):
            cols[F_QA + g, kvh * 64:(kvh + 1) * 64] = OFF_QA + (kvh * 4 + g) * 64 + PERM
            cols[F_QAS + g, kvh * 64:(kvh + 1) * 64] = OFF_QA + (kvh * 4 + g) * 64 + PERM_SW
            cols[F_QB + g, kvh * 64:(kvh + 1) * 64] = OFF_QB + (kvh * 4 + g) * 64 + PERM
            cols[F_QBS + g, kvh * 64:(kvh + 1) * 64] = OFF_QB + (kvh * 4 + g) * 64 + PERM_SW
    for kvh in range(2):
        cols[F_KA, kvh * 64:(kvh + 1) * 64] = OFF_KA + kvh * 64 + PERM
        cols[F_KAS, kvh * 64:(kvh + 1) * 64] = OFF_KA + kvh * 64 + PERM_SW
        cols[F_KB, kvh * 64:(kvh + 1) * 64] = OFF_KB + kvh * 64 + PERM
        cols[F_KBS, kvh * 64:(kvh + 1) * 64] = OFF_KB + kvh * 64 + PERM_SW
    for j in range(16):
        cols[F_G + j] = OFF_GA + j * 128 + np.arange(128)
    return cols


def _rope_tables(pos, valid):
    n = pos.shape[0]
    rows = (pos // GRID_W).astype(np.float32)
    colsp = (pos % GRID_W).astype(np.float32)
    nfreq = HD // 4
    inv = (np.float32(10000.0) ** (-np.arange(nfreq, dtype=np.float32) / np.float32(nfreq))).astype(np.float32)
    ang_r = rows[None, :] * inv[:, None]
    ang_c = colsp[None, :] * inv[:, None]
    cos64 = np.zeros((64, n), np.float32)
    sin64 = np.zeros((64, n), np.float32)
    for half in range(2):
        sgn = -1.0 if half == 0 else 1.0
        cos64[half * 32:half * 32 + 16] = np.cos(ang_r)
        cos64[half * 32 + 16:half * 32 + 32] = np.cos(ang_c)
        sin64[half * 32:half * 32 + 16] = sgn * np.sin(ang_r)
        sin64[half * 32 + 16:half * 32 + 32] = sgn * np.sin(ang_c)
    cos64[:, ~valid] = 1.0
    sin64[:, ~valid] = 0.0
    return np.concatenate([cos64, cos64], 0), np.concatenate([sin64, sin64], 0)


def _shared_inputs(inp):
    f = np.float32
    w_in = np.asarray(inp["w_in"][0], f)
    b_in = np.asarray(inp["b_in"][0], f)
    cols = _chunk_cols()
    sh = {}
    sh["w_in_fm"] = np.ascontiguousarray(w_in[:, cols.reshape(-1)])
    sh["w_in_v"] = np.ascontiguousarray(np.concatenate([w_in[:, OFF_VA:OFF_VA + 128], w_in[:, OFF_VB:OFF_VB + 128]], 1))
    sh["bias_fm"] = np.ascontiguousarray(b_in[cols].T)
    gq = np.asarray(inp["q_norm_g"][0], f)
    gk = np.asarray(inp["k_norm_g"][0], f)
    cv = np.stack([np.tile(gq[PERM], 2), np.tile(gq[PERM_SW], 2), np.tile(gk[PERM], 2), np.tile(gk[PERM_SW], 2)], 1)
    sh["cvec"] = np.ascontiguousarray(cv.astype(f))
    vb = np.concatenate([b_in[OFF_VA:OFF_VA + 128], b_in[OFF_VB:OFF_VB + 128]])
    sh["vbias"] = np.ascontiguousarray(np.tile(vb[None, :], (128, 1)))
    p = np.arange(128)
    sh["blockones"] = (p[:, None] // 64 == p[None, :] // 64).astype(f)
    sh["ident"] = np.eye(128, dtype=f)
    sink = np.asarray(inp["attn_sink"][0], f)
    sr = np.zeros((128, 4, 128), f)
    for g in range(4):
        sr[0:64, g, :] = sink[4 + g]
        sr[64:128, g, :] = sink[g]
    sh["sinkraw"] = sr.reshape(128, 512)
    for nm, key in (("wba", "w_branch_a"), ("wbb", "w_branch_b")):
        w = np.asarray(inp[key][0], f).reshape(2, 4, 64, D)
        sh[nm] = np.ascontiguousarray(w.transpose(0, 2, 1, 3).reshape(128, 4 * D))
    sh["w_out"] = np.asarray(inp["w_out"][0], f)
    sh["w_mod"] = np.asarray(inp["w_mod"][0], f)
    sh["b_mod"] = np.asarray(inp["b_mod"], f).reshape(1, 6 * D)
    sh["lnp"] = np.ascontiguousarray(np.tile(np.stack([inp["ln1_g"][0], inp["ln1_b"][0], inp["ln2_g"][0],
                                                        inp["ln2_b"][0]], 0).astype(f).reshape(1, 4 * D), (128, 1)))
    sh["w_up"] = np.asarray(inp["w_up"][0], f)
    sh["w_down"] = np.asarray(inp["w_down"][0], f)
    cw = np.asarray(inp["conv_w"][0], f)
    sh["cw"] = np.ascontiguousarray(cw.reshape(3, 44, 128).transpose(2, 1, 0).reshape(128, 132))
    sh["cb"] = np.ascontiguousarray(np.asarray(inp["conv_b"][0], f).reshape(44, 128).T)
    return sh


def _core_inputs(inp, sh, core, S):
    f = np.float32
    NOWN = S // 4
    b, r = core // 4, core % 4
    start, end = r * NOWN, (r + 1) * NOWN
    x = np.asarray(inp["x"][b], f)
    NSLOT = S + 1024
    pos = np.full(NSLOT, -1, np.int64)
    pos[0:S] = (start + np.arange(S)) % S
    pos[S:S + 128] = start - 128 + np.arange(128)
    pos[S + 128:S + 256] = end + np.arange(128)
    pos[S + 256] = start - 129
    pos[S + 257] = end + 128
    pos[S + 384] = start - 1
    pos[S + 385] = end
    valid = (pos >= 0) & (pos < S)
    xs = np.zeros((NSLOT, D), f)
    xs[valid] = x[pos[valid]]
    xs[S + 512:S + 768] = np.asarray(inp["ctx"][b], f)
    d = dict(sh)
    d["xT"] = np.ascontiguousarray(xs.T.reshape(KC, 128, NSLOT))
    d["xtok"] = np.ascontiguousarray(np.concatenate([x[start:end], xs[S + 384:S + 512]], 0))
    cs, sn = _rope_tables(np.where(valid, pos, 0), valid)
    d["cs"], d["sn"] = np.ascontiguousarray(cs), np.ascontiguousarray(sn)
    cf = np.stack([np.asarray(inp["c"][b], f), np.asarray(inp["c_ctx"], f)], 1)
    d["c_fm"] = np.ascontiguousarray(cf.reshape(KC, 128, 2).transpose(1, 0, 2).reshape(128, 16))
    j = np.arange(128)[:, None]
    q = np.arange(128)[None, :]
    lv, rv = float(start > 0), float(end < S)
    m = np.zeros((9, 128, 128), f)
    m[0] = (j >= q)
    m[1] = (j >= q) * lv
    m[2] = (j <= q)
    m[3] = (j <= q) * rv
    m[4][0, 0] = lv
    m[4][1, 1] = rv
    m[5][:, 0] = lv
    m[6][:, 0] = 1.0
    m[7][:, 1] = 1.0
    m[8][:, 1] = rv
    d["masks"] = np.ascontiguousarray(m.transpose(1, 0, 2).reshape(128, 9 * 128))
    d["hm"] = np.ascontiguousarray(np.tile(np.array([[lv, rv]], f), (128, 1)))
    return d


IN_SHAPES = lambda S: {
    "xT": [KC, 128, S + 1024], "xtok": [S // 4 + 128, D], "cs": [128, S + 1024], "sn": [128, S + 1024],
    "c_fm": [128, 16], "w_mod": [D, 6 * D], "b_mod": [1, 6 * D], "w_in_fm": [D, NF * 128], "w_in_v": [D, 256],
    "bias_fm": [128, NF], "cvec": [128, 4], "vbias": [128, 256], "blockones": [128, 128], "ident": [128, 128],
    "masks": [128, 9 * 128], "sinkraw": [128, 512], "wba": [128, 4 * D], "wbb": [128, 4 * D], "w_out": [D, D],
    "lnp": [128, 4 * D], "w_up": [D, 2 * DFF], "w_down": [DFF, D], "cw": [128, 132], "cb": [128, 44], "hm": [128, 2],
}


def _bc_mid(ap, n):
    return bass.AP(ap.tensor, ap.offset, [list(ap.ap[0]), [0, n], list(ap.ap[1])])


DEBUG = False


def build_program(S):
    NOWN = S // 4
    NT = NOWN // 128
    NTQ = NT + 1
    NSLOT = S + 1024
    NCH = NSLOT // 512
    NCH_OWN = NOWN // 512
    NQ = NOWN + 512
    NKB = S // 128 + 2
    NA = NOWN + 1024
    NAB = NA // 128
    NBS = NSLOT // 128
    X1 = S
    X2 = S + 512
    nc = bass.Bass("TRN2", target_bir_lowering=False)
    shp = IN_SHAPES(S)
    din = {k: nc.dram_tensor(k, v, F32, kind="ExternalInput").ap() for k, v in shp.items()}
    out_d = nc.dram_tensor("out", [NOWN, D], F32, kind="ExternalOutput").ap()
    qa_d = nc.dram_tensor("qa_d", [4, 128, NQ], BF16, kind=("ExternalOutput" if DEBUG else "Internal")).ap()
    qb_d = nc.dram_tensor("qb_d", [4, 128, NQ], BF16, kind=("ExternalOutput" if DEBUG else "Internal")).ap()
    g_d = nc.dram_tensor("g_d", [16, 128, NQ], BF16, kind=("ExternalOutput" if DEBUG else "Internal")).ap()
    oa_d = nc.dram_tensor("oa_d", [NTQ, 128, 512], BF16, kind=("ExternalOutput" if DEBUG else "Internal")).ap()
    ob_d = nc.dram_tensor("ob_d", [NTQ, 128, 512], BF16, kind=("ExternalOutput" if DEBUG else "Internal")).ap()
    x1_d = nc.dram_tensor("x1_d", [NOWN + 128, D], F32, kind=("ExternalOutput" if DEBUG else "Internal")).ap()
    h2_d = nc.dram_tensor("h2_d", [KC, 128, NOWN + 2], BF16, kind=("ExternalOutput" if DEBUG else "Internal")).ap()
    gate_d = nc.dram_tensor("gate_d", [128, 2 * D], F32, kind=("ExternalOutput" if DEBUG else "Internal")).ap()

    S_ = Sched(nc)
    op, dma = S_.op, S_.dma
    PS = nc.alloc_psum_tensor("ps_all", [128, 4096], F32)

    def bank(i, n=1):
        return PS[:, i * 512:(i + n) * 512]

    PB = [Buf("bank%d" % i) for i in range(8)]

    class Arena:
        def __init__(self, base):
            self.off = base

        def alloc(self, name, cols, dt):
            nbytes = cols * (4 if dt == F32 else 2)
            nbytes = (nbytes + 31) // 32 * 32
            t = nc.alloc_sbuf_tensor_at(name, [128, cols], dt, offset=self.off)
            self.off += nbytes
            assert self.off <= 229376, (name, self.off)
            return t

    A0 = Arena(16384 + 256)
    modfm = A0.alloc("modfm", 64, F32)
    mod1p = A0.alloc("mod1p", 32, F32)
    bias_fm = A0.alloc("bias_fm_sb", NF, F32)
    cvec = A0.alloc("cvec_sb", 4, F32)
    ident = A0.alloc("ident_sb", 128, F32)
    hm = A0.alloc("hm_sb", 2, F32)
    epsb = A0.alloc("epsb", 2, F32)
    B_const = Buf("const")
    PERSIST = A0.off

    A = Arena(PERSIST)
    c_sb = A.alloc("c_sb", 16, F32)
    sc = A.alloc("sc", 16, F32)
    ones_t = A.alloc("ones_t", 128, F32)
    screp = A.alloc("screp", 2 * KC * 128, F32)
    bmod = A.alloc("bmod_sb", 6 * D, F32)
    modbc = A.alloc("modbc", 6 * D, F32)
    modcc = A.alloc("modcc", 2 * D, F32)
    tmpd = A.alloc("tmpd", 6 * D, F32)
    wst = [A.alloc("wst%d" % i, KC * 512, F32) for i in range(2)]
    b_c, b_sc, b_ones, b_screp, b_bmod, b_modbc, b_modcc, b_tmpd = [Buf(n) for n in "c sc ones screp bmod modbc modcc tmpd".split()]
    b_wst = [Buf("wst0"), Buf("wst1")]

    dma("sp", lambda e: e.dma_start(out=c_sb[:], in_=din["c_fm"]), writes=[b_c])
    dma("sp", lambda e: e.dma_start(out=bias_fm[:], in_=din["bias_fm"]), writes=[B_const])
    dma("sp", lambda e: e.dma_start(out=cvec[:], in_=din["cvec"]), writes=[B_const])
    dma("sp", lambda e: e.dma_start(out=ident[:], in_=din["ident"]), writes=[B_const])
    dma("sp", lambda e: e.dma_start(out=hm[:], in_=din["hm"]), writes=[B_const])
    dma("sp", lambda e: e.dma_start(out=bmod[0:1, :], in_=din["b_mod"]), writes=[b_bmod])
    op("dve", lambda e: e.memset(ones_t[:], 1.0), writes=[b_ones])

    def _eps(e):
        e.memset(epsb[:, 0:1], LN_EPS)
        return e.memset(epsb[:, 1:2], QK_EPS)
    op("dve", _eps, writes=[B_const])
    op("act", lambda e: e.activation(sc[:], c_sb[:], AF.Silu), reads=[b_c], writes=[b_sc])
    for j in range(2):
        for kc in range(KC):
            op("dve", lambda e, j=j, kc=kc: e.tensor_scalar(screp[:, (j * KC + kc) * 128:(j * KC + kc + 1) * 128], ones_t[:],
                                                           sc[:, kc * 2 + j:kc * 2 + j + 1], None, ALU.mult),
               reads=[b_ones, b_sc], writes=[b_screp])
    wmod_v = din["w_mod"].rearrange("(kc p) n -> p kc n", p=128)
    jobs = [(0, cj) for cj in range(12)] + [(1, cj) for cj in range(4)]
    for ji, (who, cj) in enumerate(jobs):
        wb = ji % 2
        dma("sp", lambda e, wb=wb, cj=cj: e.dma_start(out=wst[wb][:, :].rearrange("p (kc n) -> p kc n", kc=KC),
                                                       in_=wmod_v[:, :, cj * 512:(cj + 1) * 512]), writes=[b_wst[wb]])
        pb = ji % 2

        def _mm(e, who=who, cj=cj, wb=wb, pb=pb):
            for kc in range(KC):
                e.matmul(bank(pb), screp[:, (who * KC + kc) * 128:(who * KC + kc + 1) * 128],
                         wst[wb][:, kc * 512:(kc + 1) * 512], start=(kc == 0), stop=False)
            return e.matmul(bank(pb), ones_t[0:1, :], bmod[0:1, cj * 512:(cj + 1) * 512], start=False, stop=True)
        op("pe", _mm, reads=[b_screp, b_wst[wb], b_bmod, b_ones], writes=[PB[pb]])
        dst = (modbc if who == 0 else modcc)
        op("act", lambda e, dst=dst, cj=cj, pb=pb: e.activation(dst[:, cj * 512:(cj + 1) * 512], bank(pb), AF.Copy),
           writes=[PB[pb], b_modbc if who == 0 else b_modcc])
    op("dve", lambda e: e.tensor_tensor(tmpd[:, :].rearrange("p (c f) -> p c f", f=128),
                                        modbc[:, :].rearrange("p (c f) -> p c f", f=128), _bc_mid(ident[:, :], 48), ALU.mult),
       reads=[b_modbc, B_const], writes=[b_tmpd])
    op("dve", lambda e: e.tensor_reduce(modfm[:, 0:48], tmpd[:, :].rearrange("p (c f) -> p c f", f=128), AX.X, ALU.add),
       reads=[b_tmpd], writes=[B_const])
    op("dve", lambda e: e.tensor_tensor(tmpd[:, 0:2 * D].rearrange("p (c f) -> p c f", f=128),
                                        modcc[:, :].rearrange("p (c f) -> p c f", f=128), _bc_mid(ident[:, :], 16), ALU.mult),
       reads=[b_modcc, B_const], writes=[b_tmpd])
    op("dve", lambda e: e.tensor_reduce(modfm[:, 48:64], tmpd[:, 0:2 * D].rearrange("p (c f) -> p c f", f=128), AX.X, ALU.add),
       reads=[b_tmpd], writes=[B_const])

    def _m1p(e):
        e.tensor_scalar(mod1p[:, 0:8], modfm[:, 8:16], 1.0, None, ALU.add)
        e.tensor_scalar(mod1p[:, 8:16], modfm[:, 32:40], 1.0, None, ALU.add)
        return e.tensor_scalar(mod1p[:, 16:24], modfm[:, 56:64], 1.0, None, ALU.add)
    op("dve", _m1p, writes=[B_const])
    dma("sp", lambda e: e.dma_start(out=gate_d[:, 0:D], in_=modbc[:, 2 * D:3 * D]), reads=[b_modbc])
    dma("sp", lambda e: e.dma_start(out=gate_d[:, D:2 * D], in_=modbc[:, 5 * D:6 * D]), reads=[b_modbc])
    S_.barrier()

    AK = Arena(PERSIST)
    KbT = AK.alloc("KbT", NSLOT, BF16)
    Vb = AK.alloc("Vb", NBS * 192, BF16)
    KaT = AK.alloc("KaT", NA, BF16)
    Va = AK.alloc("Va", NAB * 192, BF16)
    KV_END = AK.off
    b_kv = Buf("kv")

    def _ones_init(e):
        e.memset(Vb[:, :].rearrange("p (b c) -> p b c", c=192)[:, :, 64:128], 1.0)
        return e.memset(Va[:, :].rearrange("p (b c) -> p b c", c=192)[:, :, 64:128], 1.0)
    op("pool", _ones_init, writes=[b_kv])

    w_fm_v = din["w_in_fm"].rearrange("(kc p) n -> p kc n", p=128)
    w_v_v = din["w_in_v"].rearrange("(kc p) n -> p kc n", p=128)

    def run_pass(pname, chunks, feats, vjob):
        A = Arena(KV_END)
        nfe = sorted(set(sum([[j["f"]] + ([j["fs"]] if "fs" in j else []) for j in feats], [])))
        fidx = {f: i for i, f in enumerate(nfe)}
        W = A.alloc(pname + "W", len(nfe) * KC * 128, BF16)
        Wv = A.alloc(pname + "Wv", KC * 256, BF16)
        vbias = A.alloc(pname + "vb", 256, F32)
        bones = A.alloc(pname + "bo", 128, F32)
        xs = [A.alloc(pname + "xs%d" % i, 512, F32) for i in range(3)]
        hT = [A.alloc(pname + "hT%d" % i, KC * 512, BF16) for i in range(2)]
        cs_t = A.alloc(pname + "cs", 512, F32)
        sn_t = A.alloc(pname + "sn", 512, F32)
        tt = [A.alloc(pname + "t%d" % i, 512, F32) for i in range(6)]
        ob = [A.alloc(pname + "ob%d" % i, 512, BF16) for i in range(3)]
        b_W, b_cs = Buf("W"), Buf("cs")
        b_xs = [Buf("xs") for _ in xs]
        b_hT = [Buf("hT") for _ in hT]
        b_tt = [Buf("t") for _ in tt]
        b_ob = [Buf("ob") for _ in ob]
        wsg = [A.alloc(pname + "wsg%d" % i, 1024, F32) for i in range(2)]
        b_wsg = [Buf("wsg") for _ in wsg]
        for n_, f in enumerate(nfe):
            w_ = n_ % 2
            dma("sp", lambda e, f=f, w_=w_: e.dma_start(out=wsg[w_][:, :].rearrange("p (kc c) -> p kc c", kc=KC),
                                                        in_=w_fm_v[:, :, f * 128:(f + 1) * 128]), writes=[b_wsg[w_]])
            op("dve", lambda e, f=f, w_=w_: e.tensor_copy(W[:, fidx[f] * KC * 128:(fidx[f] + 1) * KC * 128], wsg[w_][:]),
               reads=[b_wsg[w_]], writes=[b_W])
        if vjob:
            for hv in range(2):
                dma("sp", lambda e, hv=hv: e.dma_start(out=wsg[hv][:, :].rearrange("p (kc c) -> p kc c", kc=4),
                                                       in_=w_v_v[:, hv * 4:(hv + 1) * 4, :]), writes=[b_wsg[hv]])
                op("dve", lambda e, hv=hv: e.tensor_copy(Wv[:, hv * 1024:(hv + 1) * 1024], wsg[hv][:]),
                   reads=[b_wsg[hv]], writes=[b_W])
            dma("sp", lambda e: e.dma_start(out=vbias[:], in_=din["vbias"]), writes=[b_W])
        dma("sp", lambda e: e.dma_start(out=bones[:], in_=din["blockones"]), writes=[b_W])
        obi = 0
        pbi = 0
        for ci, c in enumerate(chunks):
            s0 = c * 512
            is_ctx = (s0 == X2)
            hb = ci % 2
            for kc in range(KC):
                xi = (ci * KC + kc) % 3
                dma("sp", lambda e, xi=xi, kc=kc, s0=s0: e.dma_start(out=xs[xi][:], in_=din["xT"][kc, :, s0:s0 + 512]),
                    writes=[b_xs[xi]])
                sc_ap = mod1p[:, 16 + kc:17 + kc] if is_ctx else mod1p[:, kc:kc + 1]
                sh_ap = modfm[:, 48 + kc:49 + kc] if is_ctx else modfm[:, kc:kc + 1]
                eng = "dve"
                op(eng, lambda e, xi=xi, kc=kc, hb=hb, sc_ap=sc_ap, sh_ap=sh_ap: e.tensor_scalar(
                    hT[hb][:, kc * 512:(kc + 1) * 512], xs[xi][:], sc_ap, sh_ap, ALU.mult, ALU.add),
                   reads=[b_xs[xi], B_const], writes=[b_hT[hb]])
            need_rope = any(j["kind"] == "rope" for j in feats)
            if need_rope:
                dma("sp", lambda e, s0=s0: e.dma_start(out=cs_t[:], in_=din["cs"][:, s0:s0 + 512]), writes=[b_cs])
                dma("sp", lambda e, s0=s0: e.dma_start(out=sn_t[:], in_=din["sn"][:, s0:s0 + 512]), writes=[b_cs])
            for j in feats:
                f = j["f"]

                def mm(e, f, pb, hb=hb):
                    for kc in range(KC):
                        r = e.matmul(bank(pb), W[:, (fidx[f] * KC + kc) * 128:(fidx[f] * KC + kc + 1) * 128],
                                     hT[hb][:, kc * 512:(kc + 1) * 512], start=(kc == 0), stop=(kc == KC - 1))
                    return r
                if j["kind"] == "gate":
                    pb = pbi % 4
                    pbi += 1
                    op("pe", lambda e, f=f, pb=pb, mm=mm: mm(e, f, pb), reads=[b_W, b_hT[hb]], writes=[PB[pb]])
                    o = obi % 3
                    obi += 1
                    op("act", lambda e, f=f, pb=pb, o=o: e.activation(ob[o][:], bank(pb), AF.Sigmoid, bias=bias_fm[:, f:f + 1]),
                       reads=[B_const], writes=[PB[pb], b_ob[o]])
                    dst = j["dst"](s0)
                    dma("sp", lambda e, o=o, dst=dst: e.dma_start(out=dst, in_=ob[o][:]), reads=[b_ob[o]])
                    continue
                fs = j["fs"]
                pa = pbi % 4
                pbi += 1
                pbs = pbi % 4
                pbi += 1
                op("pe", lambda e, f=f, pa=pa, mm=mm: mm(e, f, pa), reads=[b_W, b_hT[hb]], writes=[PB[pa]])
                op("pe", lambda e, fs=fs, pbs=pbs, mm=mm: mm(e, fs, pbs), reads=[b_W, b_hT[hb]], writes=[PB[pbs]])
                if j.get("dst_sb") is not None:
                    dst_ap, dst_buf = j["dst_sb"](s0), b_kv
                    o = None
                else:
                    o = obi % 3
                    obi += 1
                    dst_ap, dst_buf = ob[o][:], b_ob[o]
                if not j["norm"]:
                    op("dve", lambda e, f=f, pa=pa: e.scalar_tensor_tensor(tt[0][:], bank(pa), bias_fm[:, f:f + 1], cs_t[:], ALU.add, ALU.mult),
                       reads=[b_cs, B_const], writes=[PB[pa], b_tt[0]])
                    op("dve", lambda e, fs=fs, pbs=pbs: e.scalar_tensor_tensor(tt[1][:], bank(pbs), bias_fm[:, fs:fs + 1], sn_t[:], ALU.add, ALU.mult),
                       reads=[b_cs, B_const], writes=[PB[pbs], b_tt[1]])
                    op("pool", lambda e, dst_ap=dst_ap: e.tensor_tensor(dst_ap, tt[0][:], tt[1][:], ALU.add),
                       reads=[b_tt[0], b_tt[1]], writes=[dst_buf])
                else:
                    gi = j["g"]
                    op("act", lambda e, f=f, pa=pa: e.activation(tt[2][:], bank(pa), AF.Square, bias=bias_fm[:, f:f + 1]),
                       reads=[B_const], writes=[PB[pa], b_tt[2]])
                    op("dve", lambda e, f=f, pa=pa, gi=gi: e.tensor_scalar(tt[0][:], bank(pa), bias_fm[:, f:f + 1], cvec[:, gi:gi + 1], ALU.add, ALU.mult),
                       reads=[B_const], writes=[PB[pa], b_tt[0]])
                    op("dve", lambda e, fs=fs, pbs=pbs, gi=gi: e.tensor_scalar(tt[1][:], bank(pbs), bias_fm[:, fs:fs + 1], cvec[:, gi + 1:gi + 2], ALU.add, ALU.mult),
                       reads=[B_const], writes=[PB[pbs], b_tt[1]])
                    op("pe", lambda e: e.matmul(bank(4), bones[:], tt[2][:], start=True, stop=True), reads=[b_W, b_tt[2]], writes=[PB[4]])
                    op("pool", lambda e: e.tensor_tensor(tt[0][:], tt[0][:], cs_t[:], ALU.mult), reads=[b_cs], writes=[b_tt[0]])
                    op("pool", lambda e: e.tensor_tensor(tt[1][:], tt[1][:], sn_t[:], ALU.mult), reads=[b_cs], writes=[b_tt[1]])
                    op("pool", lambda e: e.tensor_tensor(tt[0][:], tt[0][:], tt[1][:], ALU.add), reads=[b_tt[1]], writes=[b_tt[0]])
                    op("act", lambda e: e.activation(tt[3][:], bank(4), AF.Sqrt, bias=epsb[:, 1:2], scale=1.0 / 64),
                       reads=[B_const], writes=[PB[4], b_tt[3]])
                    op("dve", lambda e: e.reciprocal(tt[3][:], tt[3][:]), writes=[b_tt[3]])
                    op("dve", lambda e, dst_ap=dst_ap: e.tensor_tensor(dst_ap, tt[0][:], tt[3][:], ALU.mult),
                       reads=[b_tt[0], b_tt[3]], writes=[dst_buf])
                if o is not None:
                    dst = j["dst"](s0)
                    dma("sp", lambda e, o=o, dst=dst: e.dma_start(out=dst, in_=ob[o][:]), reads=[b_ob[o]])
            if vjob:
                for blk in range(4):
                    pv = 5 + (blk % 2)
                    ncol = 256 if vjob == "ab" else 128
                    c0 = 0 if vjob == "ab" else 128

                    def mmv(e, blk=blk, pv=pv, ncol=ncol, c0=c0, hb=hb):
                        for kc in range(KC):
                            r = e.matmul(bank(pv)[:, 0:ncol], hT[hb][:, kc * 512 + blk * 128:kc * 512 + (blk + 1) * 128],
                                         Wv[:, kc * 256 + c0:kc * 256 + c0 + ncol], start=(kc == 0), stop=(kc == KC - 1))
                        return r
                    op("pe", mmv, reads=[b_W, b_hT[hb]], writes=[PB[pv]])
                    sblk = s0 // 128 + blk
                    if vjob == "ab":
                        ablk = (sblk if s0 < NOWN else NOWN // 128 + (s0 - S) // 128 + blk)
                        op("dve", lambda e, pv=pv, ablk=ablk: e.tensor_tensor(
                            Va[:, ablk * 192:(ablk + 1) * 192].rearrange("p (a c) -> p a c", c=64)[:, 0:3:2, :],
                            bank(pv)[:, 0:128].rearrange("p (a c) -> p a c", c=64),
                            vbias[:, 0:128].rearrange("p (a c) -> p a c", c=64), ALU.add),
                           reads=[b_W], writes=[PB[pv], b_kv])
                    op("dve", lambda e, pv=pv, sblk=sblk, ncol=ncol: e.tensor_tensor(
                        Vb[:, sblk * 192:(sblk + 1) * 192].rearrange("p (a c) -> p a c", c=64)[:, 0:3:2, :],
                        bank(pv)[:, ncol - 128:ncol].rearrange("p (a c) -> p a c", c=64),
                        vbias[:, 128:256].rearrange("p (a c) -> p a c", c=64), ALU.add),
                       reads=[b_W], writes=[PB[pv], b_kv])
        S_.barrier()

    def acol(s0):
        return s0 if s0 < NOWN else NOWN + (s0 - S)

    own_chunks = list(range(NCH_OWN))
    x1c, x2c = X1 // 512, X2 // 512
    run_pass("pA", list(range(NCH)),
             [dict(kind="rope", f=F_KB, fs=F_KBS, norm=True, g=2, dst_sb=lambda s0: KbT[:, s0:s0 + 512])], "b")
    featsB = [dict(kind="rope", f=F_KA, fs=F_KAS, norm=False, dst_sb=lambda s0: KaT[:, acol(s0):acol(s0) + 512])]
    for g in range(4):
        featsB.append(dict(kind="rope", f=F_QA + g, fs=F_QAS + g, norm=False, dst_sb=None,
                           dst=lambda s0, g=g: qa_d[g, :, acol(s0):acol(s0) + 512]))
        featsB.append(dict(kind="rope", f=F_QB + g, fs=F_QBS + g, norm=True, g=0, dst_sb=None,
                           dst=lambda s0, g=g: qb_d[g, :, acol(s0):acol(s0) + 512]))
    run_pass("pB", own_chunks + [x1c], featsB, "ab")
    run_pass("pB2", [x2c], featsB[:1], "ab")
    featsC = [dict(kind="gate", f=F_G + j, dst=lambda s0, j=j: g_d[j, :, acol(s0):acol(s0) + 512]) for j in range(16)]
    run_pass("pC", own_chunks + [x1c], featsC, None)

    if DEBUG:
        dbg_mod = nc.dram_tensor("dbg_mod", [128, 96], F32, kind="ExternalOutput").ap()
        dbg_kb = nc.dram_tensor("dbg_kb", [128, NSLOT], BF16, kind="ExternalOutput").ap()
        dbg_ka = nc.dram_tensor("dbg_ka", [128, NA], BF16, kind="ExternalOutput").ap()
        dbg_vb = nc.dram_tensor("dbg_vb", [128, NBS * 192], BF16, kind="ExternalOutput").ap()
        dbg_va = nc.dram_tensor("dbg_va", [128, NAB * 192], BF16, kind="ExternalOutput").ap()
        dma("sp", lambda e: e.dma_start(out=dbg_mod[:, 0:64], in_=modfm[:]), reads=[B_const])
        dma("sp", lambda e: e.dma_start(out=dbg_mod[:, 64:96], in_=mod1p[:]), reads=[B_const])
        dma("sp", lambda e: e.dma_start(out=dbg_kb, in_=KbT[:]), reads=[b_kv])
        dma("sp", lambda e: e.dma_start(out=dbg_ka, in_=KaT[:]), reads=[b_kv])
        dma("sp", lambda e: e.dma_start(out=dbg_vb, in_=Vb[:]), reads=[b_kv])
        dma("sp", lambda e: e.dma_start(out=dbg_va, in_=Va[:]), reads=[b_kv])
        S_.barrier()
    TOP = 229376
    HIGH0 = TOP - 55296
    WUP0 = HIGH0 - KC * 2 * DFF * 2
    AH = Arena(HIGH0)
    Wa = AH.alloc("Wa", 4 * D, BF16)
    Wb = AH.alloc("Wb", 4 * D, BF16)
    Wo = AH.alloc("Wo", KC * D, BF16)
    lnp = AH.alloc("lnp_sb", 2 * D, F32)
    stgH = [AH.alloc("stgH%d" % i, D, F32) for i in range(2)]
    gbcH = AH.alloc("gbcH", D, F32)
    Wup = nc.alloc_sbuf_tensor_at("Wup", [128, KC * 2 * DFF], BF16, offset=WUP0)
    Wdn = nc.alloc_sbuf_tensor_at("Wdn", [128, NFC * D], BF16, offset=HIGH0)
    b_w2, b_gbcH, b_w3 = Buf("w2"), Buf("gbcH"), Buf("w3")
    b_stgH = [Buf("stgH0"), Buf("stgH1")]
    stg_cnt = [0]

    def stage_job(src_ap, dst_ap, ncols, mul, wbuf):
        def job():
            w = stg_cnt[0] % 2
            stg_cnt[0] += 1
            dma("sp", lambda e: e.dma_start(out=stgH[w][:, 0:ncols], in_=src_ap), writes=[b_stgH[w]])
            if mul:
                op("pool", lambda e: e.tensor_tensor(dst_ap, stgH[w][:, 0:ncols], gbcH[:, 0:ncols], ALU.mult),
                   reads=[b_stgH[w], b_gbcH], writes=[wbuf])
            else:
                op("pool", lambda e: e.tensor_copy(dst_ap, stgH[w][:, 0:ncols]), reads=[b_stgH[w]], writes=[wbuf])
        return job

    wout_v = din["w_out"].rearrange("(kc p) n -> p kc n", p=128)
    wup_v = din["w_up"].rearrange("(kc p) n -> p kc n", p=128)
    pf2 = [lambda: dma("sp", lambda e: e.dma_start(out=lnp[:], in_=din["lnp"][:, 0:2 * D]), writes=[b_w2]),
           lambda: dma("sp", lambda e: e.dma_start(out=gbcH[:], in_=gate_d[:, 0:D]), writes=[b_gbcH])]
    for g in range(4):
        pf2.append(stage_job(din["wba"][:, g * D:(g + 1) * D], Wa[:, g * D:(g + 1) * D], D, False, b_w2))
        pf2.append(stage_job(din["wbb"][:, g * D:(g + 1) * D], Wb[:, g * D:(g + 1) * D], D, False, b_w2))
    for kc in range(KC):
        pf2.append(stage_job(wout_v[:, kc, :], Wo[:, kc * D:(kc + 1) * D], D, True, b_w2))
    pf3 = []
    for kc in range(KC):
        for c0 in range(0, 2 * DFF, 1024):
            c1 = min(c0 + 1024, 2 * DFF)
            pf3.append(stage_job(wup_v[:, kc, c0:c1], Wup[:, kc * 2 * DFF + c0:kc * 2 * DFF + c1], c1 - c0, False, b_w3))

    def run_prefetch(jobs, tiles_left):
        k = -(-len(jobs) // max(tiles_left, 1))
        for _ in range(k):
            if jobs:
                jobs.pop(0)()

    A = Arena(KV_END)
    masks = A.alloc("masks_sb", 9 * 128, BF16)
    sinkexp = A.alloc("sinkexp", 512, F32)
    qa_t = [A.alloc("qa_t%d" % i, 512, BF16) for i in range(2)]
    qb_t = [A.alloc("qb_t%d" % i, 512, BF16) for i in range(2)]
    pt = [A.alloc("pt%d" % i, 1024, BF16) for i in range(3)]
    rs = A.alloc("rs", 512, F32)
    rsh = A.alloc("rsh", 512, F32)
    oT = [A.alloc("oT%d" % i, 512, BF16) for i in range(4)]
    masks32 = A.alloc("masks32", 9 * 128, F32)
    assert A.off <= HIGH0, A.off
    b_q = [Buf("q0"), Buf("q1")]
    b_pt = [Buf("pt") for _ in pt]
    b_rs, b_rsh = Buf("rs"), Buf("rsh")
    b_oT = [Buf("oT") for _ in oT]
    b_m = Buf("m")
    b_m32 = Buf("m32")
    dma("sp", lambda e: e.dma_start(out=masks32[:], in_=din["masks"]), writes=[b_m32])
    op("dve", lambda e: e.tensor_copy(masks[:], masks32[:]), reads=[b_m32], writes=[b_m])
    dma("sp", lambda e: e.dma_start(out=sinkexp[:], in_=din["sinkraw"]), writes=[b_m])
    op("act", lambda e: e.activation(sinkexp[:], sinkexp[:], AF.Exp), writes=[b_m])
    SB = [0, 2]
    ACC = {"a": (4, 5), "b": (6, 7)}
    ablk0 = NOWN // 128
    ucount = [0]
    oi = [0]
    pending_fin = []

    def flush_fin():
        while pending_fin:
            pending_fin.pop(0)()

    def attention(qi, br, qtile, units, on_done):
        a0, a1 = ACC[br]
        n = len(units)

        def s_op(u):
            KT, c0, V, vb, mk = units[u]
            sb = SB[(ucount[0] + u) % 2]

            def f(e):
                e.matmul(bank(sb), KT[0:64, c0:c0 + 128], qtile[0:64, :], start=True, stop=True)
                return e.matmul(bank(sb + 1), KT[64:128, c0:c0 + 128], qtile[64:128, :], start=True, stop=True)
            op("pe", f, reads=[b_kv, b_q[qi]], writes=[PB[sb], PB[sb + 1]])

        s_op(0)
        for u in range(n):
            KT, c0, V, vb, mk = units[u]
            if u + 1 < n:
                s_op(u + 1)
            sb = SB[(ucount[0] + u) % 2]
            pi = (ucount[0] + u) % 3
            op("act", lambda e, sb=sb, pi=pi: e.activation(pt[pi][:], bank(sb, 2), AF.Exp, scale=0.125),
               writes=[PB[sb], PB[sb + 1], b_pt[pi]])
            if mk is not None:
                op("dve", lambda e, pi=pi, mk=mk: e.tensor_tensor(
                    pt[pi][:, :].rearrange("p (h q) -> p h q", q=128), pt[pi][:, :].rearrange("p (h q) -> p h q", q=128),
                    _bc_mid(masks[:, mk * 128:(mk + 1) * 128], 8), ALU.mult), reads=[b_m], writes=[b_pt[pi]])

            def pv(e, V=V, vb=vb, pi=pi, u=u):
                e.matmul(bank(a0), V[:, vb * 192:vb * 192 + 128], pt[pi][:, 0:512], start=(u == 0), stop=(u == n - 1))
                return e.matmul(bank(a1), V[:, vb * 192 + 64:vb * 192 + 192], pt[pi][:, 512:1024], start=(u == 0), stop=(u == n - 1))
            op("pe", pv, reads=[b_kv, b_pt[pi]], writes=[PB[a0], PB[a1]])
            if u == 2:
                flush_fin()
        ucount[0] += n

        def finalize():
            if br == "a":
                def f(e):
                    e.tensor_tensor(rs[64:128, :], bank(a0)[64:128, :], sinkexp[64:128, :], ALU.add)
                    return e.tensor_tensor(rs[0:64, :], bank(a1)[0:64, :], sinkexp[0:64, :], ALU.add)
            else:
                def f(e):
                    e.tensor_copy(rs[64:128, :], bank(a0)[64:128, :])
                    return e.tensor_copy(rs[0:64, :], bank(a1)[0:64, :])
            op("dve", f, reads=[b_m], writes=[PB[a0], PB[a1], b_rs])
            op("dve", lambda e: e.reciprocal(rs[:], rs[:]), writes=[b_rs])

            def f2(e):
                e.activation(rsh[0:64, :], rs[64:128, :], AF.Copy)
                return e.activation(rsh[64:128, :], rs[0:64, :], AF.Copy)
            op("act", f2, reads=[b_rs], writes=[b_rsh])
            o = oi[0] % 4
            oi[0] += 1

            def f3(e):
                e.tensor_tensor(oT[o][0:64, :], bank(a0)[0:64, :], rsh[0:64, :], ALU.mult)
                return e.tensor_tensor(oT[o][64:128, :], bank(a1)[64:128, :], rsh[64:128, :], ALU.mult)
            op("dve", f3, reads=[b_rsh], writes=[PB[a0], PB[a1], b_oT[o]])
            on_done(o)
        pending_fin.append(finalize)

    for t in range(NTQ):
        qi = t % 2
        qc = t * 128 if t < NT else NOWN + 384
        dma("sp", lambda e, qi=qi, qc=qc: e.dma_start(out=qa_t[qi][:, :].rearrange("p (g q) -> p g q", g=4),
                                                       in_=qa_d[:, :, qc:qc + 128].rearrange("g p q -> p g q")), writes=[b_q[qi]])
        dma("sp", lambda e, qi=qi, qc=qc: e.dma_start(out=qb_t[qi][:, :].rearrange("p (g q) -> p g q", g=4),
                                                       in_=qb_d[:, :, qc:qc + 128].rearrange("g p q -> p g q")), writes=[b_q[qi]])
        ctxu = [(KaT, (ablk0 + 4) * 128, Va, ablk0 + 4, None), (KaT, (ablk0 + 5) * 128, Va, ablk0 + 5, None)]

        def ka(blk, mk):
            return (KaT, blk * 128, Va, blk, mk)
        if t < NT:
            left = ka(t - 1, 0) if t > 0 else ka(ablk0 + 0, 1)
            right = ka(t + 1, 2) if t < NT - 1 else ka(ablk0 + 1, 3)
            ua = [left, ka(t, None), right] + ctxu
        else:
            ua = [ka(ablk0 + 2, 4), ka(ablk0 + 0, 5), ka(0, 6), ka(NT - 1, 7), ka(ablk0 + 1, 8)] + ctxu
        attention(qi, "a", qa_t[qi], ua,
                  lambda o, t=t: dma("sp", lambda e: e.dma_start(out=oa_d[t], in_=oT[o][:]), reads=[b_oT[o]]))
        ub = [(KbT, kb * 128, Vb, kb, None) for kb in range(S // 128)]
        ub += [(KbT, X2 + kb * 128, Vb, X2 // 128 + kb, None) for kb in range(2)]
        attention(qi, "b", qb_t[qi], ub,
                  lambda o, t=t: dma("sp", lambda e: e.dma_start(out=ob_d[t], in_=oT[o][:]), reads=[b_oT[o]]))
        run_prefetch(pf2, NTQ - t)
    flush_fin()
    while pf2:
        pf2.pop(0)()
    S_.barrier()

    def layer_norm(z, b_z, st, b_st, g_ap, b_ap, dst, b_dst, b_par):
        op("dve", lambda e: e.memset(st[:, 0:2], 0.0), writes=[b_st])
        op("act", lambda e: e.activation(dst[:], z[:], AF.Identity, accum_out=st[:, 0:1]), reads=[b_z], writes=[b_dst, b_st])
        op("act", lambda e: e.activation(dst[:], z[:], AF.Square, accum_out=st[:, 1:2]), reads=[b_z], writes=[b_dst, b_st])
        op("dve", lambda e: e.tensor_scalar(st[:, 2:3], st[:, 0:1], 1.0 / D, None, ALU.mult), writes=[b_st])
        op("dve", lambda e: e.tensor_tensor(st[:, 3:4], st[:, 2:3], st[:, 2:3], ALU.mult), writes=[b_st])
        op("dve", lambda e: e.scalar_tensor_tensor(st[:, 4:5], st[:, 1:2], 1.0 / D, st[:, 3:4], ALU.mult, ALU.subtract), writes=[b_st])
        op("act", lambda e: e.activation(st[:, 5:6], st[:, 4:5], AF.Sqrt, bias=epsb[:, 0:1]), reads=[B_const], writes=[b_st])
        op("dve", lambda e: e.reciprocal(st[:, 5:6], st[:, 5:6]), writes=[b_st])
        op("dve", lambda e: e.scalar_tensor_tensor(st[:, 6:7], st[:, 2:3], -1.0, st[:, 5:6], ALU.mult, ALU.mult), writes=[b_st])
        op("act", lambda e: e.activation(dst[:], z[:], AF.Identity, bias=st[:, 6:7], scale=st[:, 5:6]),
           reads=[b_z, b_st], writes=[b_dst])
        op("dve", lambda e: e.tensor_tensor(dst[:], dst[:], g_ap, ALU.mult), reads=[b_par], writes=[b_dst])
        op("pool", lambda e: e.tensor_tensor(dst[:], dst[:], b_ap, ALU.add), reads=[b_par], writes=[b_dst])

    A = Arena(PERSIST)
    oa_t = [A.alloc("oa_t%d" % i, 512, BF16) for i in range(2)]
    ob_t = [A.alloc("ob_t%d" % i, 512, BF16) for i in range(2)]
    g_t = [A.alloc("g_t%d" % i, 16 * 128, BF16) for i in range(2)]
    x_t = [A.alloc("x_t%d" % i, D, F32) for i in range(2)]
    mt1 = [A.alloc("mt1_%d" % i, 512, F32) for i in range(2)]
    mt2 = [A.alloc("mt2_%d" % i, 512, F32) for i in range(2)]
    mT = [A.alloc("mT%d" % i, KC * 128, BF16) for i in range(2)]
    z_t = [A.alloc("z_t%d" % i, D, F32) for i in range(2)]
    st = [A.alloc("st%d" % i, 8, F32) for i in range(2)]
    x1_t = [A.alloc("x1_t%d" % i, D, F32) for i in range(2)]
    h2s = [A.alloc("h2s%d" % i, KC * 128, BF16) for i in range(2)]
    assert A.off <= WUP0, A.off
    b_in2 = [Buf("in2") for _ in range(2)]
    b_xt = [Buf("xt") for _ in range(2)]
    b_mt1 = [Buf("mt1") for _ in range(2)]
    b_mt2 = [Buf("mt2") for _ in range(2)]
    b_mT = [Buf("mT") for _ in range(2)]
    b_z = [Buf("z") for _ in range(2)]
    b_st = [Buf("st") for _ in range(2)]
    b_x1 = [Buf("x1") for _ in range(2)]
    b_h2s = [Buf("h2s") for _ in range(2)]

    def stageA(t):
        i2 = t % 2
        qc = t * 128 if t < NT else NOWN + 384
        dma("sp", lambda e: e.dma_start(out=oa_t[i2][:], in_=oa_d[t]), writes=[b_in2[i2]])
        dma("sp", lambda e: e.dma_start(out=ob_t[i2][:], in_=ob_d[t]), writes=[b_in2[i2]])
        dma("sp", lambda e: e.dma_start(out=g_t[i2][:, :].rearrange("p (j q) -> p j q", j=16),
                                        in_=g_d[:, :, qc:qc + 128].rearrange("j p q -> p j q")), writes=[b_in2[i2]])
        dma("sp", lambda e: e.dma_start(out=x_t[i2][:], in_=din["xtok"][t * 128:(t + 1) * 128, :]), writes=[b_xt[i2]])
        for half in range(2):
            def mmA(e, half=half):
                for oc in range(4):
                    for g in range(4):
                        r = e.matmul(bank(2 * half)[:, oc * 128:(oc + 1) * 128], Wa[:, g * D + (half * 4 + oc) * 128:g * D + (half * 4 + oc + 1) * 128],
                                     oa_t[i2][:, g * 128:(g + 1) * 128], start=(g == 0), stop=(g == 3), skip_group_check=True)
                return r

            def mmB(e, half=half):
                for oc in range(4):
                    for g in range(4):
                        r = e.matmul(bank(2 * half + 1)[:, oc * 128:(oc + 1) * 128], Wb[:, g * D + (half * 4 + oc) * 128:g * D + (half * 4 + oc + 1) * 128],
                                     ob_t[i2][:, g * 128:(g + 1) * 128], start=(g == 0), stop=(g == 3), skip_group_check=True)
                return r
            op("pe", mmA, reads=[b_w2, b_in2[i2]], writes=[PB[2 * half]])
            op("pe", mmB, reads=[b_w2, b_in2[i2]], writes=[PB[2 * half + 1]])
            op("dve", lambda e, half=half: e.tensor_tensor(mt1[half][:], bank(2 * half), g_t[i2][:, half * 512:(half + 1) * 512], ALU.mult),
               reads=[b_in2[i2]], writes=[PB[2 * half], b_mt1[half]])
            op("dve", lambda e, half=half: e.tensor_tensor(mt2[half][:], bank(2 * half + 1), g_t[i2][:, 1024 + half * 512:1024 + (half + 1) * 512], ALU.mult),
               reads=[b_in2[i2]], writes=[PB[2 * half + 1], b_mt2[half]])
            op("pool", lambda e, half=half: e.tensor_tensor(mT[i2][:, half * 512:(half + 1) * 512], mt1[half][:], mt2[half][:], ALU.add),
               reads=[b_mt1[half], b_mt2[half]], writes=[b_mT[i2]])

    def stageB(t):
        i2 = t % 2

        def mmY(e):
            for hh in range(2):
                for kc in range(KC):
                    r = e.matmul(bank(4 + hh), mT[i2][:, kc * 128:(kc + 1) * 128], Wo[:, kc * D + hh * 512:kc * D + (hh + 1) * 512],
                                 start=(kc == 0), stop=(kc == KC - 1))
            return r
        op("pe", mmY, reads=[b_w2, b_mT[i2]], writes=[PB[4], PB[5]])
        op("dve", lambda e: e.scalar_tensor_tensor(z_t[i2][:], x_t[i2][:], ALPHA, bank(4, 2), ALU.mult, ALU.add),
           reads=[b_xt[i2]], writes=[PB[4], PB[5], b_z[i2]])
        layer_norm(z_t[i2], b_z[i2], st[i2], b_st[i2], lnp[:, 0:D], lnp[:, D:2 * D], x1_t[i2], b_x1[i2], b_w2)
        dma("sp", lambda e: e.dma_start(out=x1_d[t * 128:(t + 1) * 128, :], in_=x1_t[i2][:]), reads=[b_x1[i2]])

    def stageC(t):
        i2 = t % 2

        def tr(e):
            for kc in range(KC):
                r = e.transpose(bank(6, 2)[:, kc * 128:(kc + 1) * 128], x1_t[i2][:, kc * 128:(kc + 1) * 128], ident[:])
            return r
        op("pe", tr, reads=[b_x1[i2], B_const], writes=[PB[6], PB[7]])
        for kc in range(KC):
            if kc % 2 == 0:
                op("act", lambda e, kc=kc: e.activation(h2s[i2][:, kc * 128:(kc + 1) * 128], bank(6, 2)[:, kc * 128:(kc + 1) * 128],
                                                        AF.Identity, bias=modfm[:, 24 + kc:25 + kc], scale=mod1p[:, 8 + kc:9 + kc]),
                   reads=[B_const], writes=[PB[6], PB[7], b_h2s[i2]])
            else:
                op("dve", lambda e, kc=kc: e.tensor_scalar(h2s[i2][:, kc * 128:(kc + 1) * 128], bank(6, 2)[:, kc * 128:(kc + 1) * 128],
                                                           mod1p[:, 8 + kc:9 + kc], modfm[:, 24 + kc:25 + kc], ALU.mult, ALU.add),
                   reads=[B_const], writes=[PB[6], PB[7], b_h2s[i2]])
        h2v = h2s[i2][:, :].rearrange("p (kc q) -> p kc q", kc=KC)
        if t < NT:
            dma("sp", lambda e: e.dma_start(out=h2_d[:, :, 1 + t * 128:1 + (t + 1) * 128].rearrange("kc p q -> p kc q"), in_=h2v),
                reads=[b_h2s[i2]])
        else:
            op("dve", lambda e: e.tensor_tensor(h2v[:, :, 0:2], h2v[:, :, 0:2], _bc_mid(hm[:, 0:2], KC), ALU.mult),
               reads=[B_const], writes=[b_h2s[i2]])
            dma("sp", lambda e: e.dma_start(out=h2_d[:, :, 0:1].rearrange("kc p q -> p kc q"), in_=h2v[:, :, 0:1],
                                            allow_slow_non_contiguous=True), reads=[b_h2s[i2]])
            dma("sp", lambda e: e.dma_start(out=h2_d[:, :, NOWN + 1:NOWN + 2].rearrange("kc p q -> p kc q"), in_=h2v[:, :, 1:2],
                                            allow_slow_non_contiguous=True), reads=[b_h2s[i2]])

    stageA(0)
    for t in range(NTQ):
        stageB(t)
        if t + 1 < NTQ:
            stageA(t + 1)
        stageC(t)
        run_prefetch(pf3, NTQ - t)
    while pf3:
        pf3.pop(0)()
    S_.barrier()

    A = Arena(PERSIST)
    lnp2 = A.alloc("lnp2", 2 * D, F32)
    cw = A.alloc("cw_sb", 132, F32)
    cb = A.alloc("cb_sb", 44, F32)
    gbc2 = A.alloc("gbc2", D, F32)
    wstg2 = [A.alloc("wstg2_%d" % i, D, F32) for i in range(2)]
    h2t = [A.alloc("h2t%d" % i, KC * 258, BF16) for i in range(2)]
    x1r = [A.alloc("x1r%d" % i, D, F32) for i in range(4)]
    cg = [A.alloc("cg%d" % i, 256, F32) for i in range(2)]
    cv_ = [A.alloc("cv%d" % i, 256, F32) for i in range(2)]
    sg = [A.alloc("sg%d" % i, 256, F32) for i in range(2)]
    aT = [A.alloc("aT%d" % i, 256, BF16) for i in range(3)]
    z2 = A.alloc("z2", D, F32)
    st2 = A.alloc("st2", 8, F32)
    o_t = [A.alloc("o_t%d" % i, D, F32) for i in range(2)]
    assert A.off <= WUP0, A.off
    b_gbc2 = Buf("gbc2")
    b_wstg2 = [Buf("wstg2") for _ in range(2)]
    b_h2t = [Buf("h2t") for _ in range(2)]
    b_x1r = [Buf("x1r") for _ in range(4)]
    b_cg = [Buf("cg") for _ in range(2)]
    b_cv = [Buf("cv") for _ in range(2)]
    b_sg = [Buf("sg") for _ in range(2)]
    b_aT = [Buf("aT") for _ in range(3)]
    b_z2, b_st2 = Buf("z2"), Buf("st2")
    b_ot = [Buf("ot") for _ in range(2)]
    b_wdn, b_c3 = Buf("wdn"), Buf("c3")
    dma("sp", lambda e: e.dma_start(out=lnp2[:], in_=din["lnp"][:, 2 * D:4 * D]), writes=[b_c3])
    dma("sp", lambda e: e.dma_start(out=cw[:], in_=din["cw"]), writes=[b_c3])
    dma("sp", lambda e: e.dma_start(out=cb[:], in_=din["cb"]), writes=[b_c3])
    dma("sp", lambda e: e.dma_start(out=gbc2[:], in_=gate_d[:, D:2 * D]), writes=[b_gbc2])
    wdn_v = din["w_down"].rearrange("(fc p) n -> p fc n", p=128)
    NT3 = NOWN // 256

    def load_tile3(i):
        hi = i % 2
        dma("sp", lambda e: e.dma_start(out=h2t[hi][:, :].rearrange("p (kc q) -> p kc q", kc=KC),
                                        in_=h2_d[:, :, i * 256:i * 256 + 258].rearrange("kc p q -> p kc q")), writes=[b_h2t[hi]])
        for tt_ in range(2):
            xi = (i * 2 + tt_) % 4
            dma("sp", lambda e, xi=xi, tt_=tt_: e.dma_start(out=x1r[xi][:], in_=x1_d[i * 256 + tt_ * 128:i * 256 + (tt_ + 1) * 128, :]),
                writes=[b_x1r[xi]])

    load_tile3(0)
    for fc in range(NFC):
        w = fc % 2
        dma("sp", lambda e, w=w, fc=fc: e.dma_start(out=wstg2[w][:], in_=wdn_v[:, fc, :]), writes=[b_wstg2[w]])
        op("pool", lambda e, w=w, fc=fc: e.tensor_tensor(Wdn[:, fc * D:(fc + 1) * D], wstg2[w][:], gbc2[:], ALU.mult),
           reads=[b_wstg2[w], b_gbc2], writes=[b_wdn])
    step = [0]
    for i in range(NT3):
        hi = i % 2

        def up(fc, hi=hi):
            k = step[0] + fc
            pg, pv_ = (k % 2) * 2, (k % 2) * 2 + 1

            def f(e):
                for (pb, col) in ((pg, fc * 128), (pv_, DFF + fc * 128)):
                    for kc in range(KC):
                        r = e.matmul(bank(pb)[:, 0:258], Wup[:, kc * 2 * DFF + col:kc * 2 * DFF + col + 128],
                                     h2t[hi][:, kc * 258:(kc + 1) * 258], start=(kc == 0), stop=(kc == KC - 1))
                return r
            op("pe", f, reads=[b_w3, b_h2t[hi]], writes=[PB[pg], PB[pv_]])

        up(0)
        for fc in range(NFC):
            k = step[0] + fc
            pg, pv_ = (k % 2) * 2, (k % 2) * 2 + 1
            ci = k % 2
            if fc + 1 < NFC:
                up(fc + 1)
            if fc == NFC // 2 and i + 1 < NT3:
                load_tile3(i + 1)
            for (pb, dst, bd, fi) in ((pg, cg[ci], b_cg[ci], fc), (pv_, cv_[ci], b_cv[ci], NFC + fc)):
                op("act", lambda e, pb=pb, dst=dst, fi=fi: e.activation(dst[:], bank(pb)[:, 1:257], AF.Identity,
                                                                        bias=cb[:, fi:fi + 1], scale=cw[:, fi * 3 + 1:fi * 3 + 2]),
                   reads=[b_c3], writes=[PB[pb], bd])
                op("dve", lambda e, pb=pb, dst=dst, fi=fi: e.scalar_tensor_tensor(dst[:], bank(pb)[:, 0:256], cw[:, fi * 3:fi * 3 + 1], dst[:],
                                                                                  ALU.mult, ALU.add), reads=[b_c3], writes=[PB[pb], bd])
                op("dve", lambda e, pb=pb, dst=dst, fi=fi: e.scalar_tensor_tensor(dst[:], bank(pb)[:, 2:258], cw[:, fi * 3 + 2:fi * 3 + 3], dst[:],
                                                                                  ALU.mult, ALU.add), reads=[b_c3], writes=[PB[pb], bd])
            op("act", lambda e, ci=ci: e.activation(sg[ci][:], cg[ci][:], AF.Silu), reads=[b_cg[ci]], writes=[b_sg[ci]])
            ai = k % 3
            op("pool", lambda e, ci=ci, ai=ai: e.tensor_tensor(aT[ai][:], sg[ci][:], cv_[ci][:], ALU.mult),
               reads=[b_sg[ci], b_cv[ci]], writes=[b_aT[ai]])

            def dn(e, fc=fc, ai=ai):
                for tt_ in range(2):
                    for hh in range(2):
                        r = e.matmul(bank(4 + tt_ * 2 + hh), aT[ai][:, tt_ * 128:(tt_ + 1) * 128], Wdn[:, fc * D + hh * 512:fc * D + (hh + 1) * 512],
                                     start=(fc == 0), stop=(fc == NFC - 1))
                return r
            op("pe", dn, reads=[b_wdn, b_aT[ai]], writes=[PB[4], PB[5], PB[6], PB[7]])
        step[0] += NFC
        for tt_ in range(2):
            xi = (i * 2 + tt_) % 4
            oi_ = (i * 2 + tt_) % 2
            op("dve", lambda e, xi=xi, tt_=tt_: e.scalar_tensor_tensor(z2[:], x1r[xi][:], ALPHA, bank(4 + tt_ * 2, 2), ALU.mult, ALU.add),
               reads=[b_x1r[xi]], writes=[PB[4 + tt_ * 2], PB[5 + tt_ * 2], b_z2])
            layer_norm(z2, b_z2, st2, b_st2, lnp2[:, 0:D], lnp2[:, D:2 * D], o_t[oi_], b_ot[oi_], b_c3)
            r0 = i * 256 + tt_ * 128
            dma("sp", lambda e, oi_=oi_, r0=r0: e.dma_start(out=out_d[r0:r0 + 128, :], in_=o_t[oi_][:]), reads=[b_ot[oi_]])
    S_.emit()
    return nc


_CACHE = {}


def run(inputs, S):
    sh = _shared_inputs(inputs)
    in_maps = []
    for core in range(8):
        d = _core_inputs(inputs, sh, core, S)
        in_maps.append({k: np.ascontiguousarray(d[k], dtype=np.float32) for k in IN_SHAPES(S)})
    if S not in _CACHE:
        _CACHE[S] = build_program(S)
    res = run_bass_kernel_spmd(_CACHE[S], in_maps, core_ids=list(range(8)))
    global LAST
    LAST = (res, in_maps)
    NOWN = S // 4
    out = np.zeros((2, S, D), np.float32)
    for core in range(8):
        b, r = core // 4, core % 4
        out[b, r * NOWN:(r + 1) * NOWN] = res.results[core]["out"]
    return out


def kernel(**inputs):
    return run(inputs, 16384)
```

```python
import contextlib
import numpy as np
import concourse.bass as bass
import concourse.mybir as mybir
from concourse.bass_utils import run_bass_kernel_spmd

F32 = mybir.dt.float32
BF16 = mybir.dt.bfloat16
AF = mybir.ActivationFunctionType
ALU = mybir.AluOpType
AX = mybir.AxisListType

D = 1024
KC = 8
GRID_W = 64
CTX = 256
HD = 64
DFF = 2816
NFC = 22
LN_EPS = 1e-5
QK_EPS = 1e-6
ALPHA = 2.0 ** 0.25
OFF_QA, OFF_KA, OFF_VA, OFF_QB, OFF_KB, OFF_VB, OFF_GA, OFF_GB = 0, 512, 640, 768, 1280, 1408, 1536, 2560
F_QA, F_QAS, F_QB, F_QBS, F_KA, F_KAS, F_KB, F_KBS, F_G = 0, 4, 8, 12, 16, 17, 18, 19, 20
NF = 36


class Buf:
    __slots__ = ("name", "w", "r")

    def __init__(self, name=""):
        self.name = name
        self.w = None
        self.r = []


class _Op:
    __slots__ = ("eng", "fn", "deps", "dma", "sig", "cnt", "sem")

    def __init__(self, eng, fn, deps, dma):
        self.eng, self.fn, self.deps, self.dma = eng, fn, deps, dma
        self.sig, self.cnt, self.sem = False, 0, None


COMPUTE = ("pe", "act", "dve", "pool")


class Sched:
    def __init__(self, nc, n_dma_sems=24, same_engine_sync=True):
        self.nc = nc
        self.ops = []
        self.n_dma_sems = n_dma_sems
        self.same_engine_sync = same_engine_sync
        self.last_on = {}
        self.pending = {}
        self.dma_ids = []

    def op(self, eng, fn, reads=(), writes=(), dma=False):
        deps = set()
        for b in reads:
            if b.w is not None:
                deps.add(b.w)
        for b in writes:
            if b.w is not None:
                deps.add(b.w)
            deps.update(b.r)
        i = len(self.ops)
        for b in reads:
            b.r.append(i)
        for b in writes:
            b.w = i
            b.r = []
        if eng in self.pending:
            deps.update(self.pending.pop(eng))
        self.ops.append(_Op(eng, fn, deps, dma))
        self.last_on[eng] = i
        if dma:
            self.dma_ids.append(i)
        return i

    def dma(self, queue, fn, reads=(), writes=()):
        return self.op(queue, fn, reads, writes, dma=True)

    def barrier(self):
        lasts = set(v for e, v in self.last_on.items() if e in COMPUTE)
        lasts.update(self.dma_ids[-self.n_dma_sems:])
        for e in ("pe", "act", "dve", "pool", "sp"):
            self.pending.setdefault(e, set()).update(lasts)

    def emit(self, final_wait_engine="sp"):
        nc, ops = self.nc, self.ops
        for o in ops:
            for d in o.deps:
                p = ops[d]
                if p.dma:
                    continue
                if p.eng == o.eng and not o.dma and (o.eng == "pe" or not self.same_engine_sync):
                    continue
                p.sig = True
        for e, i in self.last_on.items():
            if e in COMPUTE:
                ops[i].sig = True
        with contextlib.ExitStack() as st:
            esem = {e: st.enter_context(nc.semaphore("s_" + e)) for e in COMPUTE}
            dsems = [st.enter_context(nc.semaphore("d_%d" % k)) for k in range(self.n_dma_sems)]
            ecnt = {e: 0 for e in COMPUTE}
            dcnt = [0] * self.n_dma_sems
            rr = 0
            for o in ops:
                if o.dma:
                    o.sem = rr
                    dcnt[rr] += 16
                    o.cnt = dcnt[rr]
                    rr = (rr + 1) % self.n_dma_sems
                elif o.sig:
                    ecnt[o.eng] += 1
                    o.cnt = ecnt[o.eng]
            per_eng = {}
            for i, o in enumerate(ops):
                per_eng.setdefault(o.eng, []).append(i)
            block = st.enter_context(nc.Block())

            def run(engname, eng):
                seen = {}

                def wait(key, semh, val):
                    if seen.get(key, 0) >= val:
                        return
                    seen[key] = val
                    eng.wait_ge(semh, val)

                for i in per_eng.get(engname, []):
                    o = ops[i]
                    need = {}
                    for d in o.deps:
                        p = ops[d]
                        if p.dma:
                            key, semh = ("d", p.sem), dsems[p.sem]
                        else:
                            if p.eng == engname and not o.dma and (engname == "pe" or not self.same_engine_sync):
                                continue
                            key, semh = ("e", p.eng), esem[p.eng]
                        if key not in need or need[key][1] < p.cnt:
                            need[key] = (semh, p.cnt)
                    for key in sorted(need):
                        wait(key, need[key][0], need[key][1])
                    if o.dma:
                        if o.cnt > 16:
                            wait(("d", o.sem), dsems[o.sem], o.cnt - 16)
                        o.fn(eng).then_inc(dsems[o.sem], 16)
                    else:
                        ins = o.fn(eng)
                        if o.sig:
                            ins.then_inc(esem[o.eng], 1)
                if engname == final_wait_engine:
                    for e in COMPUTE:
                        if ecnt[e]:
                            wait(("e", e), esem[e], ecnt[e])
                    for k in range(self.n_dma_sems):
                        if dcnt[k]:
                            wait(("d", k), dsems[k], dcnt[k])

            @block.tensor
            def _(e):
                run("pe", e)

            @block.scalar
            def _(e):
                run("act", e)

            @block.vector
            def _(e):
                run("dve", e)

            @block.gpsimd
            def _(e):
                run("pool", e)

            @block.sync
            def _(e):
                run("sp", e)


PERM = np.r_[0:16, 32:48, 16:32, 48:64]
PERM_SW = PERM[(np.arange(64) + 32) % 64]


def _chunk_cols():
    cols = np.zeros((NF, 128), np.int64)
    for g in range(4):
        for kvh in range(2):
            cols[F_QA + g, kvh * 64:(kvh + 1) * 64] = OFF_QA + (kvh * 4 + g) * 64 + PERM
            cols[F_QAS + g, kvh * 64:(kvh + 1) * 64] = OFF_QA + (kvh * 4 + g) * 64 + PERM_SW
            cols[F_QB + g, kvh * 64:(kvh + 1) * 64] = OFF_QB + (kvh * 4 + g) * 64 + PERM
            cols[F_QBS + g, kvh * 64:(kvh + 1) * 64] = OFF_QB + (kvh * 4 + g) * 64 + PERM_SW
    for kvh in range(2):
        cols[F_KA, kvh * 64:(kvh + 1) * 64] = OFF_KA + kvh * 64 + PERM
        cols[F_KAS, kvh * 64:(kvh + 1) * 64] = OFF_KA + kvh * 64 + PERM_SW
        cols[F_KB, kvh * 64:(kvh + 1) * 64] = OFF_KB + kvh * 64 + PERM
        cols[F_KBS, kvh * 64:(kvh + 1) * 64] = OFF_KB + kvh * 64 + PERM_SW
    for j in range(16):
        cols[F_G + j] = OFF_GA + j * 128 + np.arange(128)
    return cols


def _rope_tables(pos, valid):
    n = pos.shape[0]
    rows = (pos // GRID_W).astype(np.float32)
    colsp = (pos % GRID_W).astype(np.float32)
    nfreq = HD // 4
    inv = (np.float32(10000.0) ** (-np.arange(nfreq, dtype=np.float32) / np.float32(nfreq))).astype(np.float32)
    ang_r = rows[None, :] * inv[:, None]
    ang_c = colsp[None, :] * inv[:, None]
    cos64 = np.zeros((64, n), np.float32)
    sin64 = np.zeros((64, n), np.float32)
    for half in range(2):
        sgn = -1.0 if half == 0 else 1.0
        cos64[half * 32:half * 32 + 16] = np.cos(ang_r)
        cos64[half * 32 + 16:half * 32 + 32] = np.cos(ang_c)
        sin64[half * 32:half * 32 + 16] = sgn * np.sin(ang_r)
        sin64[half * 32 + 16:half * 32 + 32] = sgn * np.sin(ang_c)
    cos64[:, ~valid] = 1.0
    sin64[:, ~valid] = 0.0
    return np.concatenate([cos64, cos64], 0), np.concatenate([sin64, sin64], 0)


def _shared_inputs(inp):
    f = np.float32
    w_in = np.asarray(inp["w_in"][0], f)
    b_in = np.asarray(inp["b_in"][0], f)
    cols = _chunk_cols()
    sh = {}
    sh["w_in_fm"] = np.ascontiguousarray(w_in[:, cols.reshape(-1)])
    sh["w_in_v"] = np.ascontiguousarray(np.concatenate([w_in[:, OFF_VA:OFF_VA + 128], w_in[:, OFF_VB:OFF_VB + 128]], 1))
    sh["bias_fm"] = np.ascontiguousarray(b_in[cols].T)
    gq = np.asarray(inp["q_norm_g"][0], f)
    gk = np.asarray(inp["k_norm_g"][0], f)
    cv = np.stack([np.tile(gq[PERM], 2), np.tile(gq[PERM_SW], 2), np.tile(gk[PERM], 2), np.tile(gk[PERM_SW], 2)], 1)
    sh["cvec"] = np.ascontiguousarray(cv.astype(f))
    vb = np.concatenate([b_in[OFF_VA:OFF_VA + 128], b_in[OFF_VB:OFF_VB + 128]])
    sh["vbias"] = np.ascontiguousarray(np.tile(vb[None, :], (128, 1)))
    p = np.arange(128)
    sh["blockones"] = (p[:, None] // 64 == p[None, :] // 64).astype(f)
    sh["ident"] = np.eye(128, dtype=f)
    sink = np.asarray(inp["attn_sink"][0], f)
    sr = np.zeros((128, 4, 128), f)
    for g in range(4):
        sr[0:64, g, :] = sink[4 + g]
        sr[64:128, g, :] = sink[g]
    sh["sinkraw"] = sr.reshape(128, 512)
    for nm, key in (("wba", "w_branch_a"), ("wbb", "w_branch_b")):
        w = np.asarray(inp[key][0], f).reshape(2, 4, 64, D)
        sh[nm] = np.ascontiguousarray(w.transpose(0, 2, 1, 3).reshape(128, 4 * D))
    sh["w_out"] = np.asarray(inp["w_out"][0], f)
    sh["w_mod"] = np.asarray(inp["w_mod"][0], f)
    sh["b_mod"] = np.asarray(inp["b_mod"], f).reshape(1, 6 * D)
    sh["lnp"] = np.ascontiguousarray(np.tile(np.stack([inp["ln1_g"][0], inp["ln1_b"][0], inp["ln2_g"][0],
                                                        inp["ln2_b"][0]], 0).astype(f).reshape(1, 4 * D), (128, 1)))
    sh["w_up"] = np.asarray(inp["w_up"][0], f)
    sh["w_down"] = np.asarray(inp["w_down"][0], f)
    cw = np.asarray(inp["conv_w"][0], f)
    sh["cw"] = np.ascontiguousarray(cw.reshape(3, 44, 128).transpose(2, 1, 0).reshape(128, 132))
    sh["cb"] = np.ascontiguousarray(np.asarray(inp["conv_b"][0], f).reshape(44, 128).T)
    return sh


def _core_inputs(inp, sh, core, S):
    f = np.float32
    NOWN = S // 4
    b, r = core // 4, core % 4
    start, end = r * NOWN, (r + 1) * NOWN
    x = np.asarray(inp["x"][b], f)
    NSLOT = S + 1024
    pos = np.full(NSLOT, -1, np.int64)
    pos[0:S] = (start + np.arange(S)) % S
    pos[S:S + 128] = start - 128 + np.arange(128)
    pos[S + 128:S + 256] = end + np.arange(128)
    pos[S + 256] = start - 129
    pos[S + 257] = end + 128
    pos[S + 384] = start - 1
    pos[S + 385] = end
    valid = (pos >= 0) & (pos < S)
    xs = np.zeros((NSLOT, D), f)
    xs[valid] = x[pos[valid]]
    xs[S + 512:S + 768] = np.asarray(inp["ctx"][b], f)
    d = dict(sh)
    d["xT"] = np.ascontiguousarray(xs.T.reshape(KC, 128, NSLOT))
    d["xtok"] = np.ascontiguousarray(np.concatenate([x[start:end], xs[S + 384:S + 512]], 0))
    cs, sn = _rope_tables(np.where(valid, pos, 0), valid)
    d["cs"], d["sn"] = np.ascontiguousarray(cs), np.ascontiguousarray(sn)
    cf = np.stack([np.asarray(inp["c"][b], f), np.asarray(inp["c_ctx"], f)], 1)
    d["c_fm"] = np.ascontiguousarray(cf.reshape(KC, 128, 2).transpose(1, 0, 2).reshape(128, 16))
    j = np.arange(128)[:, None]
    q = np.arange(128)[None, :]
    lv, rv = float(start > 0), float(end < S)
    m = np.zeros((9, 128, 128), f)
    m[0] = (j >= q)
    m[1] = (j >= q) * lv
    m[2] = (j <= q)
    m[3] = (j <= q) * rv
    m[4][0, 0] = lv
    m[4][1, 1] = rv
    m[5][:, 0] = lv
    m[6][:, 0] = 1.0
    m[7][:, 1] = 1.0
    m[8][:, 1] = rv
    d["masks"] = np.ascontiguousarray(m.transpose(1, 0, 2).reshape(128, 9 * 128))
    d["hm"] = np.ascontiguousarray(np.tile(np.array([[lv, rv]], f), (128, 1)))
    return d


IN_SHAPES = lambda S: {
    "xT": [KC, 128, S + 1024], "xtok": [S // 4 + 128, D], "cs": [128, S + 1024], "sn": [128, S + 1024],
    "c_fm": [128, 16], "w_mod": [D, 6 * D], "b_mod": [1, 6 * D], "w_in_fm": [D, NF * 128], "w_in_v": [D, 256],
    "bias_fm": [128, NF], "cvec": [128, 4], "vbias": [128, 256], "blockones": [128, 128], "ident": [128, 128],
    "masks": [128, 9 * 128], "sinkraw": [128, 512], "wba": [128, 4 * D], "wbb": [128, 4 * D], "w_out": [D, D],
    "lnp": [128, 4 * D], "w_up": [D, 2 * DFF], "w_down": [DFF, D], "cw": [128, 132], "cb": [128, 44], "hm": [128, 2],
}


def _bc_mid(ap, n):
    return bass.AP(ap.tensor, ap.offset, [list(ap.ap[0]), [0, n], list(ap.ap[1])])


DEBUG = False


def build_program(S):
    NOWN = S // 4
    NT = NOWN // 128
    NTQ = NT + 1
    NSLOT = S + 1024
    NCH = NSLOT // 512
    NCH_OWN = NOWN // 512
    NQ = NOWN + 512
    NKB = S // 128 + 2
    NA = NOWN + 1024
    NAB = NA // 128
    NBS = NSLOT // 128
    X1 = S
    X2 = S + 512
    nc = bass.Bass("TRN2", target_bir_lowering=False)
    shp = IN_SHAPES(S)
    din = {k: nc.dram_tensor(k, v, F32, kind="ExternalInput").ap() for k, v in shp.items()}
    out_d = nc.dram_tensor("out", [NOWN, D], F32, kind="ExternalOutput").ap()
    qa_d = nc.dram_tensor("qa_d", [4, 128, NQ], BF16, kind=("ExternalOutput" if DEBUG else "Internal")).ap()
    qb_d = nc.dram_tensor("qb_d", [4, 128, NQ], BF16, kind=("ExternalOutput" if DEBUG else "Internal")).ap()
    g_d = nc.dram_tensor("g_d", [16, 128, NQ], BF16, kind=("ExternalOutput" if DEBUG else "Internal")).ap()
    oa_d = nc.dram_tensor("oa_d", [NTQ, 128, 512], BF16, kind=("ExternalOutput" if DEBUG else "Internal")).ap()
    ob_d = nc.dram_tensor("ob_d", [NTQ, 128, 512], BF16, kind=("ExternalOutput" if DEBUG else "Internal")).ap()
    x1_d = nc.dram_tensor("x1_d", [NOWN + 128, D], F32, kind=("ExternalOutput" if DEBUG else "Internal")).ap()
    h2_d = nc.dram_tensor("h2_d", [KC, 128, NOWN + 2], BF16, kind=("ExternalOutput" if DEBUG else "Internal")).ap()
    gate_d = nc.dram_tensor("gate_d", [128, 2 * D], F32, kind=("ExternalOutput" if DEBUG else "Internal")).ap()

    S_ = Sched(nc)
    op, dma = S_.op, S_.dma
    PS = nc.alloc_psum_tensor("ps_all", [128, 4096], F32)

    def bank(i, n=1):
        return PS[:, i * 512:(i + n) * 512]

    PB = [Buf("bank%d" % i) for i in range(8)]

    class Arena:
        def __init__(self, base):
            self.off = base

        def alloc(self, name, cols, dt):
            nbytes = cols * (4 if dt == F32 else 2)
            nbytes = (nbytes + 31) // 32 * 32
            t = nc.alloc_sbuf_tensor_at(name, [128, cols], dt, offset=self.off)
            self.off += nbytes
            assert self.off <= 229376, (name, self.off)
            return t

    A0 = Arena(16384 + 256)
    modfm = A0.alloc("modfm", 64, F32)
    mod1p = A0.alloc("mod1p", 32, F32)
    bias_fm = A0.alloc("bias_fm_sb", NF, F32)
    cvec = A0.alloc("cvec_sb", 4, F32)
    ident = A0.alloc("ident_sb", 128, F32)
    hm = A0.alloc("hm_sb", 2, F32)
    epsb = A0.alloc("epsb", 2, F32)
    B_const = Buf("const")
    PERSIST = A0.off

    A = Arena(PERSIST)
    c_sb = A.alloc("c_sb", 16, F32)
    sc = A.alloc("sc", 16, F32)
    ones_t = A.alloc("ones_t", 128, F32)
    screp = A.alloc("screp", 2 * KC * 128, F32)
    bmod = A.alloc("bmod_sb", 6 * D, F32)
    modbc = A.alloc("modbc", 6 * D, F32)
    modcc = A.alloc("modcc", 2 * D, F32)
    tmpd = A.alloc("tmpd", 6 * D, F32)
    wst = [A.alloc("wst%d" % i, KC * 512, F32) for i in range(2)]
    b_c, b_sc, b_ones, b_screp, b_bmod, b_modbc, b_modcc, b_tmpd = [Buf(n) for n in "c sc ones screp bmod modbc modcc tmpd".split()]
    b_wst = [Buf("wst0"), Buf("wst1")]

    dma("sp", lambda e: e.dma_start(out=c_sb[:], in_=din["c_fm"]), writes=[b_c])
    dma("sp", lambda e: e.dma_start(out=bias_fm[:], in_=din["bias_fm"]), writes=[B_const])
    dma("sp", lambda e: e.dma_start(out=cvec[:], in_=din["cvec"]), writes=[B_const])
    dma("sp", lambda e: e.dma_start(out=ident[:], in_=din["ident"]), writes=[B_const])
    dma("sp", lambda e: e.dma_start(out=hm[:], in_=din["hm"]), writes=[B_const])
    dma("sp", lambda e: e.dma_start(out=bmod[0:1, :], in_=din["b_mod"]), writes=[b_bmod])
    op("dve", lambda e: e.memset(ones_t[:], 1.0), writes=[b_ones])

    def _eps(e):
        e.memset(epsb[:, 0:1], LN_EPS)
        return e.memset(epsb[:, 1:2], QK_EPS)
    op("dve", _eps, writes=[B_const])
    op("act", lambda e: e.activation(sc[:], c_sb[:], AF.Silu), reads=[b_c], writes=[b_sc])
    for j in range(2):
        for kc in range(KC):
            op("dve", lambda e, j=j, kc=kc: e.tensor_scalar(screp[:, (j * KC + kc) * 128:(j * KC + kc + 1) * 128], ones_t[:],
                                                           sc[:, kc * 2 + j:kc * 2 + j + 1], None, ALU.mult),
               reads=[b_ones, b_sc], writes=[b_screp])
    wmod_v = din["w_mod"].rearrange("(kc p) n -> p kc n", p=128)
    jobs = [(0, cj) for cj in range(12)] + [(1, cj) for cj in range(4)]
    for ji, (who, cj) in enumerate(jobs):
        wb = ji % 2
        dma("sp", lambda e, wb=wb, cj=cj: e.dma_start(out=wst[wb][:, :].rearrange("p (kc n) -> p kc n", kc=KC),
                                                       in_=wmod_v[:, :, cj * 512:(cj + 1) * 512]), writes=[b_wst[wb]])
        pb = ji % 2

        def _mm(e, who=who, cj=cj, wb=wb, pb=pb):
            for kc in range(KC):
                e.matmul(bank(pb), screp[:, (who * KC + kc) * 128:(who * KC + kc + 1) * 128],
                         wst[wb][:, kc * 512:(kc + 1) * 512], start=(kc == 0), stop=False)
            return e.matmul(bank(pb), ones_t[0:1, :], bmod[0:1, cj * 512:(cj + 1) * 512], start=False, stop=True)
        op("pe", _mm, reads=[b_screp, b_wst[wb], b_bmod, b_ones], writes=[PB[pb]])
        dst = (modbc if who == 0 else modcc)
        op("act", lambda e, dst=dst, cj=cj, pb=pb: e.activation(dst[:, cj * 512:(cj + 1) * 512], bank(pb), AF.Copy),
           writes=[PB[pb], b_modbc if who == 0 else b_modcc])
    op("dve", lambda e: e.tensor_tensor(tmpd[:, :].rearrange("p (c f) -> p c f", f=128),
                                        modbc[:, :].rearrange("p (c f) -> p c f", f=128), _bc_mid(ident[:, :], 48), ALU.mult),
       reads=[b_modbc, B_const], writes=[b_tmpd])
    op("dve", lambda e: e.tensor_reduce(modfm[:, 0:48], tmpd[:, :].rearrange("p (c f) -> p c f", f=128), AX.X, ALU.add),
       reads=[b_tmpd], writes=[B_const])
    op("dve", lambda e: e.tensor_tensor(tmpd[:, 0:2 * D].rearrange("p (c f) -> p c f", f=128),
                                        modcc[:, :].rearrange("p (c f) -> p c f", f=128), _bc_mid(ident[:, :], 16), ALU.mult),
       reads=[b_modcc, B_const], writes=[b_tmpd])
    op("dve", lambda e: e.tensor_reduce(modfm[:, 48:64], tmpd[:, 0:2 * D].rearrange("p (c f) -> p c f", f=128), AX.X, ALU.add),
       reads=[b_tmpd], writes=[B_const])

    def _m1p(e):
        e.tensor_scalar(mod1p[:, 0:8], modfm[:, 8:16], 1.0, None, ALU.add)
        e.tensor_scalar(mod1p[:, 8:16], modfm[:, 32:40], 1.0, None, ALU.add)
        return e.tensor_scalar(mod1p[:, 16:24], modfm[:, 56:64], 1.0, None, ALU.add)
    op("dve", _m1p, writes=[B_const])
    dma("sp", lambda e: e.dma_start(out=gate_d[:, 0:D], in_=modbc[:, 2 * D:3 * D]), reads=[b_modbc])
    dma("sp", lambda e: e.dma_start(out=gate_d[:, D:2 * D], in_=modbc[:, 5 * D:6 * D]), reads=[b_modbc])
    S_.barrier()

    AK = Arena(PERSIST)
    KbT = AK.alloc("KbT", NSLOT, BF16)
    Vb = AK.alloc("Vb", NBS * 192, BF16)
    KaT = AK.alloc("KaT", NA, BF16)
    Va = AK.alloc("Va", NAB * 192, BF16)
    KV_END = AK.off
    b_kv = Buf("kv")

    def _ones_init(e):
        e.memset(Vb[:, :].rearrange("p (b c) -> p b c", c=192)[:, :, 64:128], 1.0)
        return e.memset(Va[:, :].rearrange("p (b c) -> p b c", c=192)[:, :, 64:128], 1.0)
    op("pool", _ones_init, writes=[b_kv])

    w_fm_v = din["w_in_fm"].rearrange("(kc p) n -> p kc n", p=128)
    w_v_v = din["w_in_v"].rearrange("(kc p) n -> p kc n", p=128)

    def run_pass(pname, chunks, feats, vjob, bigx=False):
        A = Arena(KV_END)
        nfe = sorted(set(sum([[j["f"]] + ([j["fs"]] if "fs" in j else []) for j in feats], [])))
        fidx = {f: i for i, f in enumerate(nfe)}
        W = A.alloc(pname + "W", len(nfe) * KC * 128, BF16)
        Wv = A.alloc(pname + "Wv", KC * 256, BF16)
        vbias = A.alloc(pname + "vb", 256, F32)
        bones = A.alloc(pname + "bo", 128, F32)
        xs = [A.alloc(pname + "xs%d" % i, 512, F32) for i in range(3)]
        xsb = [A.alloc(pname + "xsb%d" % i, KC * 512, F32) for i in range(2)] if bigx else None
        b_xsb = [Buf("xsb0"), Buf("xsb1")]
        hT = [A.alloc(pname + "hT%d" % i, KC * 512, BF16) for i in range(2)]
        cs_t = A.alloc(pname + "cs", 512, F32)
        sn_t = A.alloc(pname + "sn", 512, F32)
        tt = [A.alloc(pname + "t%d" % i, 512, F32) for i in range(6)]
        ob = [A.alloc(pname + "ob%d" % i, 512, BF16) for i in range(3)]
        b_W, b_cs = Buf("W"), Buf("cs")
        b_xs = [Buf("xs") for _ in xs]
        b_hT = [Buf("hT") for _ in hT]
        b_tt = [Buf("t") for _ in tt]
        b_ob = [Buf("ob") for _ in ob]
        wsg = [A.alloc(pname + "wsg%d" % i, 1024, F32) for i in range(2)]
        b_wsg = [Buf("wsg") for _ in wsg]
        for n_, f in enumerate(nfe):
            w_ = n_ % 2
            dma("sp", lambda e, f=f, w_=w_: e.dma_start(out=wsg[w_][:, :].rearrange("p (kc c) -> p kc c", kc=KC),
                                                        in_=w_fm_v[:, :, f * 128:(f + 1) * 128]), writes=[b_wsg[w_]])
            op("dve", lambda e, f=f, w_=w_: e.tensor_copy(W[:, fidx[f] * KC * 128:(fidx[f] + 1) * KC * 128], wsg[w_][:]),
               reads=[b_wsg[w_]], writes=[b_W])
        if vjob:
            for hv in range(2):
                dma("sp", lambda e, hv=hv: e.dma_start(out=wsg[hv][:, :].rearrange("p (kc c) -> p kc c", kc=4),
                                                       in_=w_v_v[:, hv * 4:(hv + 1) * 4, :]), writes=[b_wsg[hv]])
                op("dve", lambda e, hv=hv: e.tensor_copy(Wv[:, hv * 1024:(hv + 1) * 1024], wsg[hv][:]),
                   reads=[b_wsg[hv]], writes=[b_W])
            dma("sp", lambda e: e.dma_start(out=vbias[:], in_=din["vbias"]), writes=[b_W])
        dma("sp", lambda e: e.dma_start(out=bones[:], in_=din["blockones"]), writes=[b_W])
        obi = 0
        pbi = 0
        for ci, c in enumerate(chunks):
            s0 = c * 512
            is_ctx = (s0 == X2)
            hb = ci % 2
            if bigx:
                dma("sp", lambda e, hb=hb, s0=s0: e.dma_start(out=xsb[hb][:, :].rearrange("p (kc s) -> p kc s", kc=KC),
                                                               in_=din["xT"][:, :, s0:s0 + 512].rearrange("kc p s -> p kc s")), writes=[b_xsb[hb]])
            for kc in range(KC):
                xi = (ci * KC + kc) % 3
                if bigx:
                    sc_ap = mod1p[:, 16 + kc:17 + kc] if is_ctx else mod1p[:, kc:kc + 1]
                    sh_ap = modfm[:, 48 + kc:49 + kc] if is_ctx else modfm[:, kc:kc + 1]
                    op("dve", lambda e, kc=kc, hb=hb, sc_ap=sc_ap, sh_ap=sh_ap: e.tensor_scalar(
                        hT[hb][:, kc * 512:(kc + 1) * 512], xsb[hb][:, kc * 512:(kc + 1) * 512], sc_ap, sh_ap, ALU.mult, ALU.add),
                       reads=[b_xsb[hb], B_const], writes=[b_hT[hb]])
                    continue
                dma("sp", lambda e, xi=xi, kc=kc, s0=s0: e.dma_start(out=xs[xi][:], in_=din["xT"][kc, :, s0:s0 + 512]),
                    writes=[b_xs[xi]])
                sc_ap = mod1p[:, 16 + kc:17 + kc] if is_ctx else mod1p[:, kc:kc + 1]
                sh_ap = modfm[:, 48 + kc:49 + kc] if is_ctx else modfm[:, kc:kc + 1]
                eng = "dve"
                op(eng, lambda e, xi=xi, kc=kc, hb=hb, sc_ap=sc_ap, sh_ap=sh_ap: e.tensor_scalar(
                    hT[hb][:, kc * 512:(kc + 1) * 512], xs[xi][:], sc_ap, sh_ap, ALU.mult, ALU.add),
                   reads=[b_xs[xi], B_const], writes=[b_hT[hb]])
            need_rope = any(j["kind"] == "rope" for j in feats)
            if need_rope:
                dma("sp", lambda e, s0=s0: e.dma_start(out=cs_t[:], in_=din["cs"][:, s0:s0 + 512]), writes=[b_cs])
                dma("sp", lambda e, s0=s0: e.dma_start(out=sn_t[:], in_=din["sn"][:, s0:s0 + 512]), writes=[b_cs])
            for j in feats:
                f = j["f"]

                def mm(e, f, pb, hb=hb):
                    for kc in range(KC):
                        r = e.matmul(bank(pb), W[:, (fidx[f] * KC + kc) * 128:(fidx[f] * KC + kc + 1) * 128],
                                     hT[hb][:, kc * 512:(kc + 1) * 512], start=(kc == 0), stop=(kc == KC - 1))
                    return r
                if j["kind"] == "gate":
                    pb = pbi % 4
                    pbi += 1
                    op("pe", lambda e, f=f, pb=pb, mm=mm: mm(e, f, pb), reads=[b_W, b_hT[hb]], writes=[PB[pb]])
                    o = obi % 3
                    obi += 1
                    op("act", lambda e, f=f, pb=pb, o=o: e.activation(ob[o][:], bank(pb), AF.Sigmoid, bias=bias_fm[:, f:f + 1]),
                       reads=[B_const], writes=[PB[pb], b_ob[o]])
                    dst = j["dst"](s0)
                    dma("sp", lambda e, o=o, dst=dst: e.dma_start(out=dst, in_=ob[o][:]), reads=[b_ob[o]])
                    continue
                fs = j["fs"]
                pa = pbi % 4
                pbi += 1
                pbs = pbi % 4
                pbi += 1
                op("pe", lambda e, f=f, pa=pa, mm=mm: mm(e, f, pa), reads=[b_W, b_hT[hb]], writes=[PB[pa]])
                op("pe", lambda e, fs=fs, pbs=pbs, mm=mm: mm(e, fs, pbs), reads=[b_W, b_hT[hb]], writes=[PB[pbs]])
                if j.get("dst_sb") is not None:
                    dst_ap, dst_buf = j["dst_sb"](s0), b_kv
                    o = None
                else:
                    o = obi % 3
                    obi += 1
                    dst_ap, dst_buf = ob[o][:], b_ob[o]
                if not j["norm"]:
                    op("dve", lambda e, f=f, pa=pa: e.scalar_tensor_tensor(tt[0][:], bank(pa), bias_fm[:, f:f + 1], cs_t[:], ALU.add, ALU.mult),
                       reads=[b_cs, B_const], writes=[PB[pa], b_tt[0]])
                    op("dve", lambda e, fs=fs, pbs=pbs: e.scalar_tensor_tensor(tt[1][:], bank(pbs), bias_fm[:, fs:fs + 1], sn_t[:], ALU.add, ALU.mult),
                       reads=[b_cs, B_const], writes=[PB[pbs], b_tt[1]])
                    op("pool", lambda e, dst_ap=dst_ap: e.tensor_tensor(dst_ap, tt[0][:], tt[1][:], ALU.add),
                       reads=[b_tt[0], b_tt[1]], writes=[dst_buf])
                else:
                    gi = j["g"]
                    op("act", lambda e, f=f, pa=pa: e.activation(tt[2][:], bank(pa), AF.Square, bias=bias_fm[:, f:f + 1]),
                       reads=[B_const], writes=[PB[pa], b_tt[2]])
                    op("dve", lambda e, f=f, pa=pa, gi=gi: e.tensor_scalar(tt[0][:], bank(pa), bias_fm[:, f:f + 1], cvec[:, gi:gi + 1], ALU.add, ALU.mult),
                       reads=[B_const], writes=[PB[pa], b_tt[0]])
                    op("dve", lambda e, fs=fs, pbs=pbs, gi=gi: e.tensor_scalar(tt[1][:], bank(pbs), bias_fm[:, fs:fs + 1], cvec[:, gi + 1:gi + 2], ALU.add, ALU.mult),
                       reads=[B_const], writes=[PB[pbs], b_tt[1]])
                    op("pe", lambda e: e.matmul(bank(4), bones[:], tt[2][:], start=True, stop=True), reads=[b_W, b_tt[2]], writes=[PB[4]])
                    op("pool", lambda e: e.tensor_tensor(tt[0][:], tt[0][:], cs_t[:], ALU.mult), reads=[b_cs], writes=[b_tt[0]])
                    op("pool", lambda e: e.tensor_tensor(tt[1][:], tt[1][:], sn_t[:], ALU.mult), reads=[b_cs], writes=[b_tt[1]])
                    op("pool", lambda e: e.tensor_tensor(tt[0][:], tt[0][:], tt[1][:], ALU.add), reads=[b_tt[1]], writes=[b_tt[0]])
                    op("act", lambda e: e.activation(tt[3][:], bank(4), AF.Sqrt, bias=epsb[:, 1:2], scale=1.0 / 64),
                       reads=[B_const], writes=[PB[4], b_tt[3]])
                    op("dve", lambda e: e.reciprocal(tt[3][:], tt[3][:]), writes=[b_tt[3]])
                    op("dve", lambda e, dst_ap=dst_ap: e.tensor_tensor(dst_ap, tt[0][:], tt[3][:], ALU.mult),
                       reads=[b_tt[0], b_tt[3]], writes=[dst_buf])
                if o is not None:
                    dst = j["dst"](s0)
                    dma("sp", lambda e, o=o, dst=dst: e.dma_start(out=dst, in_=ob[o][:]), reads=[b_ob[o]])
            if vjob:
                for blk in range(4):
                    pv = 5 + (blk % 2)
                    ncol = 256 if vjob == "ab" else 128
                    c0 = 0 if vjob == "ab" else 128

                    def mmv(e, blk=blk, pv=pv, ncol=ncol, c0=c0, hb=hb):
                        for kc in range(KC):
                            r = e.matmul(bank(pv)[:, 0:ncol], hT[hb][:, kc * 512 + blk * 128:kc * 512 + (blk + 1) * 128],
                                         Wv[:, kc * 256 + c0:kc * 256 + c0 + ncol], start=(kc == 0), stop=(kc == KC - 1))
                        return r
                    op("pe", mmv, reads=[b_W, b_hT[hb]], writes=[PB[pv]])
                    sblk = s0 // 128 + blk
                    if vjob == "ab":
                        ablk = (sblk if s0 < NOWN else NOWN // 128 + (s0 - S) // 128 + blk)
                        op("dve", lambda e, pv=pv, ablk=ablk: e.tensor_tensor(
                            Va[:, ablk * 192:(ablk + 1) * 192].rearrange("p (a c) -> p a c", c=64)[:, 0:3:2, :],
                            bank(pv)[:, 0:128].rearrange("p (a c) -> p a c", c=64),
                            vbias[:, 0:128].rearrange("p (a c) -> p a c", c=64), ALU.add),
                           reads=[b_W], writes=[PB[pv], b_kv])
                    op("dve", lambda e, pv=pv, sblk=sblk, ncol=ncol: e.tensor_tensor(
                        Vb[:, sblk * 192:(sblk + 1) * 192].rearrange("p (a c) -> p a c", c=64)[:, 0:3:2, :],
                        bank(pv)[:, ncol - 128:ncol].rearrange("p (a c) -> p a c", c=64),
                        vbias[:, 128:256].rearrange("p (a c) -> p a c", c=64), ALU.add),
                       reads=[b_W], writes=[PB[pv], b_kv])
        S_.barrier()

    def acol(s0):
        return s0 if s0 < NOWN else NOWN + (s0 - S)

    own_chunks = list(range(NCH_OWN))
    x1c, x2c = X1 // 512, X2 // 512
    run_pass("pA", list(range(NCH)),
             [dict(kind="rope", f=F_KB, fs=F_KBS, norm=True, g=2, dst_sb=lambda s0: KbT[:, s0:s0 + 512])], "b", bigx=True)
    featsB = [dict(kind="rope", f=F_KA, fs=F_KAS, norm=False, dst_sb=lambda s0: KaT[:, acol(s0):acol(s0) + 512])]
    for g in range(4):
        featsB.append(dict(kind="rope", f=F_QA + g, fs=F_QAS + g, norm=False, dst_sb=None,
                           dst=lambda s0, g=g: qa_d[g, :, acol(s0):acol(s0) + 512]))
        featsB.append(dict(kind="rope", f=F_QB + g, fs=F_QBS + g, norm=True, g=0, dst_sb=None,
                           dst=lambda s0, g=g: qb_d[g, :, acol(s0):acol(s0) + 512]))
    run_pass("pB", own_chunks + [x1c], featsB, "ab")
    run_pass("pB2", [x2c], featsB[:1], "ab")
    featsC = [dict(kind="gate", f=F_G + j, dst=lambda s0, j=j: g_d[j, :, acol(s0):acol(s0) + 512]) for j in range(16)]
    run_pass("pC", own_chunks + [x1c], featsC, None)

    if DEBUG:
        dbg_mod = nc.dram_tensor("dbg_mod", [128, 96], F32, kind="ExternalOutput").ap()
        dbg_kb = nc.dram_tensor("dbg_kb", [128, NSLOT], BF16, kind="ExternalOutput").ap()
        dbg_ka = nc.dram_tensor("dbg_ka", [128, NA], BF16, kind="ExternalOutput").ap()
        dbg_vb = nc.dram_tensor("dbg_vb", [128, NBS * 192], BF16, kind="ExternalOutput").ap()
        dbg_va = nc.dram_tensor("dbg_va", [128, NAB * 192], BF16, kind="ExternalOutput").ap()
        dma("sp", lambda e: e.dma_start(out=dbg_mod[:, 0:64], in_=modfm[:]), reads=[B_const])
        dma("sp", lambda e: e.dma_start(out=dbg_mod[:, 64:96], in_=mod1p[:]), reads=[B_const])
        dma("sp", lambda e: e.dma_start(out=dbg_kb, in_=KbT[:]), reads=[b_kv])
        dma("sp", lambda e: e.dma_start(out=dbg_ka, in_=KaT[:]), reads=[b_kv])
        dma("sp", lambda e: e.dma_start(out=dbg_vb, in_=Vb[:]), reads=[b_kv])
        dma("sp", lambda e: e.dma_start(out=dbg_va, in_=Va[:]), reads=[b_kv])
        S_.barrier()
    TOP = 229376
    HIGH0 = TOP - 55296
    WUP0 = HIGH0 - KC * 2 * DFF * 2
    AH = Arena(HIGH0)
    Wa = AH.alloc("Wa", 4 * D, BF16)
    Wb = AH.alloc("Wb", 4 * D, BF16)
    Wo = AH.alloc("Wo", KC * D, BF16)
    lnp = AH.alloc("lnp_sb", 2 * D, F32)
    stgH = [AH.alloc("stgH%d" % i, D, F32) for i in range(2)]
    gbcH = AH.alloc("gbcH", D, F32)
    Wup = nc.alloc_sbuf_tensor_at("Wup", [128, KC * 2 * DFF], BF16, offset=WUP0)
    Wdn = nc.alloc_sbuf_tensor_at("Wdn", [128, NFC * D], BF16, offset=HIGH0)
    b_w2, b_gbcH, b_w3 = Buf("w2"), Buf("gbcH"), Buf("w3")
    b_stgH = [Buf("stgH0"), Buf("stgH1")]
    stg_cnt = [0]

    def stage_job(src_ap, dst_ap, ncols, mul, wbuf, eng="pool"):
        def job():
            w = stg_cnt[0] % 2
            stg_cnt[0] += 1
            dma("sp", lambda e: e.dma_start(out=stgH[w][:, 0:ncols], in_=src_ap), writes=[b_stgH[w]])
            if mul:
                op("pool", lambda e: e.tensor_tensor(dst_ap, stgH[w][:, 0:ncols], gbcH[:, 0:ncols], ALU.mult),
                   reads=[b_stgH[w], b_gbcH], writes=[wbuf])
            elif eng == "act":
                op("act", lambda e: e.activation(dst_ap, stgH[w][:, 0:ncols], AF.Copy), reads=[b_stgH[w]], writes=[wbuf])
            else:
                op("pool", lambda e: e.tensor_copy(dst_ap, stgH[w][:, 0:ncols]), reads=[b_stgH[w]], writes=[wbuf])
        return job

    wout_v = din["w_out"].rearrange("(kc p) n -> p kc n", p=128)
    wup_v = din["w_up"].rearrange("(kc p) n -> p kc n", p=128)
    pf2 = [lambda: dma("sp", lambda e: e.dma_start(out=lnp[:], in_=din["lnp"][:, 0:2 * D]), writes=[b_w2]),
           lambda: dma("sp", lambda e: e.dma_start(out=gbcH[:], in_=gate_d[:, 0:D]), writes=[b_gbcH])]
    for g in range(4):
        pf2.append(stage_job(din["wba"][:, g * D:(g + 1) * D], Wa[:, g * D:(g + 1) * D], D, False, b_w2))
        pf2.append(stage_job(din["wbb"][:, g * D:(g + 1) * D], Wb[:, g * D:(g + 1) * D], D, False, b_w2))
    for kc in range(KC):
        pf2.append(stage_job(wout_v[:, kc, :], Wo[:, kc * D:(kc + 1) * D], D, True, b_w2))
    pf3 = []
    for kc in range(KC):
        for c0 in range(0, 2 * DFF, 1024):
            c1 = min(c0 + 1024, 2 * DFF)
            pf3.append(stage_job(wup_v[:, kc, c0:c1], Wup[:, kc * 2 * DFF + c0:kc * 2 * DFF + c1], c1 - c0, False, b_w3, "act"))

    def run_prefetch(jobs, tiles_left):
        k = -(-len(jobs) // max(tiles_left, 1))
        for _ in range(k):
            if jobs:
                jobs.pop(0)()

    A = Arena(KV_END)
    masks = A.alloc("masks_sb", 9 * 128, BF16)
    sinkexp = A.alloc("sinkexp", 512, F32)
    qa_t = [A.alloc("qa_t%d" % i, 512, BF16) for i in range(2)]
    qb_t = [A.alloc("qb_t%d" % i, 512, BF16) for i in range(2)]
    pt = [A.alloc("pt%d" % i, 1024, BF16) for i in range(3)]
    rs = A.alloc("rs", 512, F32)
    rsh = A.alloc("rsh", 512, F32)
    oT = [A.alloc("oT%d" % i, 512, BF16) for i in range(4)]
    masks32 = A.alloc("masks32", 9 * 128, F32)
    assert A.off <= HIGH0, A.off
    b_q = [Buf("q0"), Buf("q1")]
    b_pt = [Buf("pt") for _ in pt]
    b_rs, b_rsh = Buf("rs"), Buf("rsh")
    b_oT = [Buf("oT") for _ in oT]
    b_m = Buf("m")
    b_m32 = Buf("m32")
    dma("sp", lambda e: e.dma_start(out=masks32[:], in_=din["masks"]), writes=[b_m32])
    op("dve", lambda e: e.tensor_copy(masks[:], masks32[:]), reads=[b_m32], writes=[b_m])
    dma("sp", lambda e: e.dma_start(out=sinkexp[:], in_=din["sinkraw"]), writes=[b_m])
    op("act", lambda e: e.activation(sinkexp[:], sinkexp[:], AF.Exp), writes=[b_m])
    SB = [0, 2]
    ACC = {"a": (4, 5), "b": (6, 7)}
    ablk0 = NOWN // 128
    ucount = [0]
    oi = [0]
    pending_fin = []

    def flush_fin():
        while pending_fin:
            pending_fin.pop(0)()

    def attention(qi, br, qtile, units, on_done):
        a0, a1 = ACC[br]
        n = len(units)

        def s_op(u):
            KT, c0, V, vb, mk = units[u]
            sb = SB[(ucount[0] + u) % 2]

            def f(e):
                e.matmul(bank(sb), KT[0:64, c0:c0 + 128], qtile[0:64, :], start=True, stop=True)
                return e.matmul(bank(sb + 1), KT[64:128, c0:c0 + 128], qtile[64:128, :], start=True, stop=True)
            op("pe", f, reads=[b_kv, b_q[qi]], writes=[PB[sb], PB[sb + 1]])

        s_op(0)
        for u in range(n):
            KT, c0, V, vb, mk = units[u]
            if u + 1 < n:
                s_op(u + 1)
            sb = SB[(ucount[0] + u) % 2]
            pi = (ucount[0] + u) % 3
            op("act", lambda e, sb=sb, pi=pi: e.activation(pt[pi][:], bank(sb, 2), AF.Exp, scale=0.125),
               writes=[PB[sb], PB[sb + 1], b_pt[pi]])
            if mk is not None:
                op("dve", lambda e, pi=pi, mk=mk: e.tensor_tensor(
                    pt[pi][:, :].rearrange("p (h q) -> p h q", q=128), pt[pi][:, :].rearrange("p (h q) -> p h q", q=128),
                    _bc_mid(masks[:, mk * 128:(mk + 1) * 128], 8), ALU.mult), reads=[b_m], writes=[b_pt[pi]])

            def pv(e, V=V, vb=vb, pi=pi, u=u):
                e.matmul(bank(a0), V[:, vb * 192:vb * 192 + 128], pt[pi][:, 0:512], start=(u == 0), stop=(u == n - 1))
                return e.matmul(bank(a1), V[:, vb * 192 + 64:vb * 192 + 192], pt[pi][:, 512:1024], start=(u == 0), stop=(u == n - 1))
            op("pe", pv, reads=[b_kv, b_pt[pi]], writes=[PB[a0], PB[a1]])
            if u == 2:
                flush_fin()
        ucount[0] += n

        def finalize():
            if br == "a":
                def f(e):
                    e.tensor_tensor(rs[64:128, :], bank(a0)[64:128, :], sinkexp[64:128, :], ALU.add)
                    return e.tensor_tensor(rs[0:64, :], bank(a1)[0:64, :], sinkexp[0:64, :], ALU.add)
            else:
                def f(e):
                    e.tensor_copy(rs[64:128, :], bank(a0)[64:128, :])
                    return e.tensor_copy(rs[0:64, :], bank(a1)[0:64, :])
            op("dve", f, reads=[b_m], writes=[PB[a0], PB[a1], b_rs])
            op("dve", lambda e: e.reciprocal(rs[:], rs[:]), writes=[b_rs])

            def f2(e):
                e.activation(rsh[0:64, :], rs[64:128, :], AF.Copy)
                return e.activation(rsh[64:128, :], rs[0:64, :], AF.Copy)
            op("act", f2, reads=[b_rs], writes=[b_rsh])
            o = oi[0] % 4
            oi[0] += 1

            def f3(e):
                e.tensor_tensor(oT[o][0:64, :], bank(a0)[0:64, :], rsh[0:64, :], ALU.mult)
                return e.tensor_tensor(oT[o][64:128, :], bank(a1)[64:128, :], rsh[64:128, :], ALU.mult)
            op("dve", f3, reads=[b_rsh], writes=[PB[a0], PB[a1], b_oT[o]])
            on_done(o)
        pending_fin.append(finalize)

    for t in range(NTQ):
        qi = t % 2
        qc = t * 128 if t < NT else NOWN + 384
        dma("sp", lambda e, qi=qi, qc=qc: e.dma_start(out=qa_t[qi][:, :].rearrange("p (g q) -> p g q", g=4),
                                                       in_=qa_d[:, :, qc:qc + 128].rearrange("g p q -> p g q")), writes=[b_q[qi]])
        dma("sp", lambda e, qi=qi, qc=qc: e.dma_start(out=qb_t[qi][:, :].rearrange("p (g q) -> p g q", g=4),
                                                       in_=qb_d[:, :, qc:qc + 128].rearrange("g p q -> p g q")), writes=[b_q[qi]])
        ctxu = [(KaT, (ablk0 + 4) * 128, Va, ablk0 + 4, None), (KaT, (ablk0 + 5) * 128, Va, ablk0 + 5, None)]

        def ka(blk, mk):
            return (KaT, blk * 128, Va, blk, mk)
        if t < NT:
            left = ka(t - 1, 0) if t > 0 else ka(ablk0 + 0, 1)
            right = ka(t + 1, 2) if t < NT - 1 else ka(ablk0 + 1, 3)
            ua = [left, ka(t, None), right] + ctxu
        else:
            ua = [ka(ablk0 + 2, 4), ka(ablk0 + 0, 5), ka(0, 6), ka(NT - 1, 7), ka(ablk0 + 1, 8)] + ctxu
        attention(qi, "a", qa_t[qi], ua,
                  lambda o, t=t: dma("sp", lambda e: e.dma_start(out=oa_d[t], in_=oT[o][:]), reads=[b_oT[o]]))
        ub = [(KbT, kb * 128, Vb, kb, None) for kb in range(S // 128)]
        ub += [(KbT, X2 + kb * 128, Vb, X2 // 128 + kb, None) for kb in range(2)]
        attention(qi, "b", qb_t[qi], ub,
                  lambda o, t=t: dma("sp", lambda e: e.dma_start(out=ob_d[t], in_=oT[o][:]), reads=[b_oT[o]]))
        run_prefetch(pf2, NTQ - t)
    flush_fin()
    while pf2:
        pf2.pop(0)()
    S_.barrier()

    def layer_norm(z, b_z, st, b_st, g_ap, b_ap, dst, b_dst, b_par):
        op("dve", lambda e: e.memset(st[:, 0:2], 0.0), writes=[b_st])
        op("act", lambda e: e.activation(dst[:], z[:], AF.Identity, accum_out=st[:, 0:1]), reads=[b_z], writes=[b_dst, b_st])
        op("act", lambda e: e.activation(dst[:], z[:], AF.Square, accum_out=st[:, 1:2]), reads=[b_z], writes=[b_dst, b_st])
        op("dve", lambda e: e.tensor_scalar(st[:, 2:3], st[:, 0:1], 1.0 / D, None, ALU.mult), writes=[b_st])
        op("dve", lambda e: e.tensor_tensor(st[:, 3:4], st[:, 2:3], st[:, 2:3], ALU.mult), writes=[b_st])
        op("dve", lambda e: e.scalar_tensor_tensor(st[:, 4:5], st[:, 1:2], 1.0 / D, st[:, 3:4], ALU.mult, ALU.subtract), writes=[b_st])
        op("act", lambda e: e.activation(st[:, 5:6], st[:, 4:5], AF.Sqrt, bias=epsb[:, 0:1]), reads=[B_const], writes=[b_st])
        op("dve", lambda e: e.reciprocal(st[:, 5:6], st[:, 5:6]), writes=[b_st])
        op("dve", lambda e: e.scalar_tensor_tensor(dst[:], z[:], st[:, 2:3], g_ap, ALU.subtract, ALU.mult),
           reads=[b_z, b_par], writes=[b_dst])
        op("dve", lambda e: e.scalar_tensor_tensor(dst[:], dst[:], st[:, 5:6], b_ap, ALU.mult, ALU.add),
           reads=[b_par, b_st], writes=[b_dst])

    A = Arena(PERSIST)
    oa_t = [A.alloc("oa_t%d" % i, 512, BF16) for i in range(2)]
    ob_t = [A.alloc("ob_t%d" % i, 512, BF16) for i in range(2)]
    g_t = [A.alloc("g_t%d" % i, 16 * 128, BF16) for i in range(2)]
    x_t = [A.alloc("x_t%d" % i, D, F32) for i in range(2)]
    mt1 = [A.alloc("mt1_%d" % i, 512, F32) for i in range(2)]
    mt2 = [A.alloc("mt2_%d" % i, 512, F32) for i in range(2)]
    mT = [A.alloc("mT%d" % i, KC * 128, BF16) for i in range(2)]
    z_t = [A.alloc("z_t%d" % i, D, F32) for i in range(2)]
    st = [A.alloc("st%d" % i, 8, F32) for i in range(2)]
    x1_t = [A.alloc("x1_t%d" % i, D, F32) for i in range(2)]
    h2s = [A.alloc("h2s%d" % i, KC * 128, BF16) for i in range(2)]
    assert A.off <= WUP0, A.off
    b_in2 = [Buf("in2") for _ in range(2)]
    b_xt = [Buf("xt") for _ in range(2)]
    b_mt1 = [Buf("mt1") for _ in range(2)]
    b_mt2 = [Buf("mt2") for _ in range(2)]
    b_mT = [Buf("mT") for _ in range(2)]
    b_z = [Buf("z") for _ in range(2)]
    b_st = [Buf("st") for _ in range(2)]
    b_x1 = [Buf("x1") for _ in range(2)]
    b_h2s = [Buf("h2s") for _ in range(2)]

    b_oa = [Buf("oa") for _ in range(2)]
    b_ob = [Buf("ob") for _ in range(2)]
    b_g = [Buf("g") for _ in range(2)]

    def loads2(t):
        i2 = t % 2
        qc = t * 128 if t < NT else NOWN + 384
        dma("sp", lambda e: e.dma_start(out=oa_t[i2][:], in_=oa_d[t]), writes=[b_oa[i2]])
        dma("sp", lambda e: e.dma_start(out=ob_t[i2][:], in_=ob_d[t]), writes=[b_ob[i2]])
        dma("sp", lambda e: e.dma_start(out=g_t[i2][:, :].rearrange("p (j q) -> p j q", j=16),
                                        in_=g_d[:, :, qc:qc + 128].rearrange("j p q -> p j q")), writes=[b_g[i2]])
        dma("sp", lambda e: e.dma_start(out=x_t[i2][:], in_=din["xtok"][t * 128:(t + 1) * 128, :]), writes=[b_xt[i2]])

    def stageA(t):
        i2 = t % 2
        for half in range(2):
            def mmA(e, half=half):
                for oc in range(4):
                    for g in range(4):
                        r = e.matmul(bank(2 * half)[:, oc * 128:(oc + 1) * 128], Wa[:, g * D + (half * 4 + oc) * 128:g * D + (half * 4 + oc + 1) * 128],
                                     oa_t[i2][:, g * 128:(g + 1) * 128], start=(g == 0), stop=(g == 3), skip_group_check=True)
                return r

            def mmB(e, half=half):
                for oc in range(4):
                    for g in range(4):
                        r = e.matmul(bank(2 * half + 1)[:, oc * 128:(oc + 1) * 128], Wb[:, g * D + (half * 4 + oc) * 128:g * D + (half * 4 + oc + 1) * 128],
                                     ob_t[i2][:, g * 128:(g + 1) * 128], start=(g == 0), stop=(g == 3), skip_group_check=True)
                return r
            op("pe", mmA, reads=[b_w2, b_oa[i2]], writes=[PB[2 * half]])
            op("pe", mmB, reads=[b_w2, b_ob[i2]], writes=[PB[2 * half + 1]])
            op("dve", lambda e, half=half: e.tensor_tensor(mt1[half][:], bank(2 * half), g_t[i2][:, half * 512:(half + 1) * 512], ALU.mult),
               reads=[b_g[i2]], writes=[PB[2 * half], b_mt1[half]])
            op("dve", lambda e, half=half: e.tensor_tensor(mt2[half][:], bank(2 * half + 1), g_t[i2][:, 1024 + half * 512:1024 + (half + 1) * 512], ALU.mult),
               reads=[b_g[i2]], writes=[PB[2 * half + 1], b_mt2[half]])
            op("pool", lambda e, half=half: e.tensor_tensor(mT[i2][:, half * 512:(half + 1) * 512], mt1[half][:], mt2[half][:], ALU.add),
               reads=[b_mt1[half], b_mt2[half]], writes=[b_mT[i2]])

    def stageB(t):
        i2 = t % 2

        def mmY(e):
            for hh in range(2):
                for kc in range(KC):
                    r = e.matmul(bank(4 + hh), mT[i2][:, kc * 128:(kc + 1) * 128], Wo[:, kc * D + hh * 512:kc * D + (hh + 1) * 512],
                                 start=(kc == 0), stop=(kc == KC - 1))
            return r
        op("pe", mmY, reads=[b_w2, b_mT[i2]], writes=[PB[4], PB[5]])
        op("dve", lambda e: e.scalar_tensor_tensor(z_t[i2][:], x_t[i2][:], ALPHA, bank(4, 2), ALU.mult, ALU.add),
           reads=[b_xt[i2]], writes=[PB[4], PB[5], b_z[i2]])
        layer_norm(z_t[i2], b_z[i2], st[i2], b_st[i2], lnp[:, 0:D], lnp[:, D:2 * D], x1_t[i2], b_x1[i2], b_w2)
        dma("sp", lambda e: e.dma_start(out=x1_d[t * 128:(t + 1) * 128, :], in_=x1_t[i2][:]), reads=[b_x1[i2]])

    def stageC(t):
        i2 = t % 2

        def tr(e):
            for kc in range(KC):
                r = e.transpose(bank(6, 2)[:, kc * 128:(kc + 1) * 128], x1_t[i2][:, kc * 128:(kc + 1) * 128], ident[:])
            return r
        op("pe", tr, reads=[b_x1[i2], B_const], writes=[PB[6], PB[7]])
        for kc in range(KC):
            if kc % 2 == 0:
                op("act", lambda e, kc=kc: e.activation(h2s[i2][:, kc * 128:(kc + 1) * 128], bank(6, 2)[:, kc * 128:(kc + 1) * 128],
                                                        AF.Identity, bias=modfm[:, 24 + kc:25 + kc], scale=mod1p[:, 8 + kc:9 + kc]),
                   reads=[B_const], writes=[PB[6], PB[7], b_h2s[i2]])
            else:
                op("dve", lambda e, kc=kc: e.tensor_scalar(h2s[i2][:, kc * 128:(kc + 1) * 128], bank(6, 2)[:, kc * 128:(kc + 1) * 128],
                                                           mod1p[:, 8 + kc:9 + kc], modfm[:, 24 + kc:25 + kc], ALU.mult, ALU.add),
                   reads=[B_const], writes=[PB[6], PB[7], b_h2s[i2]])
        h2v = h2s[i2][:, :].rearrange("p (kc q) -> p kc q", kc=KC)
        if t < NT:
            dma("sp", lambda e: e.dma_start(out=h2_d[:, :, 1 + t * 128:1 + (t + 1) * 128].rearrange("kc p q -> p kc q"), in_=h2v),
                reads=[b_h2s[i2]])
        else:
            op("dve", lambda e: e.tensor_tensor(h2v[:, :, 0:2], h2v[:, :, 0:2], _bc_mid(hm[:, 0:2], KC), ALU.mult),
               reads=[B_const], writes=[b_h2s[i2]])
            dma("sp", lambda e: e.dma_start(out=h2_d[:, :, 0:1].rearrange("kc p q -> p kc q"), in_=h2v[:, :, 0:1],
                                            allow_slow_non_contiguous=True), reads=[b_h2s[i2]])
            dma("sp", lambda e: e.dma_start(out=h2_d[:, :, NOWN + 1:NOWN + 2].rearrange("kc p q -> p kc q"), in_=h2v[:, :, 1:2],
                                            allow_slow_non_contiguous=True), reads=[b_h2s[i2]])

    loads2(0)
    if NTQ > 1:
        loads2(1)
    stageA(0)
    for t in range(NTQ):
        stageB(t)
        if t + 1 < NTQ:
            stageA(t + 1)
        stageC(t)
        run_prefetch(pf3, NTQ - t)
        if t + 2 < NTQ:
            loads2(t + 2)
    while pf3:
        pf3.pop(0)()
    S_.barrier()

    A = Arena(PERSIST)
    lnp2 = A.alloc("lnp2", 2 * D, F32)
    cw = A.alloc("cw_sb", 132, F32)
    cb = A.alloc("cb_sb", 44, F32)
    gbc2 = A.alloc("gbc2", D, F32)
    wstg2 = [A.alloc("wstg2_%d" % i, D, F32) for i in range(2)]
    h2t = [A.alloc("h2t%d" % i, KC * 258, BF16) for i in range(2)]
    x1r = [A.alloc("x1r%d" % i, D, F32) for i in range(4)]
    cg = [A.alloc("cg%d" % i, 256, F32) for i in range(2)]
    cv_ = [A.alloc("cv%d" % i, 256, F32) for i in range(2)]
    sg = [A.alloc("sg%d" % i, 256, F32) for i in range(2)]
    aT = [A.alloc("aT%d" % i, 256, BF16) for i in range(3)]
    st2 = [A.alloc("st2_%d" % i, 8, F32) for i in range(2)]
    o_t = [A.alloc("o_t%d" % i, D, F32) for i in range(2)]
    assert A.off <= WUP0, A.off
    b_gbc2 = Buf("gbc2")
    b_wstg2 = [Buf("wstg2") for _ in range(2)]
    b_h2t = [Buf("h2t") for _ in range(2)]
    b_x1r = [Buf("x1r") for _ in range(4)]
    b_cg = [Buf("cg") for _ in range(2)]
    b_cv = [Buf("cv") for _ in range(2)]
    b_sg = [Buf("sg") for _ in range(2)]
    b_aT = [Buf("aT") for _ in range(3)]
    b_st2 = [Buf("st2a"), Buf("st2b")]
    pending_ln = []
    b_ot = [Buf("ot") for _ in range(2)]
    b_wdn, b_c3 = Buf("wdn"), Buf("c3")
    dma("sp", lambda e: e.dma_start(out=lnp2[:], in_=din["lnp"][:, 2 * D:4 * D]), writes=[b_c3])
    dma("sp", lambda e: e.dma_start(out=cw[:], in_=din["cw"]), writes=[b_c3])
    dma("sp", lambda e: e.dma_start(out=cb[:], in_=din["cb"]), writes=[b_c3])
    dma("sp", lambda e: e.dma_start(out=gbc2[:], in_=gate_d[:, D:2 * D]), writes=[b_gbc2])
    wdn_v = din["w_down"].rearrange("(fc p) n -> p fc n", p=128)
    NT3 = NOWN // 256

    def load_tile3(i):
        hi = i % 2
        dma("sp", lambda e: e.dma_start(out=h2t[hi][:, :].rearrange("p (kc q) -> p kc q", kc=KC),
                                        in_=h2_d[:, :, i * 256:i * 256 + 258].rearrange("kc p q -> p kc q")), writes=[b_h2t[hi]])
        for tt_ in range(2):
            xi = (i * 2 + tt_) % 4
            dma("sp", lambda e, xi=xi, tt_=tt_: e.dma_start(out=x1r[xi][:], in_=x1_d[i * 256 + tt_ * 128:i * 256 + (tt_ + 1) * 128, :]),
                writes=[b_x1r[xi]])

    load_tile3(0)
    for fc in range(NFC):
        w = fc % 2
        dma("sp", lambda e, w=w, fc=fc: e.dma_start(out=wstg2[w][:], in_=wdn_v[:, fc, :]), writes=[b_wstg2[w]])
        op("pool", lambda e, w=w, fc=fc: e.tensor_tensor(Wdn[:, fc * D:(fc + 1) * D], wstg2[w][:], gbc2[:], ALU.mult),
           reads=[b_wstg2[w], b_gbc2], writes=[b_wdn])
    step = [0]
    for i in range(NT3):
        hi = i % 2

        def up(fc, hi=hi):
            k = step[0] + fc
            pg, pv_ = (k % 2) * 2, (k % 2) * 2 + 1

            def f(e):
                for (pb, col) in ((pg, fc * 128), (pv_, DFF + fc * 128)):
                    for kc in range(KC):
                        r = e.matmul(bank(pb)[:, 0:258], Wup[:, kc * 2 * DFF + col:kc * 2 * DFF + col + 128],
                                     h2t[hi][:, kc * 258:(kc + 1) * 258], start=(kc == 0), stop=(kc == KC - 1))
                return r
            op("pe", f, reads=[b_w3, b_h2t[hi]], writes=[PB[pg], PB[pv_]])

        up(0)
        for fc in range(NFC):
            k = step[0] + fc
            pg, pv_ = (k % 2) * 2, (k % 2) * 2 + 1
            ci = k % 2
            if fc + 1 < NFC:
                up(fc + 1)
            if fc == NFC // 2 and i + 1 < NT3:
                load_tile3(i + 1)
            if fc in (2, 6) and pending_ln:
                pending_ln.pop(0)()
            for (pb, dst, bd, fi) in ((pg, cg[ci], b_cg[ci], fc), (pv_, cv_[ci], b_cv[ci], NFC + fc)):
                op("act", lambda e, pb=pb, dst=dst, fi=fi: e.activation(dst[:], bank(pb)[:, 1:257], AF.Identity,
                                                                        bias=cb[:, fi:fi + 1], scale=cw[:, fi * 3 + 1:fi * 3 + 2]),
                   reads=[b_c3], writes=[PB[pb], bd])
                op("dve", lambda e, pb=pb, dst=dst, fi=fi: e.scalar_tensor_tensor(dst[:], bank(pb)[:, 0:256], cw[:, fi * 3:fi * 3 + 1], dst[:],
                                                                                  ALU.mult, ALU.add), reads=[b_c3], writes=[PB[pb], bd])
                op("dve", lambda e, pb=pb, dst=dst, fi=fi: e.scalar_tensor_tensor(dst[:], bank(pb)[:, 2:258], cw[:, fi * 3 + 2:fi * 3 + 3], dst[:],
                                                                                  ALU.mult, ALU.add), reads=[b_c3], writes=[PB[pb], bd])
            op("act", lambda e, ci=ci: e.activation(sg[ci][:], cg[ci][:], AF.Silu), reads=[b_cg[ci]], writes=[b_sg[ci]])
            ai = k % 3
            op("pool", lambda e, ci=ci, ai=ai: e.tensor_tensor(aT[ai][:], sg[ci][:], cv_[ci][:], ALU.mult),
               reads=[b_sg[ci], b_cv[ci]], writes=[b_aT[ai]])

            def dn(e, fc=fc, ai=ai):
                for tt_ in range(2):
                    for hh in range(2):
                        r = e.matmul(bank(4 + tt_ * 2 + hh), aT[ai][:, tt_ * 128:(tt_ + 1) * 128], Wdn[:, fc * D + hh * 512:fc * D + (hh + 1) * 512],
                                     start=(fc == 0), stop=(fc == NFC - 1))
                return r
            op("pe", dn, reads=[b_wdn, b_aT[ai]], writes=[PB[4], PB[5], PB[6], PB[7]])
        step[0] += NFC
        for tt_ in range(2):
            xi = (i * 2 + tt_) % 4
            op("dve", lambda e, xi=xi, tt_=tt_: e.scalar_tensor_tensor(x1r[xi][:], x1r[xi][:], ALPHA, bank(4 + tt_ * 2, 2), ALU.mult, ALU.add),
               writes=[PB[4 + tt_ * 2], PB[5 + tt_ * 2], b_x1r[xi]])

            def ln_job(xi=xi, i=i, tt_=tt_):
                oi_ = (i * 2 + tt_) % 2
                layer_norm(x1r[xi], b_x1r[xi], st2[tt_], b_st2[tt_], lnp2[:, 0:D], lnp2[:, D:2 * D], o_t[oi_], b_ot[oi_], b_c3)
                r0 = i * 256 + tt_ * 128
                dma("sp", lambda e: e.dma_start(out=out_d[r0:r0 + 128, :], in_=o_t[oi_][:]), reads=[b_ot[oi_]])
            pending_ln.append(ln_job)
    while pending_ln:
        pending_ln.pop(0)()
    S_.emit()
    return nc


_CACHE = {}


def run(inputs, S):
    sh = _shared_inputs(inputs)
    in_maps = []
    for core in range(8):
        d = _core_inputs(inputs, sh, core, S)
        in_maps.append({k: np.ascontiguousarray(d[k], dtype=np.float32) for k in IN_SHAPES(S)})
    if S not in _CACHE:
        _CACHE[S] = build_program(S)
    res = run_bass_kernel_spmd(_CACHE[S], in_maps, core_ids=list(range(8)))
    global LAST
    LAST = (res, in_maps)
    NOWN = S // 4
    out = np.zeros((2, S, D), np.float32)
    for core in range(8):
        b, r = core // 4, core % 4
        out[b, r * NOWN:(r + 1) * NOWN] = res.results[core]["out"]
    return out


def kernel(**inputs):
    return run(inputs, 16384)
```
